# Optimizing a Trainium2 kernel written in Bass

```python
import math
import jax, jax.numpy as jnp
from jax import lax
import numpy as np

D_MODEL = 1024
BATCH = 8
SEQ = 2048
DEPTH = 2
DEC_BATCH = 128
DEC_SEQ = 4
PAST_LEN = 16384
PAGE_SIZE = 128

D_LRU = D_MODEL // 2
LRU_HEADS = 8
LRU_HEAD_DIM = D_LRU // LRU_HEADS
CONV_W = 4
C_GATE = 8.0
D_S5 = D_MODEL - D_LRU
S5_GROUP = 16
N_S5_GROUPS = D_S5 // S5_GROUP
S5_STATE = 64
D_IN = 2 * D_LRU + D_S5
D_FF = 2816
FFN_RES = 0.5
N_MOD = 9
EPS = 1e-6

kernel_name = "hymba_rglru_s5_macaron_adaln_step"


def rmsnorm(x, g):
    xf = x.astype(jnp.float32)
    y = xf * lax.rsqrt(jnp.mean(xf * xf, axis=-1, keepdims=True) + EPS)
    return (y * g.astype(jnp.float32)).astype(x.dtype)


def modulate(h, shift, scale):
    return h * (1.0 + scale[:, None, :]) + shift[:, None, :]


def swiglu(h, w1, w3, w2):
    return (jax.nn.silu(h @ w1) * (h @ w3)) @ w2


def _lin_op(e1, e2):
    a1, b1 = e1
    a2, b2 = e2
    return a2 * a1, a2 * b1 + b2


def _cplx_op(e1, e2):
    a1r, a1i, b1r, b1i = e1
    a2r, a2i, b2r, b2i = e2
    return (a2r * a1r - a2i * a1i,
            a2r * a1i + a2i * a1r,
            a2r * b1r - a2i * b1i + b2r,
            a2r * b1i + a2i * b1r + b2i)


def lru_scan(a, b, h0):
    b = b.at[:, 0].add(a[:, 0] * h0)
    _, h = lax.associative_scan(_lin_op, (a, b), axis=1)
    return h


def s5_scan(ar, ai, br, bi, s0r, s0i):
    br = br.at[:, 0].add(ar * s0r - ai * s0i)
    bi = bi.at[:, 0].add(ar * s0i + ai * s0r)
    Ar = jnp.broadcast_to(ar, br.shape)
    Ai = jnp.broadcast_to(ai, bi.shape)
    _, _, sr, si = lax.associative_scan(_cplx_op, (Ar, Ai, br, bi), axis=1)
    return sr, si


def mixer(h, conv0, h0, sr0, si0, w_in, conv_w, conv_b, w_rg, b_rg, w_ig, b_ig, lam,
          a_re, a_im, log_dt, b_re, b_im, c_re, c_im, d_skip, w_glu, b_glu, w_out):
    B, T, _ = h.shape
    z = h @ w_in
    xb = z[..., :D_LRU]
    yb = z[..., D_LRU:2 * D_LRU]
    u = z[..., 2 * D_LRU:]

    xp = jnp.concatenate([conv0.astype(xb.dtype), xb], axis=1)
    new_conv = xp[:, -(CONV_W - 1):]
    xc = conv_b + sum(xp[:, k:k + T] * conv_w[k] for k in range(CONV_W))
    xc32 = xc.astype(jnp.float32)
    xh = xc32.reshape(B, T, LRU_HEADS, LRU_HEAD_DIM)
    r = jax.nn.sigmoid(jnp.einsum('bthi,hij->bthj', xh, w_rg.astype(jnp.float32)).reshape(B, T, D_LRU)
                       + b_rg.astype(jnp.float32))
    ig = jax.nn.sigmoid(jnp.einsum('bthi,hij->bthj', xh, w_ig.astype(jnp.float32)).reshape(B, T, D_LRU)
                        + b_ig.astype(jnp.float32))
    log_a = -C_GATE * r * jax.nn.softplus(-lam.astype(jnp.float32))
    a = jnp.exp(log_a)
    mult = jnp.sqrt(-jnp.expm1(2.0 * log_a))
    hs = lru_scan(a, mult * ig * xc32, h0.astype(jnp.float32))
    new_h = hs[:, -1]
    y_lru = jax.nn.gelu(yb) * hs.astype(yb.dtype)

    Ar = jnp.minimum(a_re.astype(jnp.float32), -1e-4)
    Ai = a_im.astype(jnp.float32)
    dt = jnp.exp(log_dt.astype(jnp.float32))[:, None]
    mag = jnp.exp(Ar * dt)
    ab_r = mag * jnp.cos(Ai * dt)
    ab_i = mag * jnp.sin(Ai * dt)
    den = Ar * Ar + Ai * Ai
    f_r = ((ab_r - 1.0) * Ar + ab_i * Ai) / den
    f_i = (ab_i * Ar - (ab_r - 1.0) * Ai) / den
    Br = b_re.astype(jnp.float32)
    Bi = b_im.astype(jnp.float32)
    bb_r = f_r[..., None] * Br - f_i[..., None] * Bi
    bb_i = f_r[..., None] * Bi + f_i[..., None] * Br
    ug = u.astype(jnp.float32).reshape(B, T, N_S5_GROUPS, S5_GROUP)
    bu_r = jnp.einsum('btgj,gnj->btgn', ug, bb_r)
    bu_i = jnp.einsum('btgj,gnj->btgn', ug, bb_i)
    sr, si = s5_scan(ab_r, ab_i, bu_r, bu_i, sr0.astype(jnp.float32), si0.astype(jnp.float32))
    ys = (jnp.einsum('gjn,btgn->btgj', c_re.astype(jnp.float32), sr)
          - jnp.einsum('gjn,btgn->btgj', c_im.astype(jnp.float32), si))
    ys = ys.reshape(B, T, D_S5).astype(u.dtype) + d_skip * u
    g = jax.nn.gelu(ys)
    y_s5 = g * jax.nn.sigmoid(g @ w_glu + b_glu)

    out = jnp.concatenate([y_lru, y_s5], axis=-1) @ w_out
    return out, new_conv, new_h, sr[:, -1], si[:, -1]


def setup_inputs(seed: int = 0) -> dict:
    key = jax.random.key(seed)
    ks = iter(jax.random.split(key, 64))

    def nrm(shape, scale=1.0):
        return scale * jax.random.normal(next(ks), shape, jnp.float32)

    def gain(shape):
        return 1.0 + 0.05 * nrm(shape)

    L = DEPTH
    a8 = jax.random.uniform(next(ks), (L, D_LRU), jnp.float32, 0.9, 0.999)
    a_init = a8 ** (1.0 / C_GATE)
    lam = jnp.log(a_init) - jnp.log1p(-a_init)
    n_idx = jnp.arange(S5_STATE, dtype=jnp.float32)
    log_dt = jax.random.uniform(next(ks), (L, N_S5_GROUPS), jnp.float32,
                                math.log(0.001), math.log(0.1))
    return {
        "x_prompt": nrm((BATCH, SEQ, D_MODEL)),
        "x_sample": nrm((DEC_BATCH, DEC_SEQ, D_MODEL)),
        "c_prompt": nrm((BATCH, D_MODEL)),
        "c_sample": nrm((DEC_BATCH, D_MODEL)),
        "state_lru_conv": nrm((L, DEC_BATCH, CONV_W - 1, D_LRU)),
        "state_lru_h": nrm((L, DEC_BATCH, D_LRU), 0.5),
        "state_s5_re": nrm((L, DEC_BATCH, N_S5_GROUPS, S5_STATE), 0.1),
        "state_s5_im": nrm((L, DEC_BATCH, N_S5_GROUPS, S5_STATE), 0.1),
        "w_ada": nrm((L, D_MODEL, N_MOD * D_MODEL), 0.5 * D_MODEL ** -0.5),
        "b_ada": nrm((L, N_MOD * D_MODEL), 0.01),
        "norm_ffn1": gain((L, D_MODEL)),
        "w1_ffn1": nrm((L, D_MODEL, D_FF), D_MODEL ** -0.5),
        "w3_ffn1": nrm((L, D_MODEL, D_FF), D_MODEL ** -0.5),
        "w2_ffn1": nrm((L, D_FF, D_MODEL), D_FF ** -0.5),
        "norm_mix": gain((L, D_MODEL)),
        "w_in": nrm((L, D_MODEL, D_IN), D_MODEL ** -0.5),
        "conv_w": nrm((L, CONV_W, D_LRU), CONV_W ** -0.5),
        "conv_b": nrm((L, D_LRU), 0.01),
        "w_rg": nrm((L, LRU_HEADS, LRU_HEAD_DIM, LRU_HEAD_DIM), LRU_HEAD_DIM ** -0.5),
        "b_rg": nrm((L, D_LRU), 0.01),
        "w_ig": nrm((L, LRU_HEADS, LRU_HEAD_DIM, LRU_HEAD_DIM), LRU_HEAD_DIM ** -0.5),
        "b_ig": nrm((L, D_LRU), 0.01),
        "lru_lambda": lam,
        "s5_a_re": -0.5 + nrm((L, N_S5_GROUPS, S5_STATE), 0.01),
        "s5_a_im": jnp.pi * n_idx + nrm((L, N_S5_GROUPS, S5_STATE), 0.01),
        "s5_log_dt": log_dt,
        "s5_b_re": nrm((L, N_S5_GROUPS, S5_STATE, S5_GROUP), (2 * S5_GROUP) ** -0.5),
        "s5_b_im": nrm((L, N_S5_GROUPS, S5_STATE, S5_GROUP), (2 * S5_GROUP) ** -0.5),
        "s5_c_re": nrm((L, N_S5_GROUPS, S5_GROUP, S5_STATE), (2 * S5_STATE) ** -0.5),
        "s5_c_im": nrm((L, N_S5_GROUPS, S5_GROUP, S5_STATE), (2 * S5_STATE) ** -0.5),
        "s5_d": nrm((L, D_S5)),
        "w_glu": nrm((L, D_S5, D_S5), D_S5 ** -0.5),
        "b_glu": nrm((L, D_S5), 0.01),
        "w_out": nrm((L, D_MODEL, D_MODEL), D_MODEL ** -0.5),
        "norm_ffn2": gain((L, D_MODEL)),
        "w1_ffn2": nrm((L, D_MODEL, D_FF), D_MODEL ** -0.5),
        "w3_ffn2": nrm((L, D_MODEL, D_FF), D_MODEL ** -0.5),
        "w2_ffn2": nrm((L, D_FF, D_MODEL), D_FF ** -0.5),
        "norm_final": gain((D_MODEL,)),
    }


def reference(x_prompt, x_sample, c_prompt, c_sample, state_lru_conv, state_lru_h, state_s5_re,
              state_s5_im, w_ada, b_ada, norm_ffn1, w1_ffn1, w3_ffn1, w2_ffn1, norm_mix, w_in,
              conv_w, conv_b, w_rg, b_rg, w_ig, b_ig, lru_lambda, s5_a_re, s5_a_im, s5_log_dt,
              s5_b_re, s5_b_im, s5_c_re, s5_c_im, s5_d, w_glu, b_glu, w_out, norm_ffn2, w1_ffn2,
              w3_ffn2, w2_ffn2, norm_final):
    def run(x, c, conv0, h0, sr0, si0):
        convs, hs, srs, sis = [], [], [], []
        for l in range(DEPTH):
            mod = jax.nn.silu(c) @ w_ada[l] + b_ada[l]
            m = jnp.split(mod, N_MOD, axis=-1)
            h = modulate(rmsnorm(x, norm_ffn1[l]), m[0], m[1])
            x = x + FFN_RES * m[2][:, None, :] * swiglu(h, w1_ffn1[l], w3_ffn1[l], w2_ffn1[l])
            h = modulate(rmsnorm(x, norm_mix[l]), m[3], m[4])
            y, cv, hh, sr, si = mixer(h, conv0[l], h0[l], sr0[l], si0[l], w_in[l], conv_w[l],
                                      conv_b[l], w_rg[l], b_rg[l], w_ig[l], b_ig[l],
                                      lru_lambda[l], s5_a_re[l], s5_a_im[l], s5_log_dt[l],
                                      s5_b_re[l], s5_b_im[l], s5_c_re[l], s5_c_im[l], s5_d[l],
                                      w_glu[l], b_glu[l], w_out[l])
            x = x + m[5][:, None, :] * y
            h = modulate(rmsnorm(x, norm_ffn2[l]), m[6], m[7])
            x = x + FFN_RES * m[8][:, None, :] * swiglu(h, w1_ffn2[l], w3_ffn2[l], w2_ffn2[l])
            convs.append(cv)
            hs.append(hh)
            srs.append(sr)
            sis.append(si)
        return (rmsnorm(x, norm_final), jnp.stack(convs), jnp.stack(hs),
                jnp.stack(srs), jnp.stack(sis))

    Bp = x_prompt.shape[0]
    z_conv = jnp.zeros((DEPTH, Bp, CONV_W - 1, D_LRU), x_prompt.dtype)
    z_h = jnp.zeros((DEPTH, Bp, D_LRU), jnp.float32)
    z_s = jnp.zeros((DEPTH, Bp, N_S5_GROUPS, S5_STATE), jnp.float32)
    y_prompt, p_conv, p_h, p_sr, p_si = run(x_prompt, c_prompt, z_conv, z_h, z_s, z_s)
    y_sample, s_conv, s_h, s_sr, s_si = run(x_sample, c_sample, state_lru_conv, state_lru_h,
                                            state_s5_re, state_s5_im)
    return (y_prompt, y_sample, p_conv, p_h, p_sr, p_si, s_conv, s_h, s_sr, s_si)
```

```python
import numpy as np
from contextlib import ExitStack
import concourse.bass as bass
import concourse.mybir as mybir
from concourse.bass_utils import run_bass_kernel_spmd

F32 = mybir.dt.float32
BF16 = mybir.dt.bfloat16
I32 = mybir.dt.int32
AF = mybir.ActivationFunctionType
ALU = mybir.AluOpType

NCORES = 8
D = 1024
KC = 8
DFF = 2816
FC = 22
TP = 2048
NSQ = 16
TS = 4
NSAMP = NSQ * TS
NTOK = TP + NSAMP
NSEQ = 17
DEPTH = 2
NT5 = 128
PI = float(np.pi)
TWO_PI = float(2 * np.pi)
EPS = 1e-6

PR_NAMES = [("norm_ffn1", 8), ("norm_mix", 8), ("norm_ffn2", 8), ("conv_w", 16), ("conv_b", 4),
            ("b_rg", 4), ("b_ig", 4), ("lru_lambda", 4), ("s5_d", 4), ("b_glu", 4),
            ("s5_a_re", 16), ("s5_a_im", 16), ("b_ada", 72)]
PR_OFF = {}
_o = 0
for _n, _k in PR_NAMES:
    PR_OFF[_n] = _o
    _o += _k
PR_LAYER = _o
PR_FINAL = PR_LAYER * DEPTH
PR_ROWS = 384

SAME_ENGINE_SYNC = ('act', 'pool', 'dve')
STOP_AT = None
SETUP_PARTS = 4
import os as _os
DBG_XT = int(_os.environ.get('DBG_XT', '999'))
DBG_XV = _os.environ.get('DBG_XV', '')


class Res:
    __slots__ = ("w", "rs", "name")

    def __init__(self, name=""):
        self.w = None
        self.rs = {}
        self.name = name


class DSem:
    def __init__(self, handle, key):
        self.h = handle
        self.key = key
        self.v = 0


class Sched:
    def __init__(self, nc, es, waited=None):
        self.nc = nc
        self.es = es
        self.waited = waited
        self.rec = {k: set() for k in ("pe", "dve", "act", "pool", "sp")}
        self.last_inc = {k: 0 for k in ("pe", "dve", "act", "pool", "sp")}
        self.eng = dict(pe=nc.tensor, dve=nc.vector, act=nc.scalar, pool=nc.gpsimd, sp=nc.sync)
        self.sem = {k: es.enter_context(nc.semaphore("cs_" + k)) for k in self.eng}
        self.cnt = {k: 0 for k in self.eng}
        self.known = {k: {} for k in self.eng}
        self.ndsem = 0
        self.dsems = []

    def dsem(self):
        h = self.es.enter_context(self.nc.semaphore("ds%d" % self.ndsem))
        d = DSem(h, "d%d" % self.ndsem)
        self.ndsem += 1
        self.dsems.append(d)
        return d

    @staticmethod
    def _flat(xs):
        out = []
        for x in xs:
            if isinstance(x, (list, tuple)):
                out.extend(Sched._flat(x))
            else:
                out.append(x)
        return out

    def barrier(self, engs=("pe", "dve", "act", "sp")):
        for e in engs:
            for f in ("pe", "dve", "act"):
                if self.cnt[f] == 0 or (f == e and e in ("pe", "sp")):
                    continue
                if self.known[e].get(f, 0) < self.cnt[f]:
                    self.eng[e].wait_ge(self.sem[f], self.cnt[f])
                    self.known[e][f] = self.cnt[f]
                    self.rec[f].add(self.cnt[f])

    def _waits(self, e, reads, writes):
        reads = self._flat(reads); writes = self._flat(writes)
        deps = {}
        raw_same = [0]

        def add(tok, raw=False):
            key, h, v = tok
            if key == e:
                if raw and v > raw_same[0]:
                    raw_same[0] = v
                return
            if key not in deps or deps[key][1] < v:
                deps[key] = (h, v)
        for r in reads:
            if r.w is not None:
                add(r.w, True)
        for w in writes:
            if w.w is not None:
                add(w.w, not w.name.startswith("bank"))
            for t in w.rs.values():
                add(t, not w.name.startswith("bank"))
        if raw_same[0] > 0:
            deps[e] = (self.sem[e], raw_same[0])
        for key, (h, v) in deps.items():
            if key == e:
                if e == "pe" or e not in SAME_ENGINE_SYNC:
                    continue
                if v > self.cnt[e]:
                    continue
            if self.known[e].get(key, 0) < v:
                self.eng[e].wait_ge(h, v)
                self.known[e][key] = v
                if key in self.rec:
                    self.rec[key].add(v)

    def _record(self, tok, reads, writes):
        reads = self._flat(reads); writes = self._flat(writes)
        key = tok[0]
        for r in reads:
            if key not in r.rs or r.rs[key][2] < tok[2]:
                r.rs[key] = tok
        for w in writes:
            w.w = tok
            w.rs = {}

    def op(self, e, fn, r=(), w=(), inc=True):
        r = self._flat(r); w = self._flat(w)
        w = w + [x for x in r if x.name.startswith("bank")]
        r = [x for x in r if not x.name.startswith("bank")]
        self._waits(e, r, w)
        inst = fn()
        self.cnt[e] += 1
        k = self.cnt[e]
        if self.waited is None or k in self.waited[e]:
            inst.then_inc(self.sem[e], k - self.last_inc[e])
            self.last_inc[e] = k
        tok = (e, self.sem[e], k)
        self._record(tok, r, w)
        return inst

    def dma(self, q, ds, pairs, r=(), w=()):
        self._waits(q, r, w)
        for (o, i) in pairs:
            self.eng[q].dma_start(out=o, in_=i).then_inc(ds.h, 16)
            ds.v += 16
        tok = (ds.key, ds.h, ds.v)
        self._record(tok, r, w)

    def finish(self):
        for d in self.dsems:
            if d.v > 0 and self.known["sp"].get(d.key, 0) < d.v:
                self.nc.sync.wait_ge(d.h, d.v)
        for e in ("pe", "dve", "act", "pool"):
            if self.cnt[e] > 0:
                self.nc.sync.wait_ge(self.sem[e], self.cnt[e])
                self.rec[e].add(self.cnt[e])


class Tile:
    def __init__(self, col0, n, kind, loc):
        self.col0 = col0
        self.n = n
        self.kind = kind
        self.loc = loc


def build_program(waited=None, want_rec=False):
    if waited is None and not want_rec:
        rec = build_program(None, True)
        return build_program(rec, False)
    nc = bass.Bass("TRN2", target_bir_lowering=False)

    def din(name, shape):
        return nc.dram_tensor(name, list(shape), F32, kind="ExternalInput").ap()

    def dout(name, shape):
        return nc.dram_tensor(name, list(shape), F32, kind="ExternalOutput").ap()

    xp_d = din("xp", [TP, D])
    xs_d = din("xs", [NSAMP, D])
    c17_d = din("c17", [NSEQ, D])
    stc_d = din("st_conv", [DEPTH, NSQ, 3 * 512])
    sth_d = din("st_h", [DEPTH, NSQ, 512])
    str_d = din("st_sr", [DEPTH, NSQ, 2048])
    sti_d = din("st_si", [DEPTH, NSQ, 2048])
    prow_d = din("prow", [PR_ROWS, 128])
    dtx_d = din("dtx", [DEPTH, 128, 16])
    wada_d = din("w_ada", [DEPTH, D, 9 * D])
    w1_d = [din("w1_ffn1", [DEPTH, D, DFF]), din("w1_ffn2", [DEPTH, D, DFF])]
    w3_d = [din("w3_ffn1", [DEPTH, D, DFF]), din("w3_ffn2", [DEPTH, D, DFF])]
    w2_d = [din("w2_ffn1", [DEPTH, DFF, D]), din("w2_ffn2", [DEPTH, DFF, D])]
    win_d = din("w_in", [DEPTH, D, 1536])
    wglu_d = din("w_glu", [DEPTH, 512, 512])
    wout_d = din("w_out", [DEPTH, D, D])
    wg_d = din("wgates", [DEPTH, 128, 8 * 128])
    bnat_d = din("bnat", [DEPTH, 2, 128, 16 * 128])
    cpad_d = din("cpad", [DEPTH, 128, 32 * 128])

    yp_d = dout("y_p", [TP, D])
    ys_d = dout("y_s", [NSAMP, D])
    oconv_d = dout("o_conv", [DEPTH, NSEQ, 3 * 512])
    oh_d = dout("o_h", [DEPTH, NSEQ, 512])
    osr_d = dout("o_sr", [DEPTH, NSEQ, 2048])
    osi_d = dout("o_si", [DEPTH, NSEQ, 2048])

    with ExitStack() as es:
        K = Sched(nc, es, waited)

        def sb(name, shape, dt=F32):
            return es.enter_context(nc.sbuf_tensor(name, list(shape), dt))

        XT = sb("XT", [128, KC * NTOK])
        RING = [sb("RING%d" % i, [128, 4096], BF16) for i in range(3)]
        WORK = sb("WORK", [128, 16320])
        AUX = sb("AUX", [128, 4128])
        WBU = sb("WBU", [128, 32 * 128], BF16)
        WC = sb("WC", [128, 32 * 128], BF16)
        WG = sb("WG", [128, 8 * 128], BF16)
        ABG = sb("ABG", [128, 9 * KC * NSEQ])
        PT = sb("PT", [128, PR_ROWS])
        IDF = sb("IDF", [128, 128])
        IDB = sb("IDB", [128, 128], BF16)
        ONESB = sb("ONESB", [128, 128], BF16)
        SILUC = sb("SILUC", [128, KC * NSEQ], BF16)
        FST = sb("FST", [128, 48 * NSEQ])
        SMALL = sb("SMALL", [128, 1700])
        TAU = sb("TAU", [128, NT5 + 1])
        PSB = [es.enter_context(nc.psum_tensor("PSB%d" % i, [128, 512], F32)) for i in range(8)]

        R = {}

        def res(name):
            if name not in R:
                R[name] = Res(name)
            return R[name]

        bankres = [res("bank%d" % i) for i in range(8)]
        ringres = [res("ring%d" % i) for i in range(3)]
        ringsem = [K.dsem() for _ in range(3)]
        ring_i = [0]
        ada_loaded = []
        role_banks = {"g1": [0, 1], "g3": [2, 3], "o": [4, 5], "m": [6, 7]}
        role_i = {k: 0 for k in role_banks}

        def bank(role):
            b = role_banks[role][role_i[role] % 2]
            role_i[role] += 1
            return b

        def ring_load(pairs_fn, ada=False):
            s = ring_i[0] % 3
            ring_i[0] += 1
            K.dma("pool", ringsem[s], pairs_fn(RING[s]), w=[ringres[s]])
            return s

        sm_off = [0]

        def small(n):
            o = sm_off[0]
            sm_off[0] += n
            assert sm_off[0] <= 1700
            return SMALL[:, o:o + n]

        A_RE = small(16); A_IM = small(16); DTX = small(16); DT = small(16); ARC = small(16)
        RHO = small(16); TH = small(16); TMPA = small(16); TMPB = small(16); TMPC = small(16); TMPD = small(16)
        F_R = small(16); F_I = small(16); ABR = small(16); ABI = small(16)
        SC1 = small(4); SC2 = small(4); SPT = small(4); SCH = small(4); HBG = small(8)
        HC = small(4)
        CR = small(16); CI = small(16)
        SLR = small(16); SLI = small(16)
        H0T = small(64)
        S0R = small(256); S0I = small(256)
        S0TR = small(256); S0TI = small(256)
        EPSC = small(1)
        HIST = small(12); CRPR = small(16); CRPI = small(16); ONEC = small(1); PIC = small(1)
        rs_small = res("small_s5par")

        def xv(c, col0, n):
            return XT[:, c * NTOK + col0: c * NTOK + col0 + n]

        def wbf(off_words, nelem):
            return WORK[:, off_words: off_words + (nelem + 1) // 2].bitcast(BF16)

        PROW = AUX[:, 1024:1024 + 384]
        C17 = AUX[0:NSEQ, 0:1024]
        H_B = wbf(0, 8 * 1088)
        A_B = wbf(4352, 22 * 1088)

        def hv(c, loc, n):
            return H_B[:, c * 1088 + loc: c * 1088 + loc + n]

        def av(f, loc, n):
            return A_B[:, f * 1088 + loc: f * 1088 + loc + n]

        def norm_tmps(tb):
            if tb < 0:
                return (AUX[:, 0:2048].bitcast(BF16), AUX[:, 2048:2560], AUX[:, 2560:3072],
                        [AUX[:, 3072:3584], AUX[:, 3584:4096]],
                        (rs_auxh[0:4], [rs_auxh[4], rs_auxh[5]], [rs_auxh[6], rs_auxh[7]]))
            return (wbf(tb, 8 * 512), WORK[:, tb + 2048: tb + 2048 + 512], WORK[:, tb + 2560: tb + 2560 + 512],
                    [WORK[:, tb + 3072 + i * 512: tb + 3072 + (i + 1) * 512] for i in range(2)],
                    (res("xsq"), res("rstd"), [res("nt0_0"), res("nt0_1")]))
        S32 = [AUX[:, 0:512], AUX[:, 1024:1536]]
        T64 = [AUX[:, 2048:2112], AUX[:, 3072:3136]]

        XBW = 1091
        XB = WORK[:, 4352: 4352 + 4 * XBW]
        GY_B = wbf(8716, 4 * 1088)
        U_B = wbf(10892, 4 * 1088)
        YL_B = wbf(13068, 4 * 1088)
        XBS = WORK[:, 15244: 15244 + 448]
        TAILX = WORK[:, 15244 + 448: 16320]

        def xbv(c, loc, n):
            return XB[:, c * XBW + loc: c * XBW + loc + n]

        def gyv(c, loc, n):
            return GY_B[:, c * 1088 + loc: c * 1088 + loc + n]

        def uv(c, loc, n):
            return U_B[:, c * 1088 + loc: c * 1088 + loc + n]

        def ylv(c, loc, n):
            return YL_B[:, c * 1088 + loc: c * 1088 + loc + n]

        COS = AUX[:, 0:16 * 129]
        SIN = AUX[:, 16 * 129: 32 * 129]

        def cosv(o0, no, t0, nt):
            return COS.rearrange("p (o t) -> p o t", o=16)[:, o0:o0 + no, t0:t0 + nt]

        def sinv(o0, no, t0, nt):
            return SIN.rearrange("p (o t) -> p o t", o=16)[:, o0:o0 + no, t0:t0 + nt]

        rs_auxh = [res("auxh%d" % i) for i in range(9)]
        rs_aux = [[rs_auxh[2 * i], rs_auxh[2 * i + 1]] for i in range(4)] + [[rs_auxh[8]]]
        rs_tab = [res("tables")] + rs_auxh

        def ptc(l, name, c, n=1):
            base = PR_LAYER * l + PR_OFF[name] + c
            return PT[:, base: base + n]

        rs_pt = res("pt")

        def abg(s, which, c, s0, ns):
            base = ((s * 3 + which) * KC + c) * NSEQ
            return ABG[:, base + s0: base + s0 + ns]

        rs_abg = res("abg")
        rs_modt = res("modt")

        tiles_all = [Tile(0, 512, "p", 0), Tile(512, 512, "p", 512),
                     Tile(1024, 512, "p", 0), Tile(1536, 512, "p", 512), Tile(2048, 64, "s", 1024)]
        groups = [tiles_all[0:2], tiles_all[2:5]]

        def xres(t, c):
            return res("x_%d_%d" % (t.col0, c))

        def bc_seq(ap2, n):
            return ap2.unsqueeze(2).to_broadcast([128, NSQ, TS])

        def v3(ap, a, b):
            return ap.rearrange("p (a b) -> p a b", a=a, b=b)

        rs_const = res("const")
        K.op("pool", lambda: nc.gpsimd.memset(IDF[:], 0.0), w=[rs_const])
        K.op("pool", lambda: nc.gpsimd.affine_select(out=IDF[:], in_=IDF[:], compare_op=ALU.not_equal, fill=1.0,
                                                     base=0, pattern=[[-1, 128]], channel_multiplier=1), r=[rs_const], w=[rs_const])
        K.op("pool", lambda: nc.gpsimd.memset(ONESB[:], 1.0), w=[rs_const])
        K.op("pool", lambda: nc.gpsimd.iota(TAU[:], pattern=[[1, NT5 + 1]], base=0, channel_multiplier=0,
                                            allow_small_or_imprecise_dtypes=True), w=[rs_const])
        K.op("dve", lambda: nc.vector.tensor_copy(out=IDB[:], in_=IDF[:]), r=[rs_const], w=[rs_const])
        K.op("dve", lambda: nc.vector.memset(EPSC, EPS), w=[rs_const])
        K.op("dve", lambda: nc.vector.memset(ONEC, 1.0), w=[rs_const])
        K.op("dve", lambda: nc.vector.memset(PIC, PI), w=[rs_const])

        ds_misc = K.dsem()
        rs_prow = res("prow")
        if SETUP_PARTS >= 2: K.dma("sp", ds_misc, [(PROW[:, i * 128:(i + 1) * 128], prow_d[i * 128:(i + 1) * 128, :]) for i in range(3)],
              w=[rs_prow, rs_aux[1]])
        for i in range(3 if SETUP_PARTS >= 2 else 0):
            b = bank("m")
            K.op("pe", lambda: nc.tensor.transpose(out=PSB[b][:, 0:128], in_=PROW[:, i * 128:(i + 1) * 128], identity=IDF[:]),
                 r=[rs_prow, rs_aux[1], rs_const], w=[bankres[b]])
            K.op("dve", lambda: nc.vector.tensor_copy(out=PT[:, i * 128:(i + 1) * 128], in_=PSB[b][:, 0:128]),
                 r=[bankres[b]], w=[rs_pt])

        ds_c = K.dsem()
        rs_c17 = res("c17")
        if SETUP_PARTS >= 3: K.dma("sp", ds_c, [(C17, c17_d[:, :])], w=[rs_c17, rs_aux[0]])
        rs_siluc = res("siluc")
        for c in range(KC if SETUP_PARTS >= 3 else 0):
            b = bank("m")
            K.op("pe", lambda: nc.tensor.transpose(out=PSB[b][:, 0:NSEQ], in_=C17[:, c * 128:(c + 1) * 128],
                                                   identity=IDF[0:NSEQ, 0:NSEQ]),
                 r=[rs_c17, rs_aux[0], rs_const], w=[bankres[b]])
            K.op("act", lambda: nc.scalar.activation(out=SILUC[:, c * NSEQ:(c + 1) * NSEQ], in_=PSB[b][:, 0:NSEQ], func=AF.Silu),
                 r=[bankres[b]], w=[rs_siluc])

        ds_x = [K.dsem() for _ in range(4)]
        n_xt = 0
        for (src, ntok, colbase) in (((xp_d, TP, 0), (xs_d, NSAMP, TP)) if SETUP_PARTS >= 4 else ()):
            for t0 in range(0, ntok, 128):
                if n_xt >= DBG_XT:
                    break
                n = min(128, ntok - t0)
                si = n_xt % 4
                n_xt += 1
                stg = AUX[:, si * 1024:(si + 1) * 1024]
                K.dma("sp", ds_x[si], [(stg[0:n, :], src[t0:t0 + n, :])], w=[rs_aux[si]])
                for half in range(2):
                    b = bank("m")
                    for cc in range(4):
                        c = half * 4 + cc
                        K.op("pe", lambda: nc.tensor.transpose(out=PSB[b][:, cc * 128: cc * 128 + n],
                                                               in_=stg[0:n, c * 128:(c + 1) * 128], identity=IDF[0:n, 0:n]),
                             r=[rs_aux[si], rs_const], w=[bankres[b]], inc=(cc == 3))
                    tl = tiles_all[min((colbase + t0) // 512, 4)]
                    for cc in range(4):
                        c = half * 4 + cc
                        eng = "dve" if (cc % 2 == 0 or DBG_XV == "dve") else "act"
                        if eng == "dve":
                            K.op("dve", lambda: nc.vector.tensor_copy(out=xv(c, colbase + t0, n), in_=PSB[b][:, cc * 128: cc * 128 + n]),
                                 r=[bankres[b]], w=[xres(tl, c)])
                        else:
                            K.op("act", lambda: nc.scalar.activation(out=xv(c, colbase + t0, n), in_=PSB[b][:, cc * 128: cc * 128 + n], func=AF.Identity),
                                 r=[bankres[b]], w=[xres(tl, c)])

        def norm_parts(t, l, s, gain_name, hdst, hres, tb=4352):
            n = t.n
            XSQ, RSTD, SQT, NT0, (rs_xsq, rs_rstd, rs_nt0) = norm_tmps(tb)
            st = {}

            def p1():
                for c in range(KC):
                    K.op("act", lambda: nc.scalar.activation(out=XSQ[:, c * 512: c * 512 + n], in_=xv(c, t.col0, n), func=AF.Square),
                         r=[xres(t, c)], w=[rs_xsq])

            def p2():
                b = bank("m")
                st["b"] = b
                for c in range(KC):
                    K.op("pe", lambda: nc.tensor.matmul(PSB[b][:, 0:n], lhsT=ONESB[:], rhs=XSQ[:, c * 512: c * 512 + n],
                                                        start=(c == 0), stop=(c == KC - 1)),
                         r=[rs_xsq, rs_const], w=[bankres[b]], inc=(c == KC - 1))
                K.op("act", lambda: nc.scalar.activation(out=SQT[:, 0:n], in_=PSB[b][:, 0:n], func=AF.Sqrt,
                                                         bias=EPSC, scale=1.0 / D),
                     r=[bankres[b], rs_const], w=[rs_rstd])

            def p3():
                K.op("dve", lambda: nc.vector.reciprocal(out=RSTD[:, 0:n], in_=SQT[:, 0:n]), r=[rs_rstd], w=[rs_rstd])
                for c in range(KC):
                    tmp = NT0[c % 2]
                    rtmp = rs_nt0[c % 2]
                    K.op("dve", lambda: nc.vector.tensor_tensor(out=tmp[:, 0:n], in0=xv(c, t.col0, n), in1=RSTD[:, 0:n], op=ALU.mult),
                         r=[xres(t, c), rs_rstd], w=[rtmp])
                    if t.kind == "p":
                        K.op("act", lambda: nc.scalar.activation(out=hdst(c), in_=tmp[:, 0:n], func=AF.Identity,
                                                                 scale=abg(s, 0, c, 0, 1), bias=abg(s, 1, c, 0, 1)),
                             r=[rtmp, rs_abg], w=[hres])
                    else:
                        K.op("dve", lambda: nc.vector.tensor_tensor(out=v3(tmp[:, 0:n], NSQ, TS), in0=v3(tmp[:, 0:n], NSQ, TS),
                                                                    in1=bc_seq(abg(s, 0, c, 1, NSQ), TS), op=ALU.mult),
                             r=[rs_abg, rtmp], w=[rtmp])
                        K.op("dve", lambda: nc.vector.tensor_tensor(out=v3(hdst(c), NSQ, TS), in0=v3(tmp[:, 0:n], NSQ, TS),
                                                                    in1=bc_seq(abg(s, 1, c, 1, NSQ), TS), op=ALU.add),
                             r=[rtmp, rs_abg], w=[hres])
            return [p1, p2, p3]

        def norm_mod(t, l, s, gain_name, hdst, hres, tb=4352):
            for pfn in norm_parts(t, l, s, gain_name, hdst, hres, tb):
                pfn()

        def norm_group(tiles, l, s, gain_name, tb):
            pp = [norm_parts(t, l, s, gain_name, lambda c, t=t: hv(c, t.loc, t.n), hres_of(t), tb) for t in tiles]
            nT = len(pp)
            pp[0][0](); pp[0][1]()
            for i in range(1, nT):
                pp[i][0]()
                pp[i - 1][2]()
                pp[i][1]()
            pp[nT - 1][2]()


        def residual(t, s, d, b):
            n = t.n
            if t.kind == "p":
                K.op("dve", lambda: nc.vector.scalar_tensor_tensor(out=xv(d, t.col0, n), in0=PSB[b][:, 0:n], scalar=abg(s, 2, d, 0, 1),
                                                                   in1=xv(d, t.col0, n), op0=ALU.mult, op1=ALU.add),
                     r=[bankres[b], rs_abg], w=[xres(t, d)])
            else:
                tmp = T64[d % 2]
                rtmp = rs_auxh[4 + 2 * (d % 2)]
                K.op("dve", lambda: nc.vector.tensor_tensor(out=v3(tmp[:, 0:n], NSQ, TS), in0=v3(PSB[b][:, 0:n], NSQ, TS),
                                                            in1=bc_seq(abg(s, 2, d, 1, NSQ), TS), op=ALU.mult),
                     r=[bankres[b], rs_abg], w=[rtmp])
                K.op("dve", lambda: nc.vector.tensor_tensor(out=xv(d, t.col0, n), in0=xv(d, t.col0, n), in1=tmp[:, 0:n], op=ALU.add),
                     r=[rtmp], w=[xres(t, d)])

        def hres_of(t):
            return res("h_%d" % t.loc)

        MTS = [small(4 * NSEQ), small(4 * NSEQ)]
        rs_mts = [res("ada_mt0"), res("ada_mt1")]
        ada_q = []
        ada_evac = []
        ada_n = [0]

        def ada_load(l, j):
            wsrc = wada_d[l].rearrange("(k p) f -> p k f", p=128)
            s_ = ring_load(lambda slot: [(slot[:, 0:4096].rearrange("p (k f) -> p k f", k=8), wsrc[:, :, j * 512:(j + 1) * 512])], ada=True)
            ada_loaded.append((l, j, s_))

        def ada_mm(l, j, s_):
            slot3 = RING[s_][:, 0:4096].rearrange("p (k f) -> p k f", k=8)
            b = bank("m")
            mi = ada_n[0] % 2
            ada_n[0] += 1
            for i in range(4):
                for k in range(KC):
                    K.op("pe", lambda: nc.tensor.matmul(PSB[b][:, i * 32: i * 32 + NSEQ], lhsT=slot3[:, k, i * 128:(i + 1) * 128],
                                                        rhs=SILUC[:, k * NSEQ:(k + 1) * NSEQ], start=(k == 0), stop=(k == KC - 1)),
                         r=[ringres[s_], rs_siluc], w=[bankres[b]], inc=(k == KC - 1 and i == 3))
            for i in range(4):
                oc = 4 * j + i
                K.op("act", lambda: nc.scalar.activation(out=MTS[mi][:, i * NSEQ:(i + 1) * NSEQ], in_=PSB[b][:, i * 32: i * 32 + NSEQ],
                                                         func=AF.Identity, bias=ptc(l, "b_ada", oc), scale=1.0),
                     r=[bankres[b], rs_pt], w=[rs_mts[mi]])
            ada_evac.append((l, j, mi))

        def ada_derive(l, j, mi):
            gains = ["norm_ffn1", "norm_mix", "norm_ffn2"]
            for i in range(4):
                oc = 4 * j + i
                m = oc // 8
                c = oc % 8
                sub, which = m // 3, m % 3
                src = MTS[mi][:, i * NSEQ:(i + 1) * NSEQ]
                if which == 0:
                    K.op("dve", lambda: nc.vector.tensor_copy(out=abg(sub, 1, c, 0, NSEQ), in_=src), r=[rs_mts[mi]], w=[rs_abg])
                elif which == 1:
                    K.op("dve", lambda: nc.vector.tensor_scalar(out=abg(sub, 0, c, 0, NSEQ), in0=src, scalar1=1.0,
                                                                scalar2=ptc(l, gains[sub], c), op0=ALU.add, op1=ALU.mult),
                         r=[rs_mts[mi], rs_pt], w=[rs_abg])
                else:
                    K.op("dve", lambda: nc.vector.tensor_scalar(out=abg(sub, 2, c, 0, NSEQ), in0=src, scalar1=(1.0 if sub == 1 else 0.5),
                                                                scalar2=None, op0=ALU.mult),
                         r=[rs_mts[mi]], w=[rs_abg])

        def ada_enqueue(l, j0, j1):
            for j in range(j0, j1):
                ada_q.append((l, j))

        def ada_pump(n=1):
            for _ in range(n):
                if ada_evac:
                    ada_derive(*ada_evac.pop(0))
                if ada_loaded:
                    ada_mm(*ada_loaded.pop(0))
                if ada_q:
                    ada_load(*ada_q.pop(0))

        def ada_flush():
            while ada_q or ada_loaded or ada_evac:
                ada_pump(1)

        def ffn(l, which, s, gain_name, group, do_norm=True, next_group=None):
            w1s = w1_d[which][l].rearrange("(k p) f -> p k f", p=128)
            w3s = w3_d[which][l].rearrange("(k p) f -> p k f", p=128)
            w2s = w2_d[which][l].rearrange("(k p) f -> p k f", p=128)
            if do_norm:
                K.barrier()
                norm_group(group, l, s, gain_name, 4352)
            def load_item(i):
                if i < 11:
                    return ring_load(lambda slot: [
                        (slot[:, 0:2048].rearrange("p (k f) -> p k f", k=8), w1s[:, :, i * 256:(i + 1) * 256]),
                        (slot[:, 2048:4096].rearrange("p (k f) -> p k f", k=8), w3s[:, :, i * 256:(i + 1) * 256])])
                d_ = i - 11
                return ring_load(lambda slot: [(slot[:, 0:2816].rearrange("p (k f) -> p k f", k=FC), w2s[:, :, d_ * 128:(d_ + 1) * 128])])
            slots_ = {0: load_item(0)}
            for j in range(11):
                slots_[j + 1] = load_item(j + 1)
                sl = slots_[j]
                s1 = RING[sl][:, 0:2048].rearrange("p (k f) -> p k f", k=8)
                s3 = RING[sl][:, 2048:4096].rearrange("p (k f) -> p k f", k=8)
                for t in group:
                    n = t.n
                    for f2 in range(2):
                        f = 2 * j + f2
                        b1 = bank("g1"); b3 = bank("g3")
                        for k in range(KC):
                            K.op("pe", lambda: nc.tensor.matmul(PSB[b1][:, 0:n], lhsT=s1[:, k, f2 * 128:(f2 + 1) * 128], rhs=hv(k, t.loc, n),
                                                                start=(k == 0), stop=(k == KC - 1)),
                                 r=[ringres[sl], hres_of(t)], w=[bankres[b1]], inc=(k == KC - 1))
                        for k in range(KC):
                            K.op("pe", lambda: nc.tensor.matmul(PSB[b3][:, 0:n], lhsT=s3[:, k, f2 * 128:(f2 + 1) * 128], rhs=hv(k, t.loc, n),
                                                                start=(k == 0), stop=(k == KC - 1)),
                                 r=[ringres[sl], hres_of(t)], w=[bankres[b3]], inc=(k == KC - 1))
                        si = role_i["g1"] % 2
                        stmp = S32[si]; rst = rs_auxh[2 * si]
                        K.op("act", lambda: nc.scalar.activation(out=stmp[:, 0:n], in_=PSB[b1][:, 0:n], func=AF.Silu),
                             r=[bankres[b1]], w=[rst])
                        K.op("dve", lambda: nc.vector.tensor_tensor(out=av(f, t.loc, n), in0=stmp[:, 0:n], in1=PSB[b3][:, 0:n], op=ALU.mult),
                             r=[rst, bankres[b3]], w=[res("a_%d_%d" % (f, t.loc))])
            for d in range(KC):
                if d + 1 < KC:
                    slots_[11 + d + 1] = load_item(11 + d + 1)
                if next_group is not None:
                    if d == 0:
                        hoist = []
                        for ti_, t in enumerate(next_group):
                            pp = norm_parts(t, l, s, gain_name, lambda c, t=t: hv(c, t.loc, t.n), hres_of(t), tb=-1)
                            for pi2, pfn in enumerate(pp):
                                hoist.append((2 * ti_ + pi2, pfn))
                    for (dd, pfn) in hoist:
                        if dd == d:
                            pfn()
                sl = slots_[11 + d]
                s2 = RING[sl][:, 0:2816].rearrange("p (k f) -> p k f", k=FC)
                for t in group:
                    n = t.n
                    b = bank("o")
                    for f in range(FC):
                        K.op("pe", lambda: nc.tensor.matmul(PSB[b][:, 0:n], lhsT=s2[:, f, :], rhs=av(f, t.loc, n),
                                                            start=(f == 0), stop=(f == FC - 1)),
                             r=[ringres[sl], res("a_%d_%d" % (f, t.loc))], w=[bankres[b]], inc=(f == FC - 1))
                    residual(t, s, d, b)

        def mod_2pi(x, m, kmax, rt):
            k = kmax
            while k >= 1:
                sk = float(k) * TWO_PI
                K.op("dve", lambda: nc.vector.tensor_scalar(out=m, in0=x, scalar1=sk, scalar2=None, op0=ALU.is_ge), r=[rt], w=[rt])
                K.op("dve", lambda: nc.vector.scalar_tensor_tensor(out=x, in0=m, scalar=-sk, in1=x, op0=ALU.mult, op1=ALU.add), r=[rt], w=[rt])
                k //= 2

        def sin_reduced(dst, src, n, tmp1, tmp2i, tmp3, rr, rw, add_half_pi=False):
            rt = res("sinred_tmp")
            K.op("dve", lambda: nc.vector.tensor_scalar(out=tmp1, in0=src, scalar1=(PI / 2 if add_half_pi else 0.0), scalar2=None, op0=ALU.add),
                 r=rr, w=[rt])
            mod_2pi(tmp1, tmp3, 128, rt)
            K.op("act", lambda: nc.scalar.activation(out=dst, in_=tmp1, func=AF.Sin, scale=-1.0, bias=PIC), r=[rt, rs_const], w=rw)


        def cmul(dr, di, ar, ai, br, bi, t1, t2, rr, rw, rtmp):
            K.op("dve", lambda: nc.vector.tensor_tensor(out=t1, in0=ar, in1=br, op=ALU.mult), r=rr, w=[rtmp])
            K.op("dve", lambda: nc.vector.tensor_tensor(out=t2, in0=ai, in1=bi, op=ALU.mult), r=rr, w=[rtmp])
            K.op("dve", lambda: nc.vector.tensor_tensor(out=dr, in0=t1, in1=t2, op=ALU.subtract), r=[rtmp], w=rw)
            K.op("dve", lambda: nc.vector.tensor_tensor(out=t1, in0=ar, in1=bi, op=ALU.mult), r=rr, w=[rtmp])
            K.op("dve", lambda: nc.vector.tensor_tensor(out=t2, in0=ai, in1=br, op=ALU.mult), r=rr, w=[rtmp])
            K.op("dve", lambda: nc.vector.tensor_tensor(out=di, in0=t1, in1=t2, op=ALU.add), r=[rtmp], w=rw)

        rs_wbu = res("wbu"); rs_wc = res("wc"); rs_wg = res("wg")
        ds_w = K.dsem(); ds_w2 = K.dsem(); ds_st = K.dsem(); ds_bn = K.dsem()
        rs_fst = res("fst")
        rs_xbs = res("xbs"); rs_h0t = res("h0t"); rs_s0 = res("s0")
        rs_hc = res("hc"); rs_carry = res("carry")
        rs_work_setup = res("work_setup")

        def mixer_setup(l):
            K.barrier()
            K.dma("pool", ds_w, [(WG[:], wg_d[l])], w=[rs_wg])
            K.dma("pool", ds_w2, [(WC[:], cpad_d[l])], w=[rs_wc])
            K.op("dve", lambda: nc.vector.tensor_copy(out=A_RE, in_=ptc(l, "s5_a_re", 0, 16)), r=[rs_pt], w=[rs_small])
            K.op("dve", lambda: nc.vector.tensor_copy(out=A_IM, in_=ptc(l, "s5_a_im", 0, 16)), r=[rs_pt], w=[rs_small])
            K.dma("sp", ds_misc, [(DTX, dtx_d[l])], w=[rs_small])
            K.op("act", lambda: nc.scalar.activation(out=DT, in_=DTX, func=AF.Exp), r=[rs_small], w=[rs_small])
            K.op("dve", lambda: nc.vector.tensor_scalar(out=ARC, in0=A_RE, scalar1=-1e-4, scalar2=None, op0=ALU.min), r=[rs_small], w=[rs_small])
            K.op("dve", lambda: nc.vector.tensor_tensor(out=TMPA, in0=ARC, in1=DT, op=ALU.mult), r=[rs_small], w=[rs_small])
            K.op("act", lambda: nc.scalar.activation(out=RHO, in_=TMPA, func=AF.Exp), r=[rs_small], w=[rs_small])
            K.op("dve", lambda: nc.vector.tensor_tensor(out=TH, in0=A_IM, in1=DT, op=ALU.mult), r=[rs_small], w=[rs_small])
            K.op("dve", lambda: nc.vector.tensor_scalar(out=TH, in0=TH, scalar1=8.0 * TWO_PI, scalar2=None, op0=ALU.add), r=[rs_small], w=[rs_small])
            mod_2pi(TH, TMPA, 8, rs_small)
            PHI = WORK[:, 0:16 * 129]
            T1 = WORK[:, 2064:2064 + 2064]
            T2 = WORK[:, 4128:4128 + 2064]
            T3 = WORK[:, 6192:6192 + 2064]
            K.op("dve", lambda: nc.vector.tensor_tensor(out=v3(PHI, 16, 129), in0=TH.unsqueeze(2).to_broadcast([128, 16, 129]),
                                                        in1=TAU[:, :].unsqueeze(1).to_broadcast([128, 16, 129]), op=ALU.mult),
                 r=[rs_small, rs_const], w=[rs_work_setup])
            sin_reduced(SIN, PHI, 2064, T1, T2.bitcast(I32), T3, [rs_work_setup], [rs_tab])
            sin_reduced(COS, PHI, 2064, T1, T2.bitcast(I32), T3, [rs_work_setup], [rs_tab], add_half_pi=True)
            c1 = cosv(0, 16, 1, 1).rearrange("p o t -> p (o t)")
            s1 = sinv(0, 16, 1, 1).rearrange("p o t -> p (o t)")
            K.op("dve", lambda: nc.vector.tensor_tensor(out=ABR, in0=RHO, in1=c1, op=ALU.mult), r=[rs_small, rs_tab], w=[rs_small])
            K.op("dve", lambda: nc.vector.tensor_tensor(out=ABI, in0=RHO, in1=s1, op=ALU.mult), r=[rs_small, rs_tab], w=[rs_small])
            K.op("dve", lambda: nc.vector.tensor_tensor(out=TMPA, in0=ARC, in1=ARC, op=ALU.mult), r=[rs_small], w=[rs_small])
            K.op("dve", lambda: nc.vector.tensor_tensor(out=TMPB, in0=A_IM, in1=A_IM, op=ALU.mult), r=[rs_small], w=[rs_small])
            K.op("dve", lambda: nc.vector.tensor_tensor(out=TMPA, in0=TMPA, in1=TMPB, op=ALU.add), r=[rs_small], w=[rs_small])
            K.op("dve", lambda: nc.vector.reciprocal(out=TMPA, in_=TMPA), r=[rs_small], w=[rs_small])
            K.op("dve", lambda: nc.vector.tensor_scalar(out=TMPB, in0=ABR, scalar1=-1.0, scalar2=None, op0=ALU.add), r=[rs_small], w=[rs_small])
            K.op("dve", lambda: nc.vector.tensor_tensor(out=TMPC, in0=TMPB, in1=ARC, op=ALU.mult), r=[rs_small], w=[rs_small])
            K.op("dve", lambda: nc.vector.tensor_tensor(out=TMPD, in0=ABI, in1=A_IM, op=ALU.mult), r=[rs_small], w=[rs_small])
            K.op("dve", lambda: nc.vector.tensor_tensor(out=TMPC, in0=TMPC, in1=TMPD, op=ALU.add), r=[rs_small], w=[rs_small])
            K.op("dve", lambda: nc.vector.tensor_tensor(out=F_R, in0=TMPC, in1=TMPA, op=ALU.mult), r=[rs_small], w=[rs_small])
            K.op("dve", lambda: nc.vector.tensor_tensor(out=TMPC, in0=ABI, in1=ARC, op=ALU.mult), r=[rs_small], w=[rs_small])
            K.op("dve", lambda: nc.vector.tensor_tensor(out=TMPD, in0=TMPB, in1=A_IM, op=ALU.mult), r=[rs_small], w=[rs_small])
            K.op("dve", lambda: nc.vector.tensor_tensor(out=TMPC, in0=TMPC, in1=TMPD, op=ALU.subtract), r=[rs_small], w=[rs_small])
            K.op("dve", lambda: nc.vector.tensor_tensor(out=F_I, in0=TMPC, in1=TMPA, op=ALU.mult), r=[rs_small], w=[rs_small])
            BNR = WORK[:, 8256: 8256 + 2048]
            BNI = WORK[:, 10304: 10304 + 2048]
            U1 = WORK[:, 12352: 12352 + 2048]
            U2 = WORK[:, 14400: 14400 + 1920]
            rs_bn = res("bnat")
            K.dma("sp", ds_bn, [(BNR, bnat_d[l, 0]), (BNI, bnat_d[l, 1])], w=[rs_bn])
            BBR = wbf(0, 2048 * 1)
            BBI = wbf(1024, 2048 * 1)
            rs_bb = res("bb")
            frb = F_R.unsqueeze(2).to_broadcast([128, 16, 128])
            fib = F_I.unsqueeze(2).to_broadcast([128, 16, 128])
            for hh in range(2):
                o0 = hh * 8
                sl_ = slice(o0 * 128, (o0 + 8) * 128)
                fr_h = F_R[:, o0:o0 + 8].unsqueeze(2).to_broadcast([128, 8, 128])
                fi_h = F_I[:, o0:o0 + 8].unsqueeze(2).to_broadcast([128, 8, 128])
                u1 = v3(U1[:, 0:1024], 8, 128); u2 = v3(U2[:, 0:1024], 8, 128)
                bnr = v3(BNR[:, sl_], 8, 128); bni = v3(BNI[:, sl_], 8, 128)
                rtu = res("bb_tmp")
                K.op("dve", lambda: nc.vector.tensor_tensor(out=u1, in0=bnr, in1=fr_h, op=ALU.mult), r=[rs_bn, rs_small, rs_tab], w=[rtu])
                K.op("dve", lambda: nc.vector.tensor_tensor(out=u2, in0=bni, in1=fi_h, op=ALU.mult), r=[rs_bn, rs_small], w=[rtu])
                K.op("dve", lambda: nc.vector.tensor_tensor(out=v3(BBR[:, sl_], 8, 128), in0=u1, in1=u2, op=ALU.subtract), r=[rtu], w=[rs_bb])
                K.op("dve", lambda: nc.vector.tensor_tensor(out=u1, in0=bnr, in1=fi_h, op=ALU.mult), r=[rs_bn, rs_small], w=[rtu])
                K.op("dve", lambda: nc.vector.tensor_tensor(out=u2, in0=bni, in1=fr_h, op=ALU.mult), r=[rs_bn, rs_small], w=[rtu])
                K.op("dve", lambda: nc.vector.tensor_tensor(out=v3(BBI[:, sl_], 8, 128), in0=u1, in1=u2, op=ALU.add), r=[rtu], w=[rs_bb])
            for comp, BB in ((0, BBR), (1, BBI)):
                for o8 in range(2):
                    b = bank("m")
                    pb = PSB[b][:, :].bitcast(BF16)
                    for oi in range(8):
                        o = o8 * 8 + oi
                        K.op("pe", lambda: nc.tensor.transpose(out=pb[:, oi * 128:(oi + 1) * 128], in_=BB[:, o * 128:(o + 1) * 128], identity=IDB[:]),
                             r=[rs_bb, rs_const], w=[bankres[b]], inc=(oi == 7))
                    base = (comp * 16 + o8 * 8) * 128
                    K.op("dve", lambda: nc.vector.tensor_copy(out=WBU[:, base: base + 1024], in_=pb[:, 0:1024]), r=[bankres[b]], w=[rs_wbu])
            K.op("act", lambda: nc.scalar.activation(out=SPT, in_=ptc(l, "lru_lambda", 0, 4), func=AF.Exp, scale=-1.0), r=[rs_pt], w=[rs_small])
            K.op("dve", lambda: nc.vector.tensor_scalar(out=SPT, in0=SPT, scalar1=1.0, scalar2=None, op0=ALU.add), r=[rs_small], w=[rs_small])
            K.op("act", lambda: nc.scalar.activation(out=SPT, in_=SPT, func=AF.Ln), r=[rs_small], w=[rs_small])
            K.op("dve", lambda: nc.vector.tensor_scalar(out=SC1, in0=SPT, scalar1=-8.0, scalar2=None, op0=ALU.mult), r=[rs_small], w=[rs_small])
            K.op("dve", lambda: nc.vector.tensor_scalar(out=SC2, in0=SPT, scalar1=-16.0, scalar2=None, op0=ALU.mult), r=[rs_small], w=[rs_small])
            K.op("dve", lambda: nc.vector.tensor_scalar(out=SCH, in0=SPT, scalar1=-4.0, scalar2=None, op0=ALU.mult), r=[rs_small], w=[rs_small])
            K.op("dve", lambda: nc.vector.tensor_scalar(out=HBG[:, 0:4], in0=ptc(l, "b_rg", 0, 4), scalar1=0.5, scalar2=None, op0=ALU.mult), r=[rs_pt], w=[rs_small])
            K.op("dve", lambda: nc.vector.tensor_scalar(out=HBG[:, 4:8], in0=ptc(l, "b_ig", 0, 4), scalar1=0.5, scalar2=None, op0=ALU.mult), r=[rs_pt], w=[rs_small])
            STG = WORK[0:NSQ, 8256: 8256 + 2048]
            xbs4 = XBS.rearrange("p (c s k) -> p c s k", c=4, s=NSQ, k=7)
            K.dma("sp", ds_st, [(STG[:, 0:1536], stc_d[l])], r=[], w=[rs_bn])
            for k3 in range(3):
                b = bank("m")
                for c in range(4):
                    K.op("pe", lambda: nc.tensor.transpose(out=PSB[b][:, c * 16:(c + 1) * 16], in_=STG[:, k3 * 512 + c * 128: k3 * 512 + (c + 1) * 128],
                                                           identity=IDF[0:NSQ, 0:NSQ]),
                         r=[rs_bn, rs_const], w=[bankres[b]], inc=(c == 3))
                K.op("dve", lambda: nc.vector.tensor_copy(out=xbs4[:, :, :, k3], in_=v3(PSB[b][:, 0:64], 4, 16)), r=[bankres[b]], w=[rs_xbs])
            K.dma("sp", ds_st, [(STG[:, 0:512], sth_d[l])], w=[rs_bn])
            b = bank("m")
            for c in range(4):
                K.op("pe", lambda: nc.tensor.transpose(out=PSB[b][:, c * 16:(c + 1) * 16], in_=STG[:, c * 128:(c + 1) * 128], identity=IDF[0:NSQ, 0:NSQ]),
                     r=[rs_bn, rs_const], w=[bankres[b]], inc=(c == 3))
            K.op("dve", lambda: nc.vector.tensor_copy(out=H0T, in_=PSB[b][:, 0:64]), r=[bankres[b]], w=[rs_h0t])
            for (srcd, dst) in ((str_d, S0R), (sti_d, S0I)):
                K.dma("sp", ds_st, [(STG[:, 0:2048], srcd[l])], w=[rs_bn])
                b = bank("m")
                for o in range(16):
                    K.op("pe", lambda: nc.tensor.transpose(out=PSB[b][:, o * 16:(o + 1) * 16], in_=STG[:, o * 128:(o + 1) * 128], identity=IDF[0:NSQ, 0:NSQ]),
                         r=[rs_bn, rs_const], w=[bankres[b]], inc=(o == 15))
                K.op("dve", lambda: nc.vector.tensor_copy(out=dst, in_=PSB[b][:, 0:256]), r=[bankres[b]], w=[rs_s0])
            c1b = cosv(0, 16, 1, 1).to_broadcast([128, 16, NSQ])
            s1b = sinv(0, 16, 1, 1).to_broadcast([128, 16, NSQ])
            t1 = v3(TAILX[:, 0:256], 16, NSQ); t2 = v3(TAILX[:, 256:512], 16, NSQ)
            cmul(v3(S0TR, 16, NSQ), v3(S0TI, 16, NSQ), v3(S0R, 16, NSQ), v3(S0I, 16, NSQ), c1b, s1b, t1, t2,
                 [rs_s0, rs_tab], [rs_s0], res("tailx"))
            K.op("dve", lambda: nc.vector.memset(HC, 0.0), w=[rs_hc])
            K.op("dve", lambda: nc.vector.memset(CR, 0.0), w=[rs_carry])
            K.op("dve", lambda: nc.vector.memset(CI, 0.0), w=[rs_carry])

        def mixer_group(l, gi, group):
            s = 1
            K.barrier()
            wins = win_d[l].rearrange("(k p) f -> p k f", p=128)
            norm_group(group, l, s, "norm_mix", 8716)
            xbs4 = XBS.rearrange("p (c s k) -> p c s k", c=4, s=NSQ, k=7)
            for part in range(3):
                sl = ring_load(lambda slot: [(slot[:, 0:4096].rearrange("p (k f) -> p k f", k=8), wins[:, :, part * 512:(part + 1) * 512])])
                s3 = RING[sl][:, 0:4096].rearrange("p (k f) -> p k f", k=8)
                for t in group:
                    n = t.n
                    for oc in range(4):
                        role = ("g1", "g3", "o")[(oc + part) % 3]
                        b = bank(role)
                        for k in range(KC):
                            K.op("pe", lambda: nc.tensor.matmul(PSB[b][:, 0:n], lhsT=s3[:, k, oc * 128:(oc + 1) * 128], rhs=hv(k, t.loc, n),
                                                                start=(k == 0), stop=(k == KC - 1)),
                                 r=[ringres[sl], hres_of(t)], w=[bankres[b]], inc=(k == KC - 1))
                        if part == 0:
                            if t.kind == "p":
                                K.op("dve", lambda: nc.vector.tensor_copy(out=xbv(oc, 3 + t.loc, n), in_=PSB[b][:, 0:n]),
                                     r=[bankres[b]], w=[res("xb_%d" % t.loc)])
                            else:
                                K.op("dve", lambda: nc.vector.tensor_copy(out=xbs4[:, oc, :, 3:7], in_=v3(PSB[b][:, 0:n], NSQ, TS)),
                                     r=[bankres[b]], w=[rs_xbs])
                        elif part == 1:
                            K.op("act", lambda: nc.scalar.activation(out=gyv(oc, t.loc, n), in_=PSB[b][:, 0:n], func=AF.Gelu_apprx_tanh),
                                 r=[bankres[b]], w=[res("gy_%d" % t.loc)])
                        else:
                            K.op("act", lambda: nc.scalar.activation(out=uv(oc, t.loc, n), in_=PSB[b][:, 0:n], func=AF.Identity),
                                 r=[bankres[b]], w=[res("u_%d" % t.loc)])
            K.barrier()
            if gi == 0:
                K.op("dve", lambda: nc.vector.memset(XB.rearrange("p (c w) -> p c w", c=4)[:, :, 0:3], 0.0), w=[res("xb_hist")])
            else:
                K.op("dve", lambda: nc.vector.tensor_copy(out=XB.rearrange("p (c w) -> p c w", c=4)[:, :, 0:3], in_=v3(HIST, 4, 3)),
                     r=[res("hist_save")], w=[res("xb_hist")])
            NL = 256
            LSETS = []
            for si_ in range(4):
                base = si_ * 1024
                xcbw = TAILX[:, si_ * 128:(si_ + 1) * 128].bitcast(BF16)
                LSETS.append(([WORK[:, base + i * NL: base + (i + 1) * NL] for i in range(4)], xcbw, res("lru_set%d" % si_)))
            lunits = []
            for t in group:
                subs = [(t.loc + i * NL, NL) for i in range(t.n // NL)] if t.kind == "p" else [(t.loc, t.n)]
                for (sloc, n) in subs:
                    for c in range(4):
                        lunits.append((t, sloc, n, c))
            batches = [lunits[i:i + 2] for i in range(0, len(lunits), 2)]

            def lru_A(k):
                for bi_, (t, sloc, n, c) in enumerate(batches[k]):
                    bufs, xcbf, rlt = LSETS[2 * (k % 2) + bi_]
                    XC, RG, IG, AA = [x[:, 0:n] for x in bufs]
                    xcb = xcbf[:, 0:n]
                    if t.kind == "p":
                        srcs = [xbv(c, sloc + kk, n) for kk in range(4)]
                        rsrc = [res("xb_%d" % t.loc)]
                        if sloc == t.loc:
                            rsrc.append(res("xb_%d" % (t.loc - 512)) if t.loc > 0 else res("xb_hist"))
                        shp = lambda a: a
                    else:
                        srcs = [xbs4[:, c, :, kk:kk + 4] for kk in range(4)]
                        rsrc = [rs_xbs]
                        shp = lambda a: v3(a, NSQ, TS)
                    K.op("dve", lambda: nc.vector.tensor_scalar(out=shp(XC), in0=srcs[3], scalar1=ptc(l, "conv_w", 3 * 4 + c),
                                                                scalar2=ptc(l, "conv_b", c), op0=ALU.mult, op1=ALU.add),
                         r=rsrc + [rs_pt], w=[rlt])
                    for kk in range(3):
                        K.op("dve", lambda: nc.vector.scalar_tensor_tensor(out=shp(XC), in0=srcs[kk], scalar=ptc(l, "conv_w", kk * 4 + c),
                                                                           in1=shp(XC), op0=ALU.mult, op1=ALU.add),
                             r=rsrc + [rs_pt, rlt], w=[rlt])
                    K.op("dve", lambda: nc.vector.tensor_copy(out=xcb, in_=XC), r=[rlt], w=[rlt])
                    b1 = bank("g1"); b3 = bank("g3")
                    K.op("pe", lambda: nc.tensor.matmul(PSB[b1][:, 0:n], lhsT=WG[:, c * 128:(c + 1) * 128], rhs=xcb, start=True, stop=True),
                         r=[rlt, rs_wg], w=[bankres[b1]])
                    K.op("pe", lambda: nc.tensor.matmul(PSB[b3][:, 0:n], lhsT=WG[:, (4 + c) * 128:(5 + c) * 128], rhs=xcb, start=True, stop=True),
                         r=[rlt, rs_wg], w=[bankres[b3]])
                    K.op("act", lambda: nc.scalar.activation(out=RG, in_=PSB[b1][:, 0:n], func=AF.Tanh, bias=HBG[:, c:c + 1], scale=0.5),
                         r=[bankres[b1], rs_small], w=[rlt])
                    K.op("act", lambda: nc.scalar.activation(out=IG, in_=PSB[b3][:, 0:n], func=AF.Tanh, bias=HBG[:, 4 + c:5 + c], scale=0.5),
                         r=[bankres[b3], rs_small], w=[rlt])
                    K.op("act", lambda: nc.scalar.activation(out=AA, in_=RG, func=AF.Exp, scale=SCH[:, c:c + 1], bias=SCH[:, c:c + 1]),
                         r=[rlt, rs_small], w=[rlt])
                    K.op("act", lambda: nc.scalar.activation(out=RG, in_=RG, func=AF.Exp, scale=SC1[:, c:c + 1], bias=SC1[:, c:c + 1]),
                         r=[rlt, rs_small], w=[rlt])

            def lru_B(k):
                for bi_, (t, sloc, n, c) in enumerate(batches[k]):
                    bufs, xcbf, rlt = LSETS[2 * (k % 2) + bi_]
                    MM = bufs[1][:, 0:n]
                    K.op("act", lambda: nc.scalar.activation(out=MM, in_=MM, func=AF.Sqrt, scale=-1.0, bias=ONEC), r=[rlt, rs_const], w=[rlt])

            def lru_C(k):
                for bi_, (t, sloc, n, c) in enumerate(batches[k]):
                    bufs, xcbf, rlt = LSETS[2 * (k % 2) + bi_]
                    XC, MM, IG, AA = [x[:, 0:n] for x in bufs]
                    BBt = IG; HS = MM
                    K.op("dve", lambda: nc.vector.tensor_scalar(out=IG, in0=IG, scalar1=0.5, scalar2=0.5, op0=ALU.mult, op1=ALU.add), r=[rlt], w=[rlt])
                    K.op("dve", lambda: nc.vector.tensor_tensor(out=BBt, in0=MM, in1=IG, op=ALU.mult), r=[rlt], w=[rlt])
                    K.op("dve", lambda: nc.vector.tensor_tensor(out=BBt, in0=BBt, in1=XC, op=ALU.mult), r=[rlt], w=[rlt])
                    if t.kind == "p":
                        K.op("dve", lambda: nc.vector.tensor_tensor_scan(out=HS, data0=AA, data1=BBt, initial=HC[:, c:c + 1],
                                                                         op0=ALU.mult, op1=ALU.add), r=[rlt, rs_hc], w=[rlt])
                        K.op("dve", lambda: nc.vector.tensor_copy(out=HC[:, c:c + 1], in_=HS[:, n - 1:n]), r=[rlt], w=[rs_hc])
                        if t.col0 + (sloc - t.loc) + n == TP:
                            K.op("dve", lambda: nc.vector.tensor_copy(out=FST[:, (12 + c) * NSEQ:(12 + c) * NSEQ + 1], in_=HS[:, n - 1:n]),
                                 r=[rlt], w=[rs_fst])
                            for kk in range(3):
                                K.op("dve", lambda: nc.vector.tensor_copy(out=FST[:, (c * 3 + kk) * NSEQ:(c * 3 + kk) * NSEQ + 1],
                                                                          in_=xbv(c, 3 + sloc + n - 3 + kk, 1)),
                                     r=[res("xb_%d" % t.loc)], w=[rs_fst])
                    else:
                        hs3 = v3(HS, NSQ, TS); aa3 = v3(AA, NSQ, TS); bb3 = v3(BBt, NSQ, TS)
                        for tt in range(TS):
                            prev = H0T[:, c * NSQ:(c + 1) * NSQ] if tt == 0 else hs3[:, :, tt - 1]
                            K.op("dve", lambda: nc.vector.tensor_tensor(out=hs3[:, :, tt], in0=aa3[:, :, tt], in1=prev, op=ALU.mult),
                                 r=[rlt, rs_h0t], w=[rlt])
                            K.op("dve", lambda: nc.vector.tensor_tensor(out=hs3[:, :, tt], in0=hs3[:, :, tt], in1=bb3[:, :, tt], op=ALU.add),
                                 r=[rlt], w=[rlt])
                        K.op("dve", lambda: nc.vector.tensor_copy(out=FST[:, (12 + c) * NSEQ + 1:(12 + c + 1) * NSEQ], in_=hs3[:, :, TS - 1]),
                             r=[rlt], w=[rs_fst])
                        for kk in range(3):
                            K.op("dve", lambda: nc.vector.tensor_copy(out=FST[:, (c * 3 + kk) * NSEQ + 1:(c * 3 + kk + 1) * NSEQ],
                                                                      in_=xbs4[:, c, :, 4 + kk]),
                                 r=[rs_xbs], w=[rs_fst])
                    K.op("dve", lambda: nc.vector.tensor_tensor(out=ylv(c, sloc, n), in0=gyv(c, sloc, n), in1=HS, op=ALU.mult),
                         r=[rlt, res("gy_%d" % t.loc)], w=[res("yl_%d" % t.loc)])

            lru_A(0)
            for k in range(len(batches)):
                if k + 1 < len(batches):
                    lru_A(k + 1)
                lru_B(k)
                lru_C(k)
            if gi == 0:
                xb3 = XB.rearrange("p (c w) -> p c w", c=4)
                K.op("dve", lambda: nc.vector.tensor_copy(out=v3(HIST, 4, 3), in_=xb3[:, :, 1024:1027]),
                     r=[res("xb_512")], w=[res("hist_save")])
            K.barrier()
            Tt = [WORK[:, i * 512:(i + 1) * 512] for i in range(4)]
            BTRf = WORK[:, 2048:2560]; BTIf = WORK[:, 2560:3072]; STRf = WORK[:, 3072:3584]; STIf = WORK[:, 3584:4096]
            PREf = WORK[:, 4096:4224]
            RHOT = WORK[:, 4352:6400]
            SRBs = [wbf(6400, 512), wbf(6656, 512)]
            NSIBs = [wbf(6912, 512), wbf(7168, 512)]
            rT = [res("s5_t%d" % i) for i in range(4)]
            r_btr = res("s5_btr"); r_bti = res("s5_bti"); r_str = res("s5_str"); r_sti = res("s5_sti")
            r_bt = [r_btr, r_bti]; r_st = [r_str, r_sti]; rpre = res("s5_pre"); r_rhot = res("s5_rhot")
            r_slr = res("s5_slr"); r_sli = res("s5_sli")
            r_srb = [res("s5_srb0"), res("s5_srb1")]
            r_crp = res("s5_crp")
            K.op("dve", lambda: nc.vector.tensor_copy(out=v3(RHOT, 16, NT5), in_=RHO.unsqueeze(2).to_broadcast([128, 16, NT5])),
                 r=[rs_small], w=[r_rhot])
            K.op("dve", lambda: nc.vector.memset(v3(RHOT, 16, NT5)[:, :, 0:1], 0.0), w=[r_rhot])
            units = []
            for t in group:
                if t.kind == "p":
                    for i in range(t.n // NT5):
                        for q in range(4):
                            units.append((t, t.loc + i * NT5, q))
            n = NT5
            ubanks = {}

            def emit_bu(ui):
                t, sloc, q = units[ui]
                br_b = bank("g1"); bi_b = bank("g3")
                ubanks[ui] = (br_b, bi_b)
                for oi in range(4):
                    o = 4 * q + oi
                    K.op("pe", lambda: nc.tensor.matmul(PSB[br_b][:, oi * n:(oi + 1) * n], lhsT=WBU[:, o * 128:(o + 1) * 128], rhs=uv(q, sloc, n),
                                                        start=True, stop=True),
                         r=[rs_wbu, res("u_%d" % t.loc)], w=[bankres[br_b]], inc=(oi == 3))
                for oi in range(4):
                    o = 4 * q + oi
                    K.op("pe", lambda: nc.tensor.matmul(PSB[bi_b][:, oi * n:(oi + 1) * n], lhsT=WBU[:, (16 + o) * 128:(17 + o) * 128], rhs=uv(q, sloc, n),
                                                        start=True, stop=True),
                         r=[rs_wbu, res("u_%d" % t.loc)], w=[bankres[bi_b]], inc=(oi == 3))

            pending_post = [None]

            def emit_post():
                if pending_post[0] is None:
                    return
                (t, sloc, q, yb_) = pending_post[0]
                pending_post[0] = None
                K.op("dve", lambda: nc.vector.scalar_tensor_tensor(out=PREf[:, 0:n], in0=uv(q, sloc, n), scalar=ptc(l, "s5_d", q),
                                                                   in1=PSB[yb_][:, 0:n], op0=ALU.mult, op1=ALU.add),
                     r=[bankres[yb_], res("u_%d" % t.loc), rs_pt], w=[rpre])
                K.op("act", lambda: nc.scalar.activation(out=gyv(q, sloc, n), in_=PREf[:, 0:n], func=AF.Gelu_apprx_tanh),
                     r=[rpre], w=[res("gy_%d" % t.loc)])

            if units:
                emit_bu(0)
            for ui, (t, sloc, q) in enumerate(units):
                if ui % 2 == 1:
                    ada_pump(1)
                if ui + 1 < len(units):
                    emit_bu(ui + 1)
                br_b, bi_b = ubanks[ui]
                if q == 0:
                    K.op("dve", lambda: nc.vector.tensor_tensor(out=CRPR, in0=RHO, in1=CR, op=ALU.mult), r=[rs_small, rs_carry], w=[r_crp])
                    K.op("dve", lambda: nc.vector.tensor_tensor(out=CRPI, in0=RHO, in1=CI, op=ALU.mult), r=[rs_small, rs_carry], w=[r_crp])
                cs = cosv(4 * q, 4, 0, n); sn = sinv(4 * q, 4, 0, n)
                sh = lambda a: v3(a, 4, n)
                pbr = sh(PSB[br_b][:, 0:4 * n]); pbi = sh(PSB[bi_b][:, 0:4 * n])
                K.op("dve", lambda: nc.vector.tensor_tensor(out=sh(Tt[0]), in0=pbr, in1=cs, op=ALU.mult), r=[bankres[br_b], rs_tab], w=[rT[0]])
                K.op("dve", lambda: nc.vector.tensor_tensor(out=sh(Tt[1]), in0=pbi, in1=sn, op=ALU.mult), r=[bankres[bi_b], rs_tab], w=[rT[1]])
                K.op("dve", lambda: nc.vector.tensor_tensor(out=sh(Tt[2]), in0=pbi, in1=cs, op=ALU.mult), r=[bankres[bi_b], rs_tab], w=[rT[2]])
                K.op("dve", lambda: nc.vector.tensor_tensor(out=sh(Tt[3]), in0=pbr, in1=sn, op=ALU.mult), r=[bankres[br_b], rs_tab], w=[rT[3]])
                emit_post()
                K.op("dve", lambda: nc.vector.tensor_tensor(out=BTRf, in0=Tt[0], in1=Tt[1], op=ALU.add), r=[rT[0], rT[1]], w=[r_btr])
                K.op("dve", lambda: nc.vector.tensor_tensor(out=BTIf, in0=Tt[2], in1=Tt[3], op=ALU.subtract), r=[rT[2], rT[3]], w=[r_bti])
                K.op("dve", lambda: nc.vector.tensor_tensor(out=sh(BTRf)[:, :, 0], in0=sh(BTRf)[:, :, 0], in1=CRPR[:, 4 * q:4 * q + 4], op=ALU.add),
                     r=[r_crp, r_btr], w=[r_btr])
                K.op("dve", lambda: nc.vector.tensor_tensor(out=sh(BTIf)[:, :, 0], in0=sh(BTIf)[:, :, 0], in1=CRPI[:, 4 * q:4 * q + 4], op=ALU.add),
                     r=[r_crp, r_bti], w=[r_bti])
                rh = RHOT[:, 4 * q * n:(4 * q + 4) * n]
                K.op("dve", lambda: nc.vector.tensor_tensor_scan(out=STRf, data0=rh, data1=BTRf, initial=0.0, op0=ALU.mult, op1=ALU.add),
                     r=[r_btr, r_rhot], w=[r_str])
                K.op("dve", lambda: nc.vector.tensor_tensor_scan(out=STIf, data0=rh, data1=BTIf, initial=0.0, op0=ALU.mult, op1=ALU.add),
                     r=[r_bti, r_rhot], w=[r_sti])
                K.op("dve", lambda: nc.vector.tensor_tensor(out=sh(Tt[0]), in0=sh(STRf), in1=cs, op=ALU.mult), r=[r_str, rs_tab], w=[rT[0]])
                K.op("dve", lambda: nc.vector.tensor_tensor(out=sh(Tt[2]), in0=sh(STRf), in1=sn, op=ALU.mult), r=[r_str, rs_tab], w=[rT[2]])
                K.op("dve", lambda: nc.vector.tensor_tensor(out=sh(Tt[1]), in0=sh(STIf), in1=sn, op=ALU.mult), r=[r_sti, rs_tab], w=[rT[1]])
                K.op("dve", lambda: nc.vector.tensor_tensor(out=sh(Tt[3]), in0=sh(STIf), in1=cs, op=ALU.mult), r=[r_sti, rs_tab], w=[rT[3]])
                sbi = ui % 2
                SRB = SRBs[sbi]; NSIB = NSIBs[sbi]; rsr = r_srb[sbi]
                K.op("dve", lambda: nc.vector.tensor_tensor(out=SRB[:, 0:4 * n], in0=Tt[0], in1=Tt[1], op=ALU.subtract), r=[rT[0], rT[1]], w=[rsr])
                K.op("dve", lambda: nc.vector.scalar_tensor_tensor(out=NSIB[:, 0:4 * n], in0=Tt[2], scalar=-1.0, in1=Tt[3], op0=ALU.mult, op1=ALU.subtract),
                     r=[rT[2], rT[3]], w=[rsr])
                K.op("dve", lambda: nc.vector.tensor_copy(out=SLR[:, 4 * q:4 * q + 4], in_=sh(STRf)[:, :, n - 1]), r=[r_str], w=[r_slr])
                K.op("dve", lambda: nc.vector.tensor_copy(out=SLI[:, 4 * q:4 * q + 4], in_=sh(STIf)[:, :, n - 1]), r=[r_sti], w=[r_sli])
                yb_ = bank("o")
                for oi in range(4):
                    o = 4 * q + oi
                    K.op("pe", lambda: nc.tensor.matmul(PSB[yb_][:, 0:n], lhsT=WC[:, o * 128:(o + 1) * 128], rhs=SRB[:, oi * n:(oi + 1) * n],
                                                        start=(oi == 0), stop=False), r=[rs_wc, rsr], w=[bankres[yb_]], inc=False)
                    K.op("pe", lambda: nc.tensor.matmul(PSB[yb_][:, 0:n], lhsT=WC[:, (16 + o) * 128:(17 + o) * 128], rhs=NSIB[:, oi * n:(oi + 1) * n],
                                                        start=False, stop=(oi == 3)), r=[rs_wc, rsr], w=[bankres[yb_]], inc=(oi == 3))
                pending_post[0] = (t, sloc, q, yb_)
                if q == 3:
                    cN = cosv(0, 16, NT5, 1).rearrange("p o t -> p (o t)"); sN = sinv(0, 16, NT5, 1).rearrange("p o t -> p (o t)")
                    if t.col0 + (sloc - t.loc) + n == TP:
                        cE = cosv(0, 16, NT5 - 1, 1); sE = sinv(0, 16, NT5 - 1, 1)
                        fr = v3(FST[:, 16 * NSEQ:32 * NSEQ], 16, NSEQ)[:, :, 0:1]
                        fi = v3(FST[:, 32 * NSEQ:48 * NSEQ], 16, NSEQ)[:, :, 0:1]
                        cmul(fr, fi, SLR.unsqueeze(2), SLI.unsqueeze(2), cE, sE, TMPC.unsqueeze(2), TMPD.unsqueeze(2),
                             [r_slr, r_sli, rs_tab], [rs_fst], res("s5_cm_tmp"))
                    else:
                        cmul(CR, CI, SLR, SLI, cN, sN, TMPC, TMPD, [r_slr, r_sli, rs_tab], [rs_carry], res("s5_cm_tmp"))
            emit_post()
            ada_flush()
            ST = [BTRf, BTIf, STRf, STIf, Tt[0], Tt[1]]
            rst = res("s5_tmp"); rsr = r_srb[0]
            SRB = SRBs[0]; NSIB = NSIBs[0]
            for t in group:
                if t.kind != "s":
                    continue
                sloc, n = t.loc, t.n
                for q in range(4):
                    br_b = bank("g1"); bi_b = bank("g3")
                    for oi in range(4):
                        o = 4 * q + oi
                        K.op("pe", lambda: nc.tensor.matmul(PSB[br_b][:, oi * n:(oi + 1) * n], lhsT=WBU[:, o * 128:(o + 1) * 128], rhs=uv(q, sloc, n),
                                                            start=True, stop=True),
                             r=[rs_wbu, res("u_%d" % t.loc)], w=[bankres[br_b]], inc=(oi == 3))
                    for oi in range(4):
                        o = 4 * q + oi
                        K.op("pe", lambda: nc.tensor.matmul(PSB[bi_b][:, oi * n:(oi + 1) * n], lhsT=WBU[:, (16 + o) * 128:(17 + o) * 128], rhs=uv(q, sloc, n),
                                                            start=True, stop=True),
                             r=[rs_wbu, res("u_%d" % t.loc)], w=[bankres[bi_b]], inc=(oi == 3))
                    allT = [rT[0], rT[1], rT[2], rT[3], r_bt, r_st]
                    BTR, BTI, STR, STI, T1, T2 = [x[:, 0:4 * n] for x in ST]
                    cs = cosv(4 * q, 4, 0, TS).unsqueeze(2).to_broadcast([128, 4, NSQ, TS])
                    sn = sinv(4 * q, 4, 0, TS).unsqueeze(2).to_broadcast([128, 4, NSQ, TS])
                    sh = lambda a: a.rearrange("p (o s t) -> p o s t", o=4, s=NSQ, t=TS)
                    pbr = sh(PSB[br_b][:, 0:4 * n]); pbi = sh(PSB[bi_b][:, 0:4 * n])
                    K.op("dve", lambda: nc.vector.tensor_tensor(out=sh(T1), in0=pbr, in1=cs, op=ALU.mult), r=[bankres[br_b], rs_tab], w=[rst, allT])
                    K.op("dve", lambda: nc.vector.tensor_tensor(out=sh(T2), in0=pbi, in1=sn, op=ALU.mult), r=[bankres[bi_b], rs_tab], w=[rst])
                    K.op("dve", lambda: nc.vector.tensor_tensor(out=BTR, in0=T1, in1=T2, op=ALU.add), r=[rst], w=[rst])
                    K.op("dve", lambda: nc.vector.tensor_tensor(out=sh(T1), in0=pbi, in1=cs, op=ALU.mult), r=[bankres[bi_b], rs_tab], w=[rst])
                    K.op("dve", lambda: nc.vector.tensor_tensor(out=sh(T2), in0=pbr, in1=sn, op=ALU.mult), r=[bankres[br_b], rs_tab], w=[rst])
                    K.op("dve", lambda: nc.vector.tensor_tensor(out=BTI, in0=T1, in1=T2, op=ALU.subtract), r=[rst], w=[rst])
                    str4 = sh(STR); sti4 = sh(STI); btr4 = sh(BTR); bti4 = sh(BTI)
                    rb = RHO[:, 4 * q:4 * q + 4].unsqueeze(2).to_broadcast([128, 4, NSQ])
                    for tt in range(TS):
                        for (s4, b4, s0) in ((str4, btr4, S0TR), (sti4, bti4, S0TI)):
                            prev = v3(s0, 16, NSQ)[:, 4 * q:4 * q + 4, :] if tt == 0 else s4[:, :, :, tt - 1]
                            K.op("dve", lambda: nc.vector.tensor_tensor(out=s4[:, :, :, tt], in0=prev, in1=rb, op=ALU.mult),
                                 r=[rst, rs_s0, rs_small], w=[rst])
                            K.op("dve", lambda: nc.vector.tensor_tensor(out=s4[:, :, :, tt], in0=s4[:, :, :, tt], in1=b4[:, :, :, tt], op=ALU.add),
                                 r=[rst], w=[rst])
                    c3 = cosv(4 * q, 4, TS - 1, 1).to_broadcast([128, 4, NSQ]); s3_ = sinv(4 * q, 4, TS - 1, 1).to_broadcast([128, 4, NSQ])
                    fr = v3(FST[:, (16 + 4 * q) * NSEQ:(16 + 4 * q + 4) * NSEQ], 4, NSEQ)[:, :, 1:NSEQ]
                    fi = v3(FST[:, (32 + 4 * q) * NSEQ:(32 + 4 * q + 4) * NSEQ], 4, NSEQ)[:, :, 1:NSEQ]
                    tt1 = v3(TAILX[:, 0:64], 4, NSQ); tt2 = v3(TAILX[:, 64:128], 4, NSQ)
                    cmul(fr, fi, str4[:, :, :, TS - 1], sti4[:, :, :, TS - 1], c3, s3_, tt1, tt2, [rst, rs_tab], [rs_fst], res("tailx"))
                    K.op("dve", lambda: nc.vector.tensor_tensor(out=sh(T1), in0=sh(STR), in1=cs, op=ALU.mult), r=[rst, rs_tab], w=[rst])
                    K.op("dve", lambda: nc.vector.tensor_tensor(out=sh(T2), in0=sh(STI), in1=sn, op=ALU.mult), r=[rst, rs_tab], w=[rst])
                    K.op("dve", lambda: nc.vector.tensor_tensor(out=SRB[:, 0:4 * n], in0=T1, in1=T2, op=ALU.subtract), r=[rst], w=[rsr])
                    K.op("dve", lambda: nc.vector.tensor_tensor(out=sh(T1), in0=sh(STR), in1=sn, op=ALU.mult), r=[rst, rs_tab], w=[rst])
                    K.op("dve", lambda: nc.vector.tensor_tensor(out=sh(T2), in0=sh(STI), in1=cs, op=ALU.mult), r=[rst, rs_tab], w=[rst])
                    K.op("dve", lambda: nc.vector.scalar_tensor_tensor(out=NSIB[:, 0:4 * n], in0=T1, scalar=-1.0, in1=T2, op0=ALU.mult, op1=ALU.subtract),
                         r=[rst], w=[rsr])
                    yb_ = bank("o")
                    for oi in range(4):
                        o = 4 * q + oi
                        K.op("pe", lambda: nc.tensor.matmul(PSB[yb_][:, 0:n], lhsT=WC[:, o * 128:(o + 1) * 128], rhs=SRB[:, oi * n:(oi + 1) * n],
                                                            start=(oi == 0), stop=False), r=[rs_wc, rsr], w=[bankres[yb_]], inc=False)
                        K.op("pe", lambda: nc.tensor.matmul(PSB[yb_][:, 0:n], lhsT=WC[:, (16 + o) * 128:(17 + o) * 128], rhs=NSIB[:, oi * n:(oi + 1) * n],
                                                            start=False, stop=(oi == 3)), r=[rs_wc, rsr], w=[bankres[yb_]], inc=(oi == 3))
                    K.op("dve", lambda: nc.vector.scalar_tensor_tensor(out=PREf[:, 0:n], in0=uv(q, sloc, n), scalar=ptc(l, "s5_d", q),
                                                                       in1=PSB[yb_][:, 0:n], op0=ALU.mult, op1=ALU.add),
                         r=[bankres[yb_], res("u_%d" % t.loc), rs_pt], w=[rpre])
                    K.op("act", lambda: nc.scalar.activation(out=gyv(q, sloc, n), in_=PREf[:, 0:n], func=AF.Gelu_apprx_tanh),
                         r=[rpre], w=[res("gy_%d" % t.loc)])
            K.barrier()
            wgl = wglu_d[l].rearrange("(k p) f -> p k f", p=128)
            sl = ring_load(lambda slot: [(slot[:, 0:2048].rearrange("p (k f) -> p k f", k=4), wgl[:, :, :])])
            sg3 = RING[sl][:, 0:2048].rearrange("p (k f) -> p k f", k=4)
            SG = [WORK[:, i * 512:(i + 1) * 512] for i in range(4)]
            for t in group:
                n = t.n
                bs = []
                for oc in range(4):
                    b = bank(("g1", "g3")[oc % 2])
                    bs.append(b)
                    for k in range(4):
                        K.op("pe", lambda: nc.tensor.matmul(PSB[b][:, 0:n], lhsT=sg3[:, k, oc * 128:(oc + 1) * 128], rhs=gyv(k, t.loc, n),
                                                            start=(k == 0), stop=(k == 3)),
                             r=[ringres[sl], res("gy_%d" % t.loc)], w=[bankres[b]], inc=(k == 3))
                for oc in range(4):
                    b = bs[oc]
                    rsg = res("sg_%d" % oc)
                    K.op("act", lambda: nc.scalar.activation(out=SG[oc][:, 0:n], in_=PSB[b][:, 0:n], func=AF.Sigmoid, bias=ptc(l, "b_glu", oc), scale=1.0),
                         r=[bankres[b], rs_pt], w=[rsg])
                for oc in range(4):
                    rsg = res("sg_%d" % oc)
                    K.op("dve", lambda: nc.vector.tensor_tensor(out=gyv(oc, t.loc, n), in0=gyv(oc, t.loc, n), in1=SG[oc][:, 0:n], op=ALU.mult),
                         r=[rsg], w=[res("gy_%d" % t.loc)])
            wos = wout_d[l].rearrange("(k p) f -> p k f", p=128)
            for half in range(2):
                sl = ring_load(lambda slot: [(slot[:, 0:4096].rearrange("p (k f) -> p k f", k=8), wos[:, :, half * 512:(half + 1) * 512])])
                so3 = RING[sl][:, 0:4096].rearrange("p (k f) -> p k f", k=8)
                for t in group:
                    n = t.n
                    for oc in range(4):
                        d = half * 4 + oc
                        b = bank("o")
                        for k in range(KC):
                            rhs = ylv(k, t.loc, n) if k < 4 else gyv(k - 4, t.loc, n)
                            K.op("pe", lambda: nc.tensor.matmul(PSB[b][:, 0:n], lhsT=so3[:, k, oc * 128:(oc + 1) * 128], rhs=rhs,
                                                                start=(k == 0), stop=(k == KC - 1)),
                                 r=[ringres[sl], res("yl_%d" % t.loc), res("gy_%d" % t.loc)], w=[bankres[b]], inc=(k == KC - 1))
                        residual(t, s, d, b)

        ds_o = K.dsem()

        def state_out(l):
            OST = AUX[0:NSEQ, 0:2048]
            rs_ost = res("ost")
            pieces = [(0, 12, oconv_d, None), (12, 4, oh_d, None), (16, 16, osr_d, None), (32, 16, osi_d, None)]
            for (c0, ncnk, dst, _) in pieces:
                for g4 in range(0, ncnk, 4):
                    b = bank("m")
                    for i in range(4):
                        cidx = c0 + g4 + i
                        K.op("pe", lambda: nc.tensor.transpose(out=PSB[b][0:NSEQ, i * 128:(i + 1) * 128], in_=FST[:, cidx * NSEQ:(cidx + 1) * NSEQ], identity=IDF[:]),
                             r=[rs_fst, rs_const], w=[bankres[b]], inc=(i == 3))
                    if c0 == 0:
                        for i in range(4):
                            cidx = g4 + i
                            c, kk = cidx // 3, cidx % 3
                            K.op("dve", lambda: nc.vector.tensor_copy(out=OST[:, kk * 512 + c * 128: kk * 512 + (c + 1) * 128], in_=PSB[b][0:NSEQ, i * 128:(i + 1) * 128]),
                                 r=[bankres[b]], w=[rs_ost, rs_aux[0], rs_aux[1]])
                    else:
                        K.op("dve", lambda: nc.vector.tensor_copy(out=OST[:, g4 * 128:(g4 + 4) * 128], in_=PSB[b][0:NSEQ, 0:512]),
                             r=[bankres[b]], w=[rs_ost, rs_aux[0], rs_aux[1]])
                K.dma("sp", ds_o, [(dst[l], OST[:, 0:ncnk * 128])], r=[rs_ost, rs_aux[0], rs_aux[1]])

        def final_out():
            K.barrier()
            ds_y = [K.dsem() for _ in range(2)]
            XN = [WORK[:, i * 1024:(i + 1) * 1024] for i in range(2)]
            YST = [WORK[:, 2048 + i * 1024: 2048 + (i + 1) * 1024] for i in range(2)]
            XSQfs = [wbf(4352, 8 * 128), wbf(4352 + 512, 8 * 128)]
            RSs = [WORK[:, 5400:5528], WORK[:, 5528:5656]]; SQs = [WORK[:, 5656:5784], WORK[:, 5784:5912]]
            ftiles = []
            for (dst, ntok, colbase) in ((yp_d, TP, 0), (ys_d, NSAMP, TP)):
                for t0 in range(0, ntok, 128):
                    ftiles.append((dst, t0, min(128, ntok - t0), colbase + t0))

            def fres(i):
                pi_ = i % 2
                return (res("f_xsq%d" % pi_), res("f_rs%d" % pi_), res("f_xn%d" % pi_), res("f_yst%d" % pi_))

            def stage_a(i):
                dst, t0, n, col = ftiles[i]
                pi_ = i % 2
                tl = tiles_all[min(col // 512, 4)]
                rxs, rrs, rxn, ryst = fres(i)
                XSQf = XSQfs[pi_]; RS = RSs[pi_]; SQ = SQs[pi_]
                for c in range(KC):
                    K.op("act", lambda: nc.scalar.activation(out=XSQf[:, c * 128: c * 128 + n], in_=xv(c, col, n), func=AF.Square),
                         r=[xres(tl, c)], w=[rxs])
                b = bank("m")
                for c in range(KC):
                    K.op("pe", lambda: nc.tensor.matmul(PSB[b][:, 0:n], lhsT=ONESB[:], rhs=XSQf[:, c * 128: c * 128 + n], start=(c == 0), stop=(c == KC - 1)),
                         r=[rxs, rs_const], w=[bankres[b]], inc=(c == KC - 1))
                K.op("act", lambda: nc.scalar.activation(out=SQ[:, 0:n], in_=PSB[b][:, 0:n], func=AF.Sqrt, bias=EPSC, scale=1.0 / D),
                     r=[bankres[b], rs_const], w=[rrs])

            def stage_b(i):
                dst, t0, n, col = ftiles[i]
                pi_ = i % 2
                tl = tiles_all[min(col // 512, 4)]
                rxs, rrs, rxn, ryst = fres(i)
                RS = RSs[pi_]; SQ = SQs[pi_]
                K.op("dve", lambda: nc.vector.reciprocal(out=RS[:, 0:n], in_=SQ[:, 0:n]), r=[rrs], w=[rrs])
                for c in range(KC):
                    K.op("dve", lambda: nc.vector.scalar_tensor_tensor(out=XN[pi_][:, c * 128: c * 128 + n], in0=xv(c, col, n),
                                                                       scalar=PT[:, PR_FINAL + c: PR_FINAL + c + 1], in1=RS[:, 0:n], op0=ALU.mult, op1=ALU.mult),
                         r=[xres(tl, c), rrs, rs_pt], w=[rxn])
                for half in range(2):
                    b = bank("o")
                    for cc in range(4):
                        c = half * 4 + cc
                        K.op("pe", lambda: nc.tensor.transpose(out=PSB[b][0:n, cc * 128:(cc + 1) * 128], in_=XN[pi_][:, c * 128: c * 128 + n], identity=IDF[:]),
                             r=[rxn, rs_const], w=[bankres[b]], inc=(cc == 3))
                    if half == 0:
                        K.op("dve", lambda: nc.vector.tensor_copy(out=YST[pi_][0:n, 0:512], in_=PSB[b][0:n, 0:512]), r=[bankres[b]], w=[ryst])
                    else:
                        K.op("act", lambda: nc.scalar.activation(out=YST[pi_][0:n, 512:1024], in_=PSB[b][0:n, 0:512], func=AF.Identity), r=[bankres[b]], w=[ryst])
                K.dma("sp", ds_y[pi_], [(dst[t0:t0 + n, :], YST[pi_][0:n, :])], r=[ryst])

            stage_a(0)
            for i in range(len(ftiles)):
                if i + 1 < len(ftiles):
                    stage_a(i + 1)
                stage_b(i)

        def main_prog():
            stage = [0]

            def stop():
                stage[0] += 1
                return STOP_AT is not None and stage[0] > STOP_AT
            if stop():
                return
            ada_enqueue(0, 0, 10)
            ada_flush()
            for l in range(DEPTH):
                if stop():
                    return
                ffn(l, 0, 0, "norm_ffn1", groups[0], True, groups[1])
                ffn(l, 0, 0, "norm_ffn1", groups[1], False, None)
                if stop():
                    return
                mixer_setup(l)
                if stop():
                    return
                for gi, g in enumerate(groups):
                    if gi == 0:
                        ada_enqueue(l, 10, 18)
                    if l + 1 < DEPTH:
                        if gi == 0:
                            ada_enqueue(l + 1, 0, 6)
                        else:
                            ada_enqueue(l + 1, 6, 10)
                    mixer_group(l, gi, g)
                    if stop():
                        return
                ada_flush()
                state_out(l)
                if stop():
                    return
                ffn(l, 1, 2, "norm_ffn2", groups[0], True, groups[1])
                ffn(l, 1, 2, "norm_ffn2", groups[1], False, None)
                if stop():
                    return
            final_out()
        main_prog()
        K.finish()
        rec_out = K.rec
    if want_rec:
        return rec_out
    return nc


_NC_CACHE = {}


def _host_layouts(inp):
    L = DEPTH
    prow = np.zeros((PR_ROWS, 128), np.float32)
    for l in range(L):
        for name, k in PR_NAMES:
            r0 = PR_LAYER * l + PR_OFF[name]
            prow[r0:r0 + k] = np.asarray(inp[name][l], np.float32).reshape(k, 128)
    prow[PR_FINAL:PR_FINAL + 8] = np.asarray(inp["norm_final"], np.float32).reshape(8, 128)
    ld = np.asarray(inp["s5_log_dt"], np.float32)
    dtx = np.repeat(ld.reshape(L, 16, 2).transpose(0, 2, 1), 64, axis=1)
    dtx = np.ascontiguousarray(dtx)
    wg = np.zeros((L, 128, 8, 128), np.float32)
    for gi, nm in enumerate(("w_rg", "w_ig")):
        w = np.asarray(inp[nm], np.float32)
        for c in range(4):
            wg[:, 0:64, gi * 4 + c, 0:64] = w[:, 2 * c]
            wg[:, 64:128, gi * 4 + c, 64:128] = w[:, 2 * c + 1]
    wg = wg.reshape(L, 128, 8 * 128)
    bnat = np.zeros((L, 2, 128, 16, 128), np.float32)
    for comp, nm in enumerate(("s5_b_re", "s5_b_im")):
        Bm = np.asarray(inp[nm], np.float32)
        for o in range(16):
            for gl in range(2):
                g = 2 * o + gl
                cb = (g % 8) * 16
                bnat[:, comp, gl * 64:(gl + 1) * 64, o, cb:cb + 16] = Bm[:, g]
    bnat = bnat.reshape(L, 2, 128, 16 * 128)
    cpad = np.zeros((L, 128, 2, 16, 128), np.float32)
    for comp, nm in enumerate(("s5_c_re", "s5_c_im")):
        Cm = np.asarray(inp[nm], np.float32)
        for o in range(16):
            for gl in range(2):
                g = 2 * o + gl
                cb = (g % 8) * 16
                cpad[:, gl * 64:(gl + 1) * 64, comp, o, cb:cb + 16] = Cm[:, g].transpose(0, 2, 1)
    cpad = cpad.reshape(L, 128, 32 * 128)
    return prow, dtx, wg, bnat, cpad


def kernel(**inp):
    if "nc" not in _NC_CACHE:
        _NC_CACHE["nc"] = build_program()
    nc = _NC_CACHE["nc"]
    f = lambda a: np.ascontiguousarray(np.asarray(a, np.float32))
    prow, dtx, wg, bnat, cpad = _host_layouts(inp)
    shared = {
        "prow": prow, "dtx": dtx, "wgates": wg, "bnat": bnat, "cpad": cpad,
        "w_ada": f(inp["w_ada"]),
        "w1_ffn1": f(inp["w1_ffn1"]), "w3_ffn1": f(inp["w3_ffn1"]), "w2_ffn1": f(inp["w2_ffn1"]),
        "w1_ffn2": f(inp["w1_ffn2"]), "w3_ffn2": f(inp["w3_ffn2"]), "w2_ffn2": f(inp["w2_ffn2"]),
        "w_in": f(inp["w_in"]), "w_glu": f(inp["w_glu"]), "w_out": f(inp["w_out"]),
    }
    xp = f(inp["x_prompt"]); xs = f(inp["x_sample"]); cp = f(inp["c_prompt"]); cs = f(inp["c_sample"])
    stc = f(inp["state_lru_conv"]); sth = f(inp["state_lru_h"]); sr = f(inp["state_s5_re"]); si = f(inp["state_s5_im"])
    in_maps = []
    for i in range(NCORES):
        s0, s1 = NSQ * i, NSQ * (i + 1)
        m = dict(shared)
        m["xp"] = xp[i]
        m["xs"] = np.ascontiguousarray(xs[s0:s1].reshape(NSAMP, D))
        m["c17"] = np.ascontiguousarray(np.concatenate([cp[i:i + 1], cs[s0:s1]], axis=0))
        m["st_conv"] = np.ascontiguousarray(stc[:, s0:s1].reshape(DEPTH, NSQ, 3 * 512))
        m["st_h"] = np.ascontiguousarray(sth[:, s0:s1])
        m["st_sr"] = np.ascontiguousarray(sr[:, s0:s1].reshape(DEPTH, NSQ, 2048))
        m["st_si"] = np.ascontiguousarray(si[:, s0:s1].reshape(DEPTH, NSQ, 2048))
        in_maps.append(m)
    res = run_bass_kernel_spmd(nc, in_maps, core_ids=list(range(NCORES)))
    R = res.results
    B = NCORES
    y_prompt = np.stack([R[i]["y_p"] for i in range(B)], axis=0).astype(np.float32)
    y_sample = np.concatenate([R[i]["y_s"].reshape(NSQ, TS, D) for i in range(B)], axis=0).astype(np.float32)

    def gather(name, tail):
        p = np.stack([R[i][name][:, 0] for i in range(B)], axis=1)
        s = np.concatenate([R[i][name][:, 1:] for i in range(B)], axis=1)
        return (p.reshape((DEPTH, B) + tail).astype(np.float32), s.reshape((DEPTH, NSQ * B) + tail).astype(np.float32))
    p_conv, s_conv = gather("o_conv", (3, 512))
    p_h, s_h = gather("o_h", (512,))
    p_sr, s_sr = gather("o_sr", (32, 64))
    p_si, s_si = gather("o_si", (32, 64))
    return (y_prompt, y_sample, p_conv, p_h, p_sr, p_si, s_conv, s_h, s_sr, s_si)
```

```python
import numpy as np
from contextlib import ExitStack
import concourse.bass as bass
import concourse.mybir as mybir
from concourse.bass_utils import run_bass_kernel_spmd

F32 = mybir.dt.float32
BF16 = mybir.dt.bfloat16
I32 = mybir.dt.int32
AF = mybir.ActivationFunctionType
ALU = mybir.AluOpType

NCORES = 8
D = 1024
KC = 8
DFF = 2816
FC = 22
TP = 2048
NSQ = 16
TS = 4
NSAMP = NSQ * TS
NTOK = TP + NSAMP
NSEQ = 17
DEPTH = 2
NT5 = 128
PI = float(np.pi)
TWO_PI = float(2 * np.pi)
EPS = 1e-6

PR_NAMES = [("norm_ffn1", 8), ("norm_mix", 8), ("norm_ffn2", 8), ("conv_w", 16), ("conv_b", 4),
            ("b_rg", 4), ("b_ig", 4), ("lru_lambda", 4), ("s5_d", 4), ("b_glu", 4),
            ("s5_a_re", 16), ("s5_a_im", 16), ("b_ada", 72)]
PR_OFF = {}
_o = 0
for _n, _k in PR_NAMES:
    PR_OFF[_n] = _o
    _o += _k
PR_LAYER = _o
PR_FINAL = PR_LAYER * DEPTH
PR_ROWS = 384

SAME_ENGINE_SYNC = ('act', 'pool', 'dve')
STOP_AT = None
SETUP_PARTS = 4
import os as _os
DBG_XT = int(_os.environ.get('DBG_XT', '999'))
DBG_XV = _os.environ.get('DBG_XV', '')


class Res:
    __slots__ = ("w", "rs", "name")

    def __init__(self, name=""):
        self.w = None
        self.rs = {}
        self.name = name


class DSem:
    def __init__(self, handle, key):
        self.h = handle
        self.key = key
        self.v = 0


class Sched:
    def __init__(self, nc, es, waited=None):
        self.nc = nc
        self.es = es
        self.waited = waited
        self.rec = {k: set() for k in ("pe", "dve", "act", "pool", "sp")}
        self.last_inc = {k: 0 for k in ("pe", "dve", "act", "pool", "sp")}
        self.eng = dict(pe=nc.tensor, dve=nc.vector, act=nc.scalar, pool=nc.gpsimd, sp=nc.sync)
        self.sem = {k: es.enter_context(nc.semaphore("cs_" + k)) for k in self.eng}
        self.cnt = {k: 0 for k in self.eng}
        self.known = {k: {} for k in self.eng}
        self.ndsem = 0
        self.dsems = []

    def dsem(self):
        h = self.es.enter_context(self.nc.semaphore("ds%d" % self.ndsem))
        d = DSem(h, "d%d" % self.ndsem)
        self.ndsem += 1
        self.dsems.append(d)
        return d

    @staticmethod
    def _flat(xs):
        out = []
        for x in xs:
            if isinstance(x, (list, tuple)):
                out.extend(Sched._flat(x))
            else:
                out.append(x)
        return out

    def barrier(self, engs=("pe", "dve", "act", "sp")):
        for e in engs:
            for f in ("pe", "dve", "act"):
                if self.cnt[f] == 0 or (f == e and e in ("pe", "sp")):
                    continue
                if self.known[e].get(f, 0) < self.cnt[f]:
                    self.eng[e].wait_ge(self.sem[f], self.cnt[f])
                    self.known[e][f] = self.cnt[f]
                    self.rec[f].add(self.cnt[f])

    def _waits(self, e, reads, writes):
        reads = self._flat(reads); writes = self._flat(writes)
        deps = {}
        raw_same = [0]

        def add(tok, raw=False):
            key, h, v = tok
            if key == e:
                if raw and v > raw_same[0]:
                    raw_same[0] = v
                return
            if key not in deps or deps[key][1] < v:
                deps[key] = (h, v)
        for r in reads:
            if r.w is not None:
                add(r.w, True)
        for w in writes:
            if w.w is not None:
                add(w.w, not w.name.startswith("bank"))
            for t in w.rs.values():
                add(t, not w.name.startswith("bank"))
        if raw_same[0] > 0:
            deps[e] = (self.sem[e], raw_same[0])
        for key, (h, v) in deps.items():
            if key == e:
                if e == "pe" or e not in SAME_ENGINE_SYNC:
                    continue
                if v > self.cnt[e]:
                    continue
            if self.known[e].get(key, 0) < v:
                self.eng[e].wait_ge(h, v)
                self.known[e][key] = v
                if key in self.rec:
                    self.rec[key].add(v)

    def _record(self, tok, reads, writes):
        reads = self._flat(reads); writes = self._flat(writes)
        key = tok[0]
        for r in reads:
            if key not in r.rs or r.rs[key][2] < tok[2]:
                r.rs[key] = tok
        for w in writes:
            w.w = tok
            w.rs = {}

    def op(self, e, fn, r=(), w=(), inc=True):
        r = self._flat(r); w = self._flat(w)
        w = w + [x for x in r if x.name.startswith("bank")]
        r = [x for x in r if not x.name.startswith("bank")]
        self._waits(e, r, w)
        inst = fn()
        self.cnt[e] += 1
        k = self.cnt[e]
        if self.waited is None or k in self.waited[e]:
            inst.then_inc(self.sem[e], k - self.last_inc[e])
            self.last_inc[e] = k
        tok = (e, self.sem[e], k)
        self._record(tok, r, w)
        return inst

    def dma(self, q, ds, pairs, r=(), w=()):
        self._waits(q, r, w)
        for (o, i) in pairs:
            self.eng[q].dma_start(out=o, in_=i).then_inc(ds.h, 16)
            ds.v += 16
        tok = (ds.key, ds.h, ds.v)
        self._record(tok, r, w)

    def finish(self):
        for d in self.dsems:
            if d.v > 0 and self.known["sp"].get(d.key, 0) < d.v:
                self.nc.sync.wait_ge(d.h, d.v)
        for e in ("pe", "dve", "act", "pool"):
            if self.cnt[e] > 0:
                self.nc.sync.wait_ge(self.sem[e], self.cnt[e])
                self.rec[e].add(self.cnt[e])


class Tile:
    def __init__(self, col0, n, kind, loc):
        self.col0 = col0
        self.n = n
        self.kind = kind
        self.loc = loc


def build_program(waited=None, want_rec=False):
    if waited is None and not want_rec:
        rec = build_program(None, True)
        return build_program(rec, False)
    nc = bass.Bass("TRN2", target_bir_lowering=False)

    def din(name, shape):
        return nc.dram_tensor(name, list(shape), F32, kind="ExternalInput").ap()

    def dout(name, shape):
        return nc.dram_tensor(name, list(shape), F32, kind="ExternalOutput").ap()

    xp_d = din("xp", [TP, D])
    xs_d = din("xs", [NSAMP, D])
    c17_d = din("c17", [NSEQ, D])
    stc_d = din("st_conv", [DEPTH, NSQ, 3 * 512])
    sth_d = din("st_h", [DEPTH, NSQ, 512])
    str_d = din("st_sr", [DEPTH, NSQ, 2048])
    sti_d = din("st_si", [DEPTH, NSQ, 2048])
    prow_d = din("prow", [PR_ROWS, 128])
    dtx_d = din("dtx", [DEPTH, 128, 16])
    wada_d = din("w_ada", [DEPTH, D, 9 * D])
    w1_d = [din("w1_ffn1", [DEPTH, D, DFF]), din("w1_ffn2", [DEPTH, D, DFF])]
    w3_d = [din("w3_ffn1", [DEPTH, D, DFF]), din("w3_ffn2", [DEPTH, D, DFF])]
    w2_d = [din("w2_ffn1", [DEPTH, DFF, D]), din("w2_ffn2", [DEPTH, DFF, D])]
    win_d = din("w_in", [DEPTH, D, 1536])
    wglu_d = din("w_glu", [DEPTH, 512, 512])
    wout_d = din("w_out", [DEPTH, D, D])
    wg_d = din("wgates", [DEPTH, 128, 8 * 128])
    bnat_d = din("bnat", [DEPTH, 2, 128, 16 * 128])
    cpad_d = din("cpad", [DEPTH, 128, 32 * 128])

    yp_d = dout("y_p", [TP, D])
    ys_d = dout("y_s", [NSAMP, D])
    oconv_d = dout("o_conv", [DEPTH, NSEQ, 3 * 512])
    oh_d = dout("o_h", [DEPTH, NSEQ, 512])
    osr_d = dout("o_sr", [DEPTH, NSEQ, 2048])
    osi_d = dout("o_si", [DEPTH, NSEQ, 2048])

    with ExitStack() as es:
        K = Sched(nc, es, waited)

        def sb(name, shape, dt=F32):
            return es.enter_context(nc.sbuf_tensor(name, list(shape), dt))

        XT = sb("XT", [128, KC * NTOK])
        RING = [sb("RING%d" % i, [128, 4096], BF16) for i in range(3)]
        WORK = sb("WORK", [128, 16320])
        AUX = sb("AUX", [128, 4128])
        WBU = sb("WBU", [128, 32 * 128], BF16)
        WC = sb("WC", [128, 32 * 128], BF16)
        WG = sb("WG", [128, 8 * 128], BF16)
        ABG = sb("ABG", [128, 9 * KC * NSEQ])
        PT = sb("PT", [128, PR_ROWS])
        IDF = sb("IDF", [128, 128])
        IDB = sb("IDB", [128, 128], BF16)
        ONESB = sb("ONESB", [128, 128], BF16)
        SILUC = sb("SILUC", [128, KC * NSEQ], BF16)
        FST = sb("FST", [128, 48 * NSEQ])
        SMALL = sb("SMALL", [128, 1700])
        TAU = sb("TAU", [128, NT5 + 1])
        PSB = [es.enter_context(nc.psum_tensor("PSB%d" % i, [128, 512], F32)) for i in range(8)]

        R = {}

        def res(name):
            if name not in R:
                R[name] = Res(name)
            return R[name]

        bankres = [res("bank%d" % i) for i in range(8)]
        ringres = [res("ring%d" % i) for i in range(3)]
        ringsem = [K.dsem() for _ in range(3)]
        ring_i = [0]
        ada_loaded = []
        role_banks = {"g1": [0, 1], "g3": [2, 3], "o": [4, 5], "m": [6, 7]}
        role_i = {k: 0 for k in role_banks}

        def bank(role):
            b = role_banks[role][role_i[role] % 2]
            role_i[role] += 1
            return b

        def ring_load(pairs_fn, ada=False):
            s = ring_i[0] % 3
            ring_i[0] += 1
            K.dma("pool", ringsem[s], pairs_fn(RING[s]), w=[ringres[s]])
            return s

        sm_off = [0]

        def small(n):
            o = sm_off[0]
            sm_off[0] += n
            assert sm_off[0] <= 1700
            return SMALL[:, o:o + n]

        A_RE = small(16); A_IM = small(16); DTX = small(16); DT = small(16); ARC = small(16)
        RHO = small(16); TH = small(16); TMPA = small(16); TMPB = small(16); TMPC = small(16); TMPD = small(16)
        F_R = small(16); F_I = small(16); ABR = small(16); ABI = small(16)
        SC1 = small(4); SC2 = small(4); SPT = small(4); SCH = small(4); HBG = small(8)
        HC = small(4)
        CR = small(16); CI = small(16)
        SLR = small(16); SLI = small(16)
        H0T = small(64)
        S0R = small(256); S0I = small(256)
        S0TR = small(256); S0TI = small(256)
        EPSC = small(1)
        HIST = small(12); CRPR = small(16); CRPI = small(16); ONEC = small(1); PIC = small(1)
        rs_small = res("small_s5par")

        def xv(c, col0, n):
            return XT[:, c * NTOK + col0: c * NTOK + col0 + n]

        def wbf(off_words, nelem):
            return WORK[:, off_words: off_words + (nelem + 1) // 2].bitcast(BF16)

        PROW = AUX[:, 1024:1024 + 384]
        C17 = AUX[0:NSEQ, 0:1024]
        H_B = wbf(0, 8 * 1088)
        A_B = wbf(4352, 22 * 1088)

        def hv(c, loc, n):
            return H_B[:, c * 1088 + loc: c * 1088 + loc + n]

        def av(f, loc, n):
            return A_B[:, f * 1088 + loc: f * 1088 + loc + n]

        def norm_tmps(tb):
            if tb < 0:
                return (AUX[:, 0:2048].bitcast(BF16), AUX[:, 2048:2560], AUX[:, 2560:3072],
                        [AUX[:, 3072:3584], AUX[:, 3584:4096]],
                        (rs_auxh[0:4], [rs_auxh[4], rs_auxh[5]], [rs_auxh[6], rs_auxh[7]]))
            return (wbf(tb, 8 * 512), WORK[:, tb + 2048: tb + 2048 + 512], WORK[:, tb + 2560: tb + 2560 + 512],
                    [WORK[:, tb + 3072 + i * 512: tb + 3072 + (i + 1) * 512] for i in range(2)],
                    (res("xsq"), res("rstd"), [res("nt0_0"), res("nt0_1")]))
        S32 = [AUX[:, 0:512], AUX[:, 1024:1536]]
        T64 = [AUX[:, 2048:2112], AUX[:, 3072:3136]]

        XBW = 1091
        XB = WORK[:, 4352: 4352 + 4 * XBW]
        GY_B = wbf(8716, 4 * 1088)
        U_B = wbf(10892, 4 * 1088)
        YL_B = wbf(13068, 4 * 1088)
        XBS = WORK[:, 15244: 15244 + 448]
        TAILX = WORK[:, 15244 + 448: 16320]

        def xbv(c, loc, n):
            return XB[:, c * XBW + loc: c * XBW + loc + n]

        def gyv(c, loc, n):
            return GY_B[:, c * 1088 + loc: c * 1088 + loc + n]

        def uv(c, loc, n):
            return U_B[:, c * 1088 + loc: c * 1088 + loc + n]

        def ylv(c, loc, n):
            return YL_B[:, c * 1088 + loc: c * 1088 + loc + n]

        COS = AUX[:, 0:16 * 129]
        SIN = AUX[:, 16 * 129: 32 * 129]

        def cosv(o0, no, t0, nt):
            return COS.rearrange("p (o t) -> p o t", o=16)[:, o0:o0 + no, t0:t0 + nt]

        def sinv(o0, no, t0, nt):
            return SIN.rearrange("p (o t) -> p o t", o=16)[:, o0:o0 + no, t0:t0 + nt]

        rs_auxh = [res("auxh%d" % i) for i in range(9)]
        rs_aux = [[rs_auxh[2 * i], rs_auxh[2 * i + 1]] for i in range(4)] + [[rs_auxh[8]]]
        rs_tab = [res("tables")] + rs_auxh

        def ptc(l, name, c, n=1):
            base = PR_LAYER * l + PR_OFF[name] + c
            return PT[:, base: base + n]

        rs_pt = res("pt")

        def abg(s, which, c, s0, ns):
            base = ((s * 3 + which) * KC + c) * NSEQ
            return ABG[:, base + s0: base + s0 + ns]

        rs_abg = res("abg")
        rs_modt = res("modt")

        tiles_all = [Tile(0, 512, "p", 0), Tile(512, 512, "p", 512),
                     Tile(1024, 512, "p", 0), Tile(1536, 512, "p", 512), Tile(2048, 64, "s", 1024)]
        groups = [tiles_all[0:2], tiles_all[2:5]]

        def xres(t, c):
            return res("x_%d_%d" % (t.col0, c))

        def bc_seq(ap2, n):
            return ap2.unsqueeze(2).to_broadcast([128, NSQ, TS])

        def v3(ap, a, b):
            return ap.rearrange("p (a b) -> p a b", a=a, b=b)

        rs_const = res("const")
        K.op("pool", lambda: nc.gpsimd.memset(IDF[:], 0.0), w=[rs_const])
        K.op("pool", lambda: nc.gpsimd.affine_select(out=IDF[:], in_=IDF[:], compare_op=ALU.not_equal, fill=1.0,
                                                     base=0, pattern=[[-1, 128]], channel_multiplier=1), r=[rs_const], w=[rs_const])
        K.op("pool", lambda: nc.gpsimd.memset(ONESB[:], 1.0), w=[rs_const])
        K.op("pool", lambda: nc.gpsimd.iota(TAU[:], pattern=[[1, NT5 + 1]], base=0, channel_multiplier=0,
                                            allow_small_or_imprecise_dtypes=True), w=[rs_const])
        K.op("dve", lambda: nc.vector.tensor_copy(out=IDB[:], in_=IDF[:]), r=[rs_const], w=[rs_const])
        K.op("dve", lambda: nc.vector.memset(EPSC, EPS), w=[rs_const])
        K.op("dve", lambda: nc.vector.memset(ONEC, 1.0), w=[rs_const])
        K.op("dve", lambda: nc.vector.memset(PIC, PI), w=[rs_const])

        ds_misc = K.dsem()
        rs_prow = res("prow")
        if SETUP_PARTS >= 2: K.dma("sp", ds_misc, [(PROW[:, i * 128:(i + 1) * 128], prow_d[i * 128:(i + 1) * 128, :]) for i in range(3)],
              w=[rs_prow, rs_aux[1]])
        for i in range(3 if SETUP_PARTS >= 2 else 0):
            b = bank("m")
            K.op("pe", lambda: nc.tensor.transpose(out=PSB[b][:, 0:128], in_=PROW[:, i * 128:(i + 1) * 128], identity=IDF[:]),
                 r=[rs_prow, rs_aux[1], rs_const], w=[bankres[b]])
            K.op("dve", lambda: nc.vector.tensor_copy(out=PT[:, i * 128:(i + 1) * 128], in_=PSB[b][:, 0:128]),
                 r=[bankres[b]], w=[rs_pt])

        ds_c = K.dsem()
        rs_c17 = res("c17")
        if SETUP_PARTS >= 3: K.dma("sp", ds_c, [(C17, c17_d[:, :])], w=[rs_c17, rs_aux[0]])
        rs_siluc = res("siluc")
        for c in range(KC if SETUP_PARTS >= 3 else 0):
            b = bank("m")
            K.op("pe", lambda: nc.tensor.transpose(out=PSB[b][:, 0:NSEQ], in_=C17[:, c * 128:(c + 1) * 128],
                                                   identity=IDF[0:NSEQ, 0:NSEQ]),
                 r=[rs_c17, rs_aux[0], rs_const], w=[bankres[b]])
            K.op("act", lambda: nc.scalar.activation(out=SILUC[:, c * NSEQ:(c + 1) * NSEQ], in_=PSB[b][:, 0:NSEQ], func=AF.Silu),
                 r=[bankres[b]], w=[rs_siluc])

        ds_x = [K.dsem() for _ in range(4)]
        n_xt = 0
        for (src, ntok, colbase) in (((xp_d, TP, 0), (xs_d, NSAMP, TP)) if SETUP_PARTS >= 4 else ()):
            for t0 in range(0, ntok, 128):
                if n_xt >= DBG_XT:
                    break
                n = min(128, ntok - t0)
                si = n_xt % 4
                n_xt += 1
                stg = AUX[:, si * 1024:(si + 1) * 1024]
                K.dma("sp", ds_x[si], [(stg[0:n, :], src[t0:t0 + n, :])], w=[rs_aux[si]])
                for half in range(2):
                    b = bank("m")
                    for cc in range(4):
                        c = half * 4 + cc
                        K.op("pe", lambda: nc.tensor.transpose(out=PSB[b][:, cc * 128: cc * 128 + n],
                                                               in_=stg[0:n, c * 128:(c + 1) * 128], identity=IDF[0:n, 0:n]),
                             r=[rs_aux[si], rs_const], w=[bankres[b]], inc=(cc == 3))
                    tl = tiles_all[min((colbase + t0) // 512, 4)]
                    for cc in range(4):
                        c = half * 4 + cc
                        eng = "dve" if (cc % 2 == 0 or DBG_XV == "dve") else "act"
                        if eng == "dve":
                            K.op("dve", lambda: nc.vector.tensor_copy(out=xv(c, colbase + t0, n), in_=PSB[b][:, cc * 128: cc * 128 + n]),
                                 r=[bankres[b]], w=[xres(tl, c)])
                        else:
                            K.op("act", lambda: nc.scalar.activation(out=xv(c, colbase + t0, n), in_=PSB[b][:, cc * 128: cc * 128 + n], func=AF.Identity),
                                 r=[bankres[b]], w=[xres(tl, c)])

        def norm_parts(t, l, s, gain_name, hdst, hres, tb=4352):
            n = t.n
            XSQ, RSTD, SQT, NT0, (rs_xsq, rs_rstd, rs_nt0) = norm_tmps(tb)
            st = {}

            def p1():
                for c in range(KC):
                    K.op("act", lambda: nc.scalar.activation(out=XSQ[:, c * 512: c * 512 + n], in_=xv(c, t.col0, n), func=AF.Square),
                         r=[xres(t, c)], w=[rs_xsq])

            def p2():
                b = bank("m")
                st["b"] = b
                for c in range(KC):
                    K.op("pe", lambda: nc.tensor.matmul(PSB[b][:, 0:n], lhsT=ONESB[:], rhs=XSQ[:, c * 512: c * 512 + n],
                                                        start=(c == 0), stop=(c == KC - 1)),
                         r=[rs_xsq, rs_const], w=[bankres[b]], inc=(c == KC - 1))
                K.op("act", lambda: nc.scalar.activation(out=SQT[:, 0:n], in_=PSB[b][:, 0:n], func=AF.Sqrt,
                                                         bias=EPSC, scale=1.0 / D),
                     r=[bankres[b], rs_const], w=[rs_rstd])

            def p3():
                K.op("dve", lambda: nc.vector.reciprocal(out=RSTD[:, 0:n], in_=SQT[:, 0:n]), r=[rs_rstd], w=[rs_rstd])
                for c in range(KC):
                    tmp = NT0[c % 2]
                    rtmp = rs_nt0[c % 2]
                    K.op("dve", lambda: nc.vector.tensor_tensor(out=tmp[:, 0:n], in0=xv(c, t.col0, n), in1=RSTD[:, 0:n], op=ALU.mult),
                         r=[xres(t, c), rs_rstd], w=[rtmp])
                    if t.kind == "p":
                        K.op("act", lambda: nc.scalar.activation(out=hdst(c), in_=tmp[:, 0:n], func=AF.Identity,
                                                                 scale=abg(s, 0, c, 0, 1), bias=abg(s, 1, c, 0, 1)),
                             r=[rtmp, rs_abg], w=[hres])
                    else:
                        K.op("dve", lambda: nc.vector.tensor_tensor(out=v3(tmp[:, 0:n], NSQ, TS), in0=v3(tmp[:, 0:n], NSQ, TS),
                                                                    in1=bc_seq(abg(s, 0, c, 1, NSQ), TS), op=ALU.mult),
                             r=[rs_abg, rtmp], w=[rtmp])
                        K.op("dve", lambda: nc.vector.tensor_tensor(out=v3(hdst(c), NSQ, TS), in0=v3(tmp[:, 0:n], NSQ, TS),
                                                                    in1=bc_seq(abg(s, 1, c, 1, NSQ), TS), op=ALU.add),
                             r=[rtmp, rs_abg], w=[hres])
            return [p1, p2, p3]

        def norm_mod(t, l, s, gain_name, hdst, hres, tb=4352):
            for pfn in norm_parts(t, l, s, gain_name, hdst, hres, tb):
                pfn()

        def norm_group(tiles, l, s, gain_name, tb):
            pp = [norm_parts(t, l, s, gain_name, lambda c, t=t: hv(c, t.loc, t.n), hres_of(t), tb) for t in tiles]
            nT = len(pp)
            pp[0][0](); pp[0][1]()
            for i in range(1, nT):
                pp[i][0]()
                pp[i - 1][2]()
                pp[i][1]()
            pp[nT - 1][2]()


        def residual(t, s, d, b):
            n = t.n
            if t.kind == "p":
                K.op("dve", lambda: nc.vector.scalar_tensor_tensor(out=xv(d, t.col0, n), in0=PSB[b][:, 0:n], scalar=abg(s, 2, d, 0, 1),
                                                                   in1=xv(d, t.col0, n), op0=ALU.mult, op1=ALU.add),
                     r=[bankres[b], rs_abg], w=[xres(t, d)])
            else:
                tmp = T64[d % 2]
                rtmp = rs_auxh[4 + 2 * (d % 2)]
                K.op("dve", lambda: nc.vector.tensor_tensor(out=v3(tmp[:, 0:n], NSQ, TS), in0=v3(PSB[b][:, 0:n], NSQ, TS),
                                                            in1=bc_seq(abg(s, 2, d, 1, NSQ), TS), op=ALU.mult),
                     r=[bankres[b], rs_abg], w=[rtmp])
                K.op("dve", lambda: nc.vector.tensor_tensor(out=xv(d, t.col0, n), in0=xv(d, t.col0, n), in1=tmp[:, 0:n], op=ALU.add),
                     r=[rtmp], w=[xres(t, d)])

        def hres_of(t):
            return res("h_%d" % t.loc)

        MTS = [small(4 * NSEQ), small(4 * NSEQ)]
        rs_mts = [res("ada_mt0"), res("ada_mt1")]
        ada_q = []
        ada_evac = []
        ada_n = [0]

        def ada_load(l, j):
            wsrc = wada_d[l].rearrange("(k p) f -> p k f", p=128)
            s_ = ring_load(lambda slot: [(slot[:, 0:4096].rearrange("p (k f) -> p k f", k=8), wsrc[:, :, j * 512:(j + 1) * 512])], ada=True)
            ada_loaded.append((l, j, s_))

        def ada_mm(l, j, s_):
            slot3 = RING[s_][:, 0:4096].rearrange("p (k f) -> p k f", k=8)
            b = bank("m")
            mi = ada_n[0] % 2
            ada_n[0] += 1
            for i in range(4):
                for k in range(KC):
                    K.op("pe", lambda: nc.tensor.matmul(PSB[b][:, i * 32: i * 32 + NSEQ], lhsT=slot3[:, k, i * 128:(i + 1) * 128],
                                                        rhs=SILUC[:, k * NSEQ:(k + 1) * NSEQ], start=(k == 0), stop=(k == KC - 1)),
                         r=[ringres[s_], rs_siluc], w=[bankres[b]], inc=(k == KC - 1 and i == 3))
            for i in range(4):
                oc = 4 * j + i
                K.op("act", lambda: nc.scalar.activation(out=MTS[mi][:, i * NSEQ:(i + 1) * NSEQ], in_=PSB[b][:, i * 32: i * 32 + NSEQ],
                                                         func=AF.Identity, bias=ptc(l, "b_ada", oc), scale=1.0),
                     r=[bankres[b], rs_pt], w=[rs_mts[mi]])
            ada_evac.append((l, j, mi))

        def ada_derive(l, j, mi):
            gains = ["norm_ffn1", "norm_mix", "norm_ffn2"]
            for i in range(4):
                oc = 4 * j + i
                m = oc // 8
                c = oc % 8
                sub, which = m // 3, m % 3
                src = MTS[mi][:, i * NSEQ:(i + 1) * NSEQ]
                if which == 0:
                    K.op("dve", lambda: nc.vector.tensor_copy(out=abg(sub, 1, c, 0, NSEQ), in_=src), r=[rs_mts[mi]], w=[rs_abg])
                elif which == 1:
                    K.op("dve", lambda: nc.vector.tensor_scalar(out=abg(sub, 0, c, 0, NSEQ), in0=src, scalar1=1.0,
                                                                scalar2=ptc(l, gains[sub], c), op0=ALU.add, op1=ALU.mult),
                         r=[rs_mts[mi], rs_pt], w=[rs_abg])
                else:
                    K.op("dve", lambda: nc.vector.tensor_scalar(out=abg(sub, 2, c, 0, NSEQ), in0=src, scalar1=(1.0 if sub == 1 else 0.5),
                                                                scalar2=None, op0=ALU.mult),
                         r=[rs_mts[mi]], w=[rs_abg])

        def ada_enqueue(l, j0, j1):
            for j in range(j0, j1):
                ada_q.append((l, j))

        def ada_pump(n=1):
            for _ in range(n):
                if ada_evac:
                    ada_derive(*ada_evac.pop(0))
                if ada_loaded:
                    ada_mm(*ada_loaded.pop(0))
                if ada_q:
                    ada_load(*ada_q.pop(0))

        def ada_flush():
            while ada_q or ada_loaded or ada_evac:
                ada_pump(1)

        def ffn(l, which, s, gain_name, group, do_norm=True, next_group=None):
            w1s = w1_d[which][l].rearrange("(k p) f -> p k f", p=128)
            w3s = w3_d[which][l].rearrange("(k p) f -> p k f", p=128)
            w2s = w2_d[which][l].rearrange("(k p) f -> p k f", p=128)
            if do_norm:
                K.barrier()
                norm_group(group, l, s, gain_name, 4352)
            def load_item(i):
                if i < 11:
                    return ring_load(lambda slot: [
                        (slot[:, 0:2048].rearrange("p (k f) -> p k f", k=8), w1s[:, :, i * 256:(i + 1) * 256]),
                        (slot[:, 2048:4096].rearrange("p (k f) -> p k f", k=8), w3s[:, :, i * 256:(i + 1) * 256])])
                d_ = i - 11
                return ring_load(lambda slot: [(slot[:, 0:2816].rearrange("p (k f) -> p k f", k=FC), w2s[:, :, d_ * 128:(d_ + 1) * 128])])
            slots_ = {0: load_item(0)}
            for j in range(11):
                slots_[j + 1] = load_item(j + 1)
                sl = slots_[j]
                s1 = RING[sl][:, 0:2048].rearrange("p (k f) -> p k f", k=8)
                s3 = RING[sl][:, 2048:4096].rearrange("p (k f) -> p k f", k=8)
                for t in group:
                    n = t.n
                    for f2 in range(2):
                        f = 2 * j + f2
                        b1 = bank("g1"); b3 = bank("g3")
                        for k in range(KC):
                            K.op("pe", lambda: nc.tensor.matmul(PSB[b1][:, 0:n], lhsT=s1[:, k, f2 * 128:(f2 + 1) * 128], rhs=hv(k, t.loc, n),
                                                                start=(k == 0), stop=(k == KC - 1)),
                                 r=[ringres[sl], hres_of(t)], w=[bankres[b1]], inc=(k == KC - 1))
                        for k in range(KC):
                            K.op("pe", lambda: nc.tensor.matmul(PSB[b3][:, 0:n], lhsT=s3[:, k, f2 * 128:(f2 + 1) * 128], rhs=hv(k, t.loc, n),
                                                                start=(k == 0), stop=(k == KC - 1)),
                                 r=[ringres[sl], hres_of(t)], w=[bankres[b3]], inc=(k == KC - 1))
                        si = role_i["g1"] % 2
                        stmp = S32[si]; rst = rs_auxh[2 * si]
                        K.op("act", lambda: nc.scalar.activation(out=stmp[:, 0:n], in_=PSB[b1][:, 0:n], func=AF.Silu),
                             r=[bankres[b1]], w=[rst])
                        K.op("dve", lambda: nc.vector.tensor_tensor(out=av(f, t.loc, n), in0=stmp[:, 0:n], in1=PSB[b3][:, 0:n], op=ALU.mult),
                             r=[rst, bankres[b3]], w=[res("a_%d_%d" % (f, t.loc))])
            for d in range(KC):
                if d + 1 < KC:
                    slots_[11 + d + 1] = load_item(11 + d + 1)
                if next_group is not None:
                    if d == 0:
                        hoist = []
                        for ti_, t in enumerate(next_group):
                            pp = norm_parts(t, l, s, gain_name, lambda c, t=t: hv(c, t.loc, t.n), hres_of(t), tb=-1)
                            for pi2, pfn in enumerate(pp):
                                hoist.append((2 * ti_ + pi2, pfn))
                    for (dd, pfn) in hoist:
                        if dd == d:
                            pfn()
                sl = slots_[11 + d]
                s2 = RING[sl][:, 0:2816].rearrange("p (k f) -> p k f", k=FC)
                for t in group:
                    n = t.n
                    b = bank("o")
                    for f in range(FC):
                        K.op("pe", lambda: nc.tensor.matmul(PSB[b][:, 0:n], lhsT=s2[:, f, :], rhs=av(f, t.loc, n),
                                                            start=(f == 0), stop=(f == FC - 1)),
                             r=[ringres[sl], res("a_%d_%d" % (f, t.loc))], w=[bankres[b]], inc=(f == FC - 1))
                    residual(t, s, d, b)

        def mod_2pi(x, m, kmax, rt):
            k = kmax
            while k >= 1:
                sk = float(k) * TWO_PI
                K.op("dve", lambda: nc.vector.tensor_scalar(out=m, in0=x, scalar1=sk, scalar2=None, op0=ALU.is_ge), r=[rt], w=[rt])
                K.op("dve", lambda: nc.vector.scalar_tensor_tensor(out=x, in0=m, scalar=-sk, in1=x, op0=ALU.mult, op1=ALU.add), r=[rt], w=[rt])
                k //= 2

        def sin_reduced(dst, src, n, tmp1, tmp2i, tmp3, rr, rw, add_half_pi=False):
            rt = res("sinred_tmp")
            K.op("dve", lambda: nc.vector.tensor_scalar(out=tmp1, in0=src, scalar1=(PI / 2 if add_half_pi else 0.0), scalar2=None, op0=ALU.add),
                 r=rr, w=[rt])
            mod_2pi(tmp1, tmp3, 128, rt)
            K.op("act", lambda: nc.scalar.activation(out=dst, in_=tmp1, func=AF.Sin, scale=-1.0, bias=PIC), r=[rt, rs_const], w=rw)


        def cmul(dr, di, ar, ai, br, bi, t1, t2, rr, rw, rtmp):
            K.op("dve", lambda: nc.vector.tensor_tensor(out=t1, in0=ar, in1=br, op=ALU.mult), r=rr, w=[rtmp])
            K.op("dve", lambda: nc.vector.tensor_tensor(out=t2, in0=ai, in1=bi, op=ALU.mult), r=rr, w=[rtmp])
            K.op("dve", lambda: nc.vector.tensor_tensor(out=dr, in0=t1, in1=t2, op=ALU.subtract), r=[rtmp], w=rw)
            K.op("dve", lambda: nc.vector.tensor_tensor(out=t1, in0=ar, in1=bi, op=ALU.mult), r=rr, w=[rtmp])
            K.op("dve", lambda: nc.vector.tensor_tensor(out=t2, in0=ai, in1=br, op=ALU.mult), r=rr, w=[rtmp])
            K.op("dve", lambda: nc.vector.tensor_tensor(out=di, in0=t1, in1=t2, op=ALU.add), r=[rtmp], w=rw)

        rs_wbu = res("wbu"); rs_wc = res("wc"); rs_wg = res("wg")
        ds_w = K.dsem(); ds_w2 = K.dsem(); ds_st = K.dsem(); ds_bn = K.dsem()
        rs_fst = res("fst")
        rs_xbs = res("xbs"); rs_h0t = res("h0t"); rs_s0 = res("s0")
        rs_hc = res("hc"); rs_carry = res("carry")
        rs_work_setup = res("work_setup")

        def mixer_setup(l):
            K.barrier()
            K.dma("pool", ds_w, [(WG[:], wg_d[l])], w=[rs_wg])
            K.dma("pool", ds_w2, [(WC[:], cpad_d[l])], w=[rs_wc])
            K.op("dve", lambda: nc.vector.tensor_copy(out=A_RE, in_=ptc(l, "s5_a_re", 0, 16)), r=[rs_pt], w=[rs_small])
            K.op("dve", lambda: nc.vector.tensor_copy(out=A_IM, in_=ptc(l, "s5_a_im", 0, 16)), r=[rs_pt], w=[rs_small])
            K.dma("sp", ds_misc, [(DTX, dtx_d[l])], w=[rs_small])
            K.op("act", lambda: nc.scalar.activation(out=DT, in_=DTX, func=AF.Exp), r=[rs_small], w=[rs_small])
            K.op("dve", lambda: nc.vector.tensor_scalar(out=ARC, in0=A_RE, scalar1=-1e-4, scalar2=None, op0=ALU.min), r=[rs_small], w=[rs_small])
            K.op("dve", lambda: nc.vector.tensor_tensor(out=TMPA, in0=ARC, in1=DT, op=ALU.mult), r=[rs_small], w=[rs_small])
            K.op("act", lambda: nc.scalar.activation(out=RHO, in_=TMPA, func=AF.Exp), r=[rs_small], w=[rs_small])
            K.op("dve", lambda: nc.vector.tensor_tensor(out=TH, in0=A_IM, in1=DT, op=ALU.mult), r=[rs_small], w=[rs_small])
            K.op("dve", lambda: nc.vector.tensor_scalar(out=TH, in0=TH, scalar1=8.0 * TWO_PI, scalar2=None, op0=ALU.add), r=[rs_small], w=[rs_small])
            mod_2pi(TH, TMPA, 8, rs_small)
            PHI = WORK[:, 0:16 * 129]
            T1 = WORK[:, 2064:2064 + 2064]
            T2 = WORK[:, 4128:4128 + 2064]
            T3 = WORK[:, 6192:6192 + 2064]
            K.op("dve", lambda: nc.vector.tensor_tensor(out=v3(PHI, 16, 129), in0=TH.unsqueeze(2).to_broadcast([128, 16, 129]),
                                                        in1=TAU[:, :].unsqueeze(1).to_broadcast([128, 16, 129]), op=ALU.mult),
                 r=[rs_small, rs_const], w=[rs_work_setup])
            rt_ = res("sinred_tmp")
            K.op("dve", lambda: nc.vector.tensor_copy(out=T1, in_=PHI), r=[rs_work_setup], w=[rt_])
            mod_2pi(T1, T3, 64, rt_)
            K.op("act", lambda: nc.scalar.activation(out=SIN, in_=T1, func=AF.Sin, scale=-1.0, bias=PIC), r=[rt_, rs_const], w=[rs_tab])
            K.op("dve", lambda: nc.vector.tensor_scalar(out=T2, in0=T1, scalar1=PI / 2, scalar2=None, op0=ALU.add), r=[rt_], w=[rt_])
            mod_2pi(T2, T3, 1, rt_)
            K.op("act", lambda: nc.scalar.activation(out=COS, in_=T2, func=AF.Sin, scale=-1.0, bias=PIC), r=[rt_, rs_const], w=[rs_tab])
            c1 = cosv(0, 16, 1, 1).rearrange("p o t -> p (o t)")
            s1 = sinv(0, 16, 1, 1).rearrange("p o t -> p (o t)")
            K.op("dve", lambda: nc.vector.tensor_tensor(out=ABR, in0=RHO, in1=c1, op=ALU.mult), r=[rs_small, rs_tab], w=[rs_small])
            K.op("dve", lambda: nc.vector.tensor_tensor(out=ABI, in0=RHO, in1=s1, op=ALU.mult), r=[rs_small, rs_tab], w=[rs_small])
            K.op("dve", lambda: nc.vector.tensor_tensor(out=TMPA, in0=ARC, in1=ARC, op=ALU.mult), r=[rs_small], w=[rs_small])
            K.op("dve", lambda: nc.vector.tensor_tensor(out=TMPB, in0=A_IM, in1=A_IM, op=ALU.mult), r=[rs_small], w=[rs_small])
            K.op("dve", lambda: nc.vector.tensor_tensor(out=TMPA, in0=TMPA, in1=TMPB, op=ALU.add), r=[rs_small], w=[rs_small])
            K.op("dve", lambda: nc.vector.reciprocal(out=TMPA, in_=TMPA), r=[rs_small], w=[rs_small])
            K.op("dve", lambda: nc.vector.tensor_scalar(out=TMPB, in0=ABR, scalar1=-1.0, scalar2=None, op0=ALU.add), r=[rs_small], w=[rs_small])
            K.op("dve", lambda: nc.vector.tensor_tensor(out=TMPC, in0=TMPB, in1=ARC, op=ALU.mult), r=[rs_small], w=[rs_small])
            K.op("dve", lambda: nc.vector.tensor_tensor(out=TMPD, in0=ABI, in1=A_IM, op=ALU.mult), r=[rs_small], w=[rs_small])
            K.op("dve", lambda: nc.vector.tensor_tensor(out=TMPC, in0=TMPC, in1=TMPD, op=ALU.add), r=[rs_small], w=[rs_small])
            K.op("dve", lambda: nc.vector.tensor_tensor(out=F_R, in0=TMPC, in1=TMPA, op=ALU.mult), r=[rs_small], w=[rs_small])
            K.op("dve", lambda: nc.vector.tensor_tensor(out=TMPC, in0=ABI, in1=ARC, op=ALU.mult), r=[rs_small], w=[rs_small])
            K.op("dve", lambda: nc.vector.tensor_tensor(out=TMPD, in0=TMPB, in1=A_IM, op=ALU.mult), r=[rs_small], w=[rs_small])
            K.op("dve", lambda: nc.vector.tensor_tensor(out=TMPC, in0=TMPC, in1=TMPD, op=ALU.subtract), r=[rs_small], w=[rs_small])
            K.op("dve", lambda: nc.vector.tensor_tensor(out=F_I, in0=TMPC, in1=TMPA, op=ALU.mult), r=[rs_small], w=[rs_small])
            BNR = WORK[:, 8256: 8256 + 2048]
            BNI = WORK[:, 10304: 10304 + 2048]
            U1 = WORK[:, 12352: 12352 + 2048]
            U2 = WORK[:, 14400: 14400 + 1920]
            rs_bn = res("bnat")
            K.dma("sp", ds_bn, [(BNR, bnat_d[l, 0]), (BNI, bnat_d[l, 1])], w=[rs_bn])
            BBR = wbf(0, 2048 * 1)
            BBI = wbf(1024, 2048 * 1)
            rs_bb = res("bb")
            frb = F_R.unsqueeze(2).to_broadcast([128, 16, 128])
            fib = F_I.unsqueeze(2).to_broadcast([128, 16, 128])
            for hh in range(2):
                o0 = hh * 8
                sl_ = slice(o0 * 128, (o0 + 8) * 128)
                fr_h = F_R[:, o0:o0 + 8].unsqueeze(2).to_broadcast([128, 8, 128])
                fi_h = F_I[:, o0:o0 + 8].unsqueeze(2).to_broadcast([128, 8, 128])
                u1 = v3(U1[:, 0:1024], 8, 128); u2 = v3(U2[:, 0:1024], 8, 128)
                bnr = v3(BNR[:, sl_], 8, 128); bni = v3(BNI[:, sl_], 8, 128)
                rtu = res("bb_tmp")
                K.op("dve", lambda: nc.vector.tensor_tensor(out=u1, in0=bnr, in1=fr_h, op=ALU.mult), r=[rs_bn, rs_small, rs_tab], w=[rtu])
                K.op("dve", lambda: nc.vector.tensor_tensor(out=u2, in0=bni, in1=fi_h, op=ALU.mult), r=[rs_bn, rs_small], w=[rtu])
                K.op("dve", lambda: nc.vector.tensor_tensor(out=v3(BBR[:, sl_], 8, 128), in0=u1, in1=u2, op=ALU.subtract), r=[rtu], w=[rs_bb])
                K.op("dve", lambda: nc.vector.tensor_tensor(out=u1, in0=bnr, in1=fi_h, op=ALU.mult), r=[rs_bn, rs_small], w=[rtu])
                K.op("dve", lambda: nc.vector.tensor_tensor(out=u2, in0=bni, in1=fr_h, op=ALU.mult), r=[rs_bn, rs_small], w=[rtu])
                K.op("dve", lambda: nc.vector.tensor_tensor(out=v3(BBI[:, sl_], 8, 128), in0=u1, in1=u2, op=ALU.add), r=[rtu], w=[rs_bb])
            for comp, BB in ((0, BBR), (1, BBI)):
                for o8 in range(2):
                    b = bank("m")
                    pb = PSB[b][:, :].bitcast(BF16)
                    for oi in range(8):
                        o = o8 * 8 + oi
                        K.op("pe", lambda: nc.tensor.transpose(out=pb[:, oi * 128:(oi + 1) * 128], in_=BB[:, o * 128:(o + 1) * 128], identity=IDB[:]),
                             r=[rs_bb, rs_const], w=[bankres[b]], inc=(oi == 7))
                    base = (comp * 16 + o8 * 8) * 128
                    K.op("dve", lambda: nc.vector.tensor_copy(out=WBU[:, base: base + 1024], in_=pb[:, 0:1024]), r=[bankres[b]], w=[rs_wbu])
            K.op("act", lambda: nc.scalar.activation(out=SPT, in_=ptc(l, "lru_lambda", 0, 4), func=AF.Exp, scale=-1.0), r=[rs_pt], w=[rs_small])
            K.op("dve", lambda: nc.vector.tensor_scalar(out=SPT, in0=SPT, scalar1=1.0, scalar2=None, op0=ALU.add), r=[rs_small], w=[rs_small])
            K.op("act", lambda: nc.scalar.activation(out=SPT, in_=SPT, func=AF.Ln), r=[rs_small], w=[rs_small])
            K.op("dve", lambda: nc.vector.tensor_scalar(out=SC1, in0=SPT, scalar1=-8.0, scalar2=None, op0=ALU.mult), r=[rs_small], w=[rs_small])
            K.op("dve", lambda: nc.vector.tensor_scalar(out=SC2, in0=SPT, scalar1=-16.0, scalar2=None, op0=ALU.mult), r=[rs_small], w=[rs_small])
            K.op("dve", lambda: nc.vector.tensor_scalar(out=SCH, in0=SPT, scalar1=-4.0, scalar2=None, op0=ALU.mult), r=[rs_small], w=[rs_small])
            K.op("dve", lambda: nc.vector.tensor_scalar(out=HBG[:, 0:4], in0=ptc(l, "b_rg", 0, 4), scalar1=0.5, scalar2=None, op0=ALU.mult), r=[rs_pt], w=[rs_small])
            K.op("dve", lambda: nc.vector.tensor_scalar(out=HBG[:, 4:8], in0=ptc(l, "b_ig", 0, 4), scalar1=0.5, scalar2=None, op0=ALU.mult), r=[rs_pt], w=[rs_small])
            STG = WORK[0:NSQ, 8256: 8256 + 2048]
            xbs4 = XBS.rearrange("p (c s k) -> p c s k", c=4, s=NSQ, k=7)
            K.dma("sp", ds_st, [(STG[:, 0:1536], stc_d[l])], r=[], w=[rs_bn])
            for k3 in range(3):
                b = bank("m")
                for c in range(4):
                    K.op("pe", lambda: nc.tensor.transpose(out=PSB[b][:, c * 16:(c + 1) * 16], in_=STG[:, k3 * 512 + c * 128: k3 * 512 + (c + 1) * 128],
                                                           identity=IDF[0:NSQ, 0:NSQ]),
                         r=[rs_bn, rs_const], w=[bankres[b]], inc=(c == 3))
                K.op("dve", lambda: nc.vector.tensor_copy(out=xbs4[:, :, :, k3], in_=v3(PSB[b][:, 0:64], 4, 16)), r=[bankres[b]], w=[rs_xbs])
            K.dma("sp", ds_st, [(STG[:, 0:512], sth_d[l])], w=[rs_bn])
            b = bank("m")
            for c in range(4):
                K.op("pe", lambda: nc.tensor.transpose(out=PSB[b][:, c * 16:(c + 1) * 16], in_=STG[:, c * 128:(c + 1) * 128], identity=IDF[0:NSQ, 0:NSQ]),
                     r=[rs_bn, rs_const], w=[bankres[b]], inc=(c == 3))
            K.op("dve", lambda: nc.vector.tensor_copy(out=H0T, in_=PSB[b][:, 0:64]), r=[bankres[b]], w=[rs_h0t])
            for (srcd, dst) in ((str_d, S0R), (sti_d, S0I)):
                K.dma("sp", ds_st, [(STG[:, 0:2048], srcd[l])], w=[rs_bn])
                b = bank("m")
                for o in range(16):
                    K.op("pe", lambda: nc.tensor.transpose(out=PSB[b][:, o * 16:(o + 1) * 16], in_=STG[:, o * 128:(o + 1) * 128], identity=IDF[0:NSQ, 0:NSQ]),
                         r=[rs_bn, rs_const], w=[bankres[b]], inc=(o == 15))
                K.op("dve", lambda: nc.vector.tensor_copy(out=dst, in_=PSB[b][:, 0:256]), r=[bankres[b]], w=[rs_s0])
            c1b = cosv(0, 16, 1, 1).to_broadcast([128, 16, NSQ])
            s1b = sinv(0, 16, 1, 1).to_broadcast([128, 16, NSQ])
            t1 = v3(TAILX[:, 0:256], 16, NSQ); t2 = v3(TAILX[:, 256:512], 16, NSQ)
            cmul(v3(S0TR, 16, NSQ), v3(S0TI, 16, NSQ), v3(S0R, 16, NSQ), v3(S0I, 16, NSQ), c1b, s1b, t1, t2,
                 [rs_s0, rs_tab], [rs_s0], res("tailx"))
            K.op("dve", lambda: nc.vector.memset(HC, 0.0), w=[rs_hc])
            K.op("dve", lambda: nc.vector.memset(CR, 0.0), w=[rs_carry])
            K.op("dve", lambda: nc.vector.memset(CI, 0.0), w=[rs_carry])

        def mixer_group(l, gi, group):
            s = 1
            K.barrier()
            wins = win_d[l].rearrange("(k p) f -> p k f", p=128)
            norm_group(group, l, s, "norm_mix", 8716)
            xbs4 = XBS.rearrange("p (c s k) -> p c s k", c=4, s=NSQ, k=7)
            for part in range(3):
                sl = ring_load(lambda slot: [(slot[:, 0:4096].rearrange("p (k f) -> p k f", k=8), wins[:, :, part * 512:(part + 1) * 512])])
                s3 = RING[sl][:, 0:4096].rearrange("p (k f) -> p k f", k=8)
                for t in group:
                    n = t.n
                    for oc in range(4):
                        role = ("g1", "g3", "o")[(oc + part) % 3]
                        b = bank(role)
                        for k in range(KC):
                            K.op("pe", lambda: nc.tensor.matmul(PSB[b][:, 0:n], lhsT=s3[:, k, oc * 128:(oc + 1) * 128], rhs=hv(k, t.loc, n),
                                                                start=(k == 0), stop=(k == KC - 1)),
                                 r=[ringres[sl], hres_of(t)], w=[bankres[b]], inc=(k == KC - 1))
                        if part == 0:
                            if t.kind == "p":
                                K.op("dve", lambda: nc.vector.tensor_copy(out=xbv(oc, 3 + t.loc, n), in_=PSB[b][:, 0:n]),
                                     r=[bankres[b]], w=[res("xb_%d" % t.loc)])
                            else:
                                K.op("dve", lambda: nc.vector.tensor_copy(out=xbs4[:, oc, :, 3:7], in_=v3(PSB[b][:, 0:n], NSQ, TS)),
                                     r=[bankres[b]], w=[rs_xbs])
                        elif part == 1:
                            K.op("act", lambda: nc.scalar.activation(out=gyv(oc, t.loc, n), in_=PSB[b][:, 0:n], func=AF.Gelu_apprx_tanh),
                                 r=[bankres[b]], w=[res("gy_%d" % t.loc)])
                        else:
                            K.op("act", lambda: nc.scalar.activation(out=uv(oc, t.loc, n), in_=PSB[b][:, 0:n], func=AF.Identity),
                                 r=[bankres[b]], w=[res("u_%d" % t.loc)])
            K.barrier()
            if gi == 0:
                K.op("dve", lambda: nc.vector.memset(XB.rearrange("p (c w) -> p c w", c=4)[:, :, 0:3], 0.0), w=[res("xb_hist")])
            else:
                K.op("dve", lambda: nc.vector.tensor_copy(out=XB.rearrange("p (c w) -> p c w", c=4)[:, :, 0:3], in_=v3(HIST, 4, 3)),
                     r=[res("hist_save")], w=[res("xb_hist")])
            NL = 256
            LSETS = []
            for si_ in range(4):
                base = si_ * 1024
                xcbw = TAILX[:, si_ * 128:(si_ + 1) * 128].bitcast(BF16)
                LSETS.append(([WORK[:, base + i * NL: base + (i + 1) * NL] for i in range(4)], xcbw, res("lru_set%d" % si_)))
            lunits = []
            for t in group:
                subs = [(t.loc + i * NL, NL) for i in range(t.n // NL)] if t.kind == "p" else [(t.loc, t.n)]
                for (sloc, n) in subs:
                    for c in range(4):
                        lunits.append((t, sloc, n, c))
            batches = [lunits[i:i + 2] for i in range(0, len(lunits), 2)]

            def lru_A(k):
                for bi_, (t, sloc, n, c) in enumerate(batches[k]):
                    bufs, xcbf, rlt = LSETS[2 * (k % 2) + bi_]
                    XC, RG, IG, AA = [x[:, 0:n] for x in bufs]
                    xcb = xcbf[:, 0:n]
                    if t.kind == "p":
                        srcs = [xbv(c, sloc + kk, n) for kk in range(4)]
                        rsrc = [res("xb_%d" % t.loc)]
                        if sloc == t.loc:
                            rsrc.append(res("xb_%d" % (t.loc - 512)) if t.loc > 0 else res("xb_hist"))
                        shp = lambda a: a
                    else:
                        srcs = [xbs4[:, c, :, kk:kk + 4] for kk in range(4)]
                        rsrc = [rs_xbs]
                        shp = lambda a: v3(a, NSQ, TS)
                    K.op("dve", lambda: nc.vector.tensor_scalar(out=shp(XC), in0=srcs[3], scalar1=ptc(l, "conv_w", 3 * 4 + c),
                                                                scalar2=ptc(l, "conv_b", c), op0=ALU.mult, op1=ALU.add),
                         r=rsrc + [rs_pt], w=[rlt])
                    for kk in range(3):
                        K.op("dve", lambda: nc.vector.scalar_tensor_tensor(out=shp(XC), in0=srcs[kk], scalar=ptc(l, "conv_w", kk * 4 + c),
                                                                           in1=shp(XC), op0=ALU.mult, op1=ALU.add),
                             r=rsrc + [rs_pt, rlt], w=[rlt])
                    K.op("dve", lambda: nc.vector.tensor_copy(out=xcb, in_=XC), r=[rlt], w=[rlt])
                    b1 = bank("g1"); b3 = bank("g3")
                    K.op("pe", lambda: nc.tensor.matmul(PSB[b1][:, 0:n], lhsT=WG[:, c * 128:(c + 1) * 128], rhs=xcb, start=True, stop=True),
                         r=[rlt, rs_wg], w=[bankres[b1]])
                    K.op("pe", lambda: nc.tensor.matmul(PSB[b3][:, 0:n], lhsT=WG[:, (4 + c) * 128:(5 + c) * 128], rhs=xcb, start=True, stop=True),
                         r=[rlt, rs_wg], w=[bankres[b3]])
                    K.op("act", lambda: nc.scalar.activation(out=RG, in_=PSB[b1][:, 0:n], func=AF.Tanh, bias=HBG[:, c:c + 1], scale=0.5),
                         r=[bankres[b1], rs_small], w=[rlt])
                    K.op("act", lambda: nc.scalar.activation(out=IG, in_=PSB[b3][:, 0:n], func=AF.Tanh, bias=HBG[:, 4 + c:5 + c], scale=0.5),
                         r=[bankres[b3], rs_small], w=[rlt])
                    K.op("act", lambda: nc.scalar.activation(out=AA, in_=RG, func=AF.Exp, scale=SCH[:, c:c + 1], bias=SCH[:, c:c + 1]),
                         r=[rlt, rs_small], w=[rlt])
                    K.op("act", lambda: nc.scalar.activation(out=RG, in_=RG, func=AF.Exp, scale=SC1[:, c:c + 1], bias=SC1[:, c:c + 1]),
                         r=[rlt, rs_small], w=[rlt])

            def lru_B(k):
                for bi_, (t, sloc, n, c) in enumerate(batches[k]):
                    bufs, xcbf, rlt = LSETS[2 * (k % 2) + bi_]
                    MM = bufs[1][:, 0:n]
                    K.op("act", lambda: nc.scalar.activation(out=MM, in_=MM, func=AF.Sqrt, scale=-1.0, bias=ONEC), r=[rlt, rs_const], w=[rlt])

            def lru_C(k):
                for bi_, (t, sloc, n, c) in enumerate(batches[k]):
                    bufs, xcbf, rlt = LSETS[2 * (k % 2) + bi_]
                    XC, MM, IG, AA = [x[:, 0:n] for x in bufs]
                    BBt = IG; HS = MM
                    K.op("dve", lambda: nc.vector.tensor_scalar(out=IG, in0=IG, scalar1=0.5, scalar2=0.5, op0=ALU.mult, op1=ALU.add), r=[rlt], w=[rlt])
                    K.op("dve", lambda: nc.vector.tensor_tensor(out=BBt, in0=MM, in1=IG, op=ALU.mult), r=[rlt], w=[rlt])
                    K.op("dve", lambda: nc.vector.tensor_tensor(out=BBt, in0=BBt, in1=XC, op=ALU.mult), r=[rlt], w=[rlt])
                    if t.kind == "p":
                        K.op("dve", lambda: nc.vector.tensor_tensor_scan(out=HS, data0=AA, data1=BBt, initial=HC[:, c:c + 1],
                                                                         op0=ALU.mult, op1=ALU.add), r=[rlt, rs_hc], w=[rlt])
                        K.op("dve", lambda: nc.vector.tensor_copy(out=HC[:, c:c + 1], in_=HS[:, n - 1:n]), r=[rlt], w=[rs_hc])
                        if t.col0 + (sloc - t.loc) + n == TP:
                            K.op("dve", lambda: nc.vector.tensor_copy(out=FST[:, (12 + c) * NSEQ:(12 + c) * NSEQ + 1], in_=HS[:, n - 1:n]),
                                 r=[rlt], w=[rs_fst])
                            for kk in range(3):
                                K.op("dve", lambda: nc.vector.tensor_copy(out=FST[:, (c * 3 + kk) * NSEQ:(c * 3 + kk) * NSEQ + 1],
                                                                          in_=xbv(c, 3 + sloc + n - 3 + kk, 1)),
                                     r=[res("xb_%d" % t.loc)], w=[rs_fst])
                    else:
                        hs3 = v3(HS, NSQ, TS); aa3 = v3(AA, NSQ, TS); bb3 = v3(BBt, NSQ, TS)
                        for tt in range(TS):
                            prev = H0T[:, c * NSQ:(c + 1) * NSQ] if tt == 0 else hs3[:, :, tt - 1]
                            K.op("dve", lambda: nc.vector.tensor_tensor(out=hs3[:, :, tt], in0=aa3[:, :, tt], in1=prev, op=ALU.mult),
                                 r=[rlt, rs_h0t], w=[rlt])
                            K.op("dve", lambda: nc.vector.tensor_tensor(out=hs3[:, :, tt], in0=hs3[:, :, tt], in1=bb3[:, :, tt], op=ALU.add),
                                 r=[rlt], w=[rlt])
                        K.op("dve", lambda: nc.vector.tensor_copy(out=FST[:, (12 + c) * NSEQ + 1:(12 + c + 1) * NSEQ], in_=hs3[:, :, TS - 1]),
                             r=[rlt], w=[rs_fst])
                        for kk in range(3):
                            K.op("dve", lambda: nc.vector.tensor_copy(out=FST[:, (c * 3 + kk) * NSEQ + 1:(c * 3 + kk + 1) * NSEQ],
                                                                      in_=xbs4[:, c, :, 4 + kk]),
                                 r=[rs_xbs], w=[rs_fst])
                    K.op("dve", lambda: nc.vector.tensor_tensor(out=ylv(c, sloc, n), in0=gyv(c, sloc, n), in1=HS, op=ALU.mult),
                         r=[rlt, res("gy_%d" % t.loc)], w=[res("yl_%d" % t.loc)])

            lru_A(0)
            for k in range(len(batches)):
                if k + 1 < len(batches):
                    lru_A(k + 1)
                lru_B(k)
                lru_C(k)
            if gi == 0:
                xb3 = XB.rearrange("p (c w) -> p c w", c=4)
                K.op("dve", lambda: nc.vector.tensor_copy(out=v3(HIST, 4, 3), in_=xb3[:, :, 1024:1027]),
                     r=[res("xb_512")], w=[res("hist_save")])
            K.barrier()
            Tt = [WORK[:, i * 512:(i + 1) * 512] for i in range(4)]
            BTRf = WORK[:, 2048:2560]; BTIf = WORK[:, 2560:3072]; STRf = WORK[:, 3072:3584]; STIf = WORK[:, 3584:4096]
            PREf = WORK[:, 4096:4224]
            RHOT = WORK[:, 4352:6400]
            SRBs = [wbf(6400, 512), wbf(6656, 512)]
            NSIBs = [wbf(6912, 512), wbf(7168, 512)]
            rT = [res("s5_t%d" % i) for i in range(4)]
            r_btr = res("s5_btr"); r_bti = res("s5_bti"); r_str = res("s5_str"); r_sti = res("s5_sti")
            r_bt = [r_btr, r_bti]; r_st = [r_str, r_sti]; rpre = res("s5_pre"); r_rhot = res("s5_rhot")
            r_slr = res("s5_slr"); r_sli = res("s5_sli")
            r_srb = [res("s5_srb0"), res("s5_srb1")]
            r_crp = res("s5_crp")
            K.op("dve", lambda: nc.vector.tensor_copy(out=v3(RHOT, 16, NT5), in_=RHO.unsqueeze(2).to_broadcast([128, 16, NT5])),
                 r=[rs_small], w=[r_rhot])
            K.op("dve", lambda: nc.vector.memset(v3(RHOT, 16, NT5)[:, :, 0:1], 0.0), w=[r_rhot])
            units = []
            for t in group:
                if t.kind == "p":
                    for i in range(t.n // NT5):
                        for q in range(4):
                            units.append((t, t.loc + i * NT5, q))
            n = NT5
            ubanks = {}

            def emit_bu(ui):
                t, sloc, q = units[ui]
                br_b = bank("g1"); bi_b = bank("g3")
                ubanks[ui] = (br_b, bi_b)
                for oi in range(4):
                    o = 4 * q + oi
                    K.op("pe", lambda: nc.tensor.matmul(PSB[br_b][:, oi * n:(oi + 1) * n], lhsT=WBU[:, o * 128:(o + 1) * 128], rhs=uv(q, sloc, n),
                                                        start=True, stop=True),
                         r=[rs_wbu, res("u_%d" % t.loc)], w=[bankres[br_b]], inc=(oi == 3))
                for oi in range(4):
                    o = 4 * q + oi
                    K.op("pe", lambda: nc.tensor.matmul(PSB[bi_b][:, oi * n:(oi + 1) * n], lhsT=WBU[:, (16 + o) * 128:(17 + o) * 128], rhs=uv(q, sloc, n),
                                                        start=True, stop=True),
                         r=[rs_wbu, res("u_%d" % t.loc)], w=[bankres[bi_b]], inc=(oi == 3))

            pending_post = [None]

            def emit_post():
                if pending_post[0] is None:
                    return
                (t, sloc, q, yb_) = pending_post[0]
                pending_post[0] = None
                K.op("dve", lambda: nc.vector.scalar_tensor_tensor(out=PREf[:, 0:n], in0=uv(q, sloc, n), scalar=ptc(l, "s5_d", q),
                                                                   in1=PSB[yb_][:, 0:n], op0=ALU.mult, op1=ALU.add),
                     r=[bankres[yb_], res("u_%d" % t.loc), rs_pt], w=[rpre])
                K.op("act", lambda: nc.scalar.activation(out=gyv(q, sloc, n), in_=PREf[:, 0:n], func=AF.Gelu_apprx_tanh),
                     r=[rpre], w=[res("gy_%d" % t.loc)])

            if units:
                emit_bu(0)
            for ui, (t, sloc, q) in enumerate(units):
                if ui % 2 == 1:
                    ada_pump(1)
                if ui + 1 < len(units):
                    emit_bu(ui + 1)
                br_b, bi_b = ubanks[ui]
                if q == 0:
                    K.op("dve", lambda: nc.vector.tensor_tensor(out=CRPR, in0=RHO, in1=CR, op=ALU.mult), r=[rs_small, rs_carry], w=[r_crp])
                    K.op("dve", lambda: nc.vector.tensor_tensor(out=CRPI, in0=RHO, in1=CI, op=ALU.mult), r=[rs_small, rs_carry], w=[r_crp])
                cs = cosv(4 * q, 4, 0, n); sn = sinv(4 * q, 4, 0, n)
                sh = lambda a: v3(a, 4, n)
                pbr = sh(PSB[br_b][:, 0:4 * n]); pbi = sh(PSB[bi_b][:, 0:4 * n])
                K.op("dve", lambda: nc.vector.tensor_tensor(out=sh(Tt[0]), in0=pbr, in1=cs, op=ALU.mult), r=[bankres[br_b], rs_tab], w=[rT[0]])
                K.op("dve", lambda: nc.vector.tensor_tensor(out=sh(Tt[1]), in0=pbi, in1=sn, op=ALU.mult), r=[bankres[bi_b], rs_tab], w=[rT[1]])
                K.op("dve", lambda: nc.vector.tensor_tensor(out=sh(Tt[2]), in0=pbi, in1=cs, op=ALU.mult), r=[bankres[bi_b], rs_tab], w=[rT[2]])
                K.op("dve", lambda: nc.vector.tensor_tensor(out=sh(Tt[3]), in0=pbr, in1=sn, op=ALU.mult), r=[bankres[br_b], rs_tab], w=[rT[3]])
                emit_post()
                K.op("dve", lambda: nc.vector.tensor_tensor(out=BTRf, in0=Tt[0], in1=Tt[1], op=ALU.add), r=[rT[0], rT[1]], w=[r_btr])
                K.op("dve", lambda: nc.vector.tensor_tensor(out=BTIf, in0=Tt[2], in1=Tt[3], op=ALU.subtract), r=[rT[2], rT[3]], w=[r_bti])
                K.op("dve", lambda: nc.vector.tensor_tensor(out=sh(BTRf)[:, :, 0], in0=sh(BTRf)[:, :, 0], in1=CRPR[:, 4 * q:4 * q + 4], op=ALU.add),
                     r=[r_crp, r_btr], w=[r_btr])
                K.op("dve", lambda: nc.vector.tensor_tensor(out=sh(BTIf)[:, :, 0], in0=sh(BTIf)[:, :, 0], in1=CRPI[:, 4 * q:4 * q + 4], op=ALU.add),
                     r=[r_crp, r_bti], w=[r_bti])
                rh = RHOT[:, 4 * q * n:(4 * q + 4) * n]
                K.op("dve", lambda: nc.vector.tensor_tensor_scan(out=STRf, data0=rh, data1=BTRf, initial=0.0, op0=ALU.mult, op1=ALU.add),
                     r=[r_btr, r_rhot], w=[r_str])
                K.op("dve", lambda: nc.vector.tensor_tensor_scan(out=STIf, data0=rh, data1=BTIf, initial=0.0, op0=ALU.mult, op1=ALU.add),
                     r=[r_bti, r_rhot], w=[r_sti])
                K.op("dve", lambda: nc.vector.tensor_tensor(out=sh(Tt[0]), in0=sh(STRf), in1=cs, op=ALU.mult), r=[r_str, rs_tab], w=[rT[0]])
                K.op("dve", lambda: nc.vector.tensor_tensor(out=sh(Tt[2]), in0=sh(STRf), in1=sn, op=ALU.mult), r=[r_str, rs_tab], w=[rT[2]])
                K.op("dve", lambda: nc.vector.tensor_tensor(out=sh(Tt[1]), in0=sh(STIf), in1=sn, op=ALU.mult), r=[r_sti, rs_tab], w=[rT[1]])
                K.op("dve", lambda: nc.vector.tensor_tensor(out=sh(Tt[3]), in0=sh(STIf), in1=cs, op=ALU.mult), r=[r_sti, rs_tab], w=[rT[3]])
                sbi = ui % 2
                SRB = SRBs[sbi]; NSIB = NSIBs[sbi]; rsr = r_srb[sbi]
                K.op("dve", lambda: nc.vector.tensor_tensor(out=SRB[:, 0:4 * n], in0=Tt[0], in1=Tt[1], op=ALU.subtract), r=[rT[0], rT[1]], w=[rsr])
                K.op("dve", lambda: nc.vector.scalar_tensor_tensor(out=NSIB[:, 0:4 * n], in0=Tt[2], scalar=-1.0, in1=Tt[3], op0=ALU.mult, op1=ALU.subtract),
                     r=[rT[2], rT[3]], w=[rsr])
                K.op("dve", lambda: nc.vector.tensor_copy(out=SLR[:, 4 * q:4 * q + 4], in_=sh(STRf)[:, :, n - 1]), r=[r_str], w=[r_slr])
                K.op("dve", lambda: nc.vector.tensor_copy(out=SLI[:, 4 * q:4 * q + 4], in_=sh(STIf)[:, :, n - 1]), r=[r_sti], w=[r_sli])
                yb_ = bank("o")
                for oi in range(4):
                    o = 4 * q + oi
                    K.op("pe", lambda: nc.tensor.matmul(PSB[yb_][:, 0:n], lhsT=WC[:, o * 128:(o + 1) * 128], rhs=SRB[:, oi * n:(oi + 1) * n],
                                                        start=(oi == 0), stop=False), r=[rs_wc, rsr], w=[bankres[yb_]], inc=False)
                    K.op("pe", lambda: nc.tensor.matmul(PSB[yb_][:, 0:n], lhsT=WC[:, (16 + o) * 128:(17 + o) * 128], rhs=NSIB[:, oi * n:(oi + 1) * n],
                                                        start=False, stop=(oi == 3)), r=[rs_wc, rsr], w=[bankres[yb_]], inc=(oi == 3))
                pending_post[0] = (t, sloc, q, yb_)
                if q == 3:
                    cN = cosv(0, 16, NT5, 1).rearrange("p o t -> p (o t)"); sN = sinv(0, 16, NT5, 1).rearrange("p o t -> p (o t)")
                    if t.col0 + (sloc - t.loc) + n == TP:
                        cE = cosv(0, 16, NT5 - 1, 1); sE = sinv(0, 16, NT5 - 1, 1)
                        fr = v3(FST[:, 16 * NSEQ:32 * NSEQ], 16, NSEQ)[:, :, 0:1]
                        fi = v3(FST[:, 32 * NSEQ:48 * NSEQ], 16, NSEQ)[:, :, 0:1]
                        cmul(fr, fi, SLR.unsqueeze(2), SLI.unsqueeze(2), cE, sE, TMPC.unsqueeze(2), TMPD.unsqueeze(2),
                             [r_slr, r_sli, rs_tab], [rs_fst], res("s5_cm_tmp"))
                    else:
                        cmul(CR, CI, SLR, SLI, cN, sN, TMPC, TMPD, [r_slr, r_sli, rs_tab], [rs_carry], res("s5_cm_tmp"))
            emit_post()
            ada_flush()
            ST = [BTRf, BTIf, STRf, STIf, Tt[0], Tt[1]]
            rst = res("s5_tmp"); rsr = r_srb[0]
            SRB = SRBs[0]; NSIB = NSIBs[0]
            for t in group:
                if t.kind != "s":
                    continue
                sloc, n = t.loc, t.n
                for q in range(4):
                    br_b = bank("g1"); bi_b = bank("g3")
                    for oi in range(4):
                        o = 4 * q + oi
                        K.op("pe", lambda: nc.tensor.matmul(PSB[br_b][:, oi * n:(oi + 1) * n], lhsT=WBU[:, o * 128:(o + 1) * 128], rhs=uv(q, sloc, n),
                                                            start=True, stop=True),
                             r=[rs_wbu, res("u_%d" % t.loc)], w=[bankres[br_b]], inc=(oi == 3))
                    for oi in range(4):
                        o = 4 * q + oi
                        K.op("pe", lambda: nc.tensor.matmul(PSB[bi_b][:, oi * n:(oi + 1) * n], lhsT=WBU[:, (16 + o) * 128:(17 + o) * 128], rhs=uv(q, sloc, n),
                                                            start=True, stop=True),
                             r=[rs_wbu, res("u_%d" % t.loc)], w=[bankres[bi_b]], inc=(oi == 3))
                    allT = [rT[0], rT[1], rT[2], rT[3], r_bt, r_st]
                    BTR, BTI, STR, STI, T1, T2 = [x[:, 0:4 * n] for x in ST]
                    cs = cosv(4 * q, 4, 0, TS).unsqueeze(2).to_broadcast([128, 4, NSQ, TS])
                    sn = sinv(4 * q, 4, 0, TS).unsqueeze(2).to_broadcast([128, 4, NSQ, TS])
                    sh = lambda a: a.rearrange("p (o s t) -> p o s t", o=4, s=NSQ, t=TS)
                    pbr = sh(PSB[br_b][:, 0:4 * n]); pbi = sh(PSB[bi_b][:, 0:4 * n])
                    K.op("dve", lambda: nc.vector.tensor_tensor(out=sh(T1), in0=pbr, in1=cs, op=ALU.mult), r=[bankres[br_b], rs_tab], w=[rst, allT])
                    K.op("dve", lambda: nc.vector.tensor_tensor(out=sh(T2), in0=pbi, in1=sn, op=ALU.mult), r=[bankres[bi_b], rs_tab], w=[rst])
                    K.op("dve", lambda: nc.vector.tensor_tensor(out=BTR, in0=T1, in1=T2, op=ALU.add), r=[rst], w=[rst])
                    K.op("dve", lambda: nc.vector.tensor_tensor(out=sh(T1), in0=pbi, in1=cs, op=ALU.mult), r=[bankres[bi_b], rs_tab], w=[rst])
                    K.op("dve", lambda: nc.vector.tensor_tensor(out=sh(T2), in0=pbr, in1=sn, op=ALU.mult), r=[bankres[br_b], rs_tab], w=[rst])
                    K.op("dve", lambda: nc.vector.tensor_tensor(out=BTI, in0=T1, in1=T2, op=ALU.subtract), r=[rst], w=[rst])
                    str4 = sh(STR); sti4 = sh(STI); btr4 = sh(BTR); bti4 = sh(BTI)
                    rb = RHO[:, 4 * q:4 * q + 4].unsqueeze(2).to_broadcast([128, 4, NSQ])
                    for tt in range(TS):
                        for (s4, b4, s0) in ((str4, btr4, S0TR), (sti4, bti4, S0TI)):
                            prev = v3(s0, 16, NSQ)[:, 4 * q:4 * q + 4, :] if tt == 0 else s4[:, :, :, tt - 1]
                            K.op("dve", lambda: nc.vector.tensor_tensor(out=s4[:, :, :, tt], in0=prev, in1=rb, op=ALU.mult),
                                 r=[rst, rs_s0, rs_small], w=[rst])
                            K.op("dve", lambda: nc.vector.tensor_tensor(out=s4[:, :, :, tt], in0=s4[:, :, :, tt], in1=b4[:, :, :, tt], op=ALU.add),
                                 r=[rst], w=[rst])
                    c3 = cosv(4 * q, 4, TS - 1, 1).to_broadcast([128, 4, NSQ]); s3_ = sinv(4 * q, 4, TS - 1, 1).to_broadcast([128, 4, NSQ])
                    fr = v3(FST[:, (16 + 4 * q) * NSEQ:(16 + 4 * q + 4) * NSEQ], 4, NSEQ)[:, :, 1:NSEQ]
                    fi = v3(FST[:, (32 + 4 * q) * NSEQ:(32 + 4 * q + 4) * NSEQ], 4, NSEQ)[:, :, 1:NSEQ]
                    tt1 = v3(TAILX[:, 0:64], 4, NSQ); tt2 = v3(TAILX[:, 64:128], 4, NSQ)
                    cmul(fr, fi, str4[:, :, :, TS - 1], sti4[:, :, :, TS - 1], c3, s3_, tt1, tt2, [rst, rs_tab], [rs_fst], res("tailx"))
                    K.op("dve", lambda: nc.vector.tensor_tensor(out=sh(T1), in0=sh(STR), in1=cs, op=ALU.mult), r=[rst, rs_tab], w=[rst])
                    K.op("dve", lambda: nc.vector.tensor_tensor(out=sh(T2), in0=sh(STI), in1=sn, op=ALU.mult), r=[rst, rs_tab], w=[rst])
                    K.op("dve", lambda: nc.vector.tensor_tensor(out=SRB[:, 0:4 * n], in0=T1, in1=T2, op=ALU.subtract), r=[rst], w=[rsr])
                    K.op("dve", lambda: nc.vector.tensor_tensor(out=sh(T1), in0=sh(STR), in1=sn, op=ALU.mult), r=[rst, rs_tab], w=[rst])
                    K.op("dve", lambda: nc.vector.tensor_tensor(out=sh(T2), in0=sh(STI), in1=cs, op=ALU.mult), r=[rst, rs_tab], w=[rst])
                    K.op("dve", lambda: nc.vector.scalar_tensor_tensor(out=NSIB[:, 0:4 * n], in0=T1, scalar=-1.0, in1=T2, op0=ALU.mult, op1=ALU.subtract),
                         r=[rst], w=[rsr])
                    yb_ = bank("o")
                    for oi in range(4):
                        o = 4 * q + oi
                        K.op("pe", lambda: nc.tensor.matmul(PSB[yb_][:, 0:n], lhsT=WC[:, o * 128:(o + 1) * 128], rhs=SRB[:, oi * n:(oi + 1) * n],
                                                            start=(oi == 0), stop=False), r=[rs_wc, rsr], w=[bankres[yb_]], inc=False)
                        K.op("pe", lambda: nc.tensor.matmul(PSB[yb_][:, 0:n], lhsT=WC[:, (16 + o) * 128:(17 + o) * 128], rhs=NSIB[:, oi * n:(oi + 1) * n],
                                                            start=False, stop=(oi == 3)), r=[rs_wc, rsr], w=[bankres[yb_]], inc=(oi == 3))
                    K.op("dve", lambda: nc.vector.scalar_tensor_tensor(out=PREf[:, 0:n], in0=uv(q, sloc, n), scalar=ptc(l, "s5_d", q),
                                                                       in1=PSB[yb_][:, 0:n], op0=ALU.mult, op1=ALU.add),
                         r=[bankres[yb_], res("u_%d" % t.loc), rs_pt], w=[rpre])
                    K.op("act", lambda: nc.scalar.activation(out=gyv(q, sloc, n), in_=PREf[:, 0:n], func=AF.Gelu_apprx_tanh),
                         r=[rpre], w=[res("gy_%d" % t.loc)])
            K.barrier()
            wgl = wglu_d[l].rearrange("(k p) f -> p k f", p=128)
            sl = ring_load(lambda slot: [(slot[:, 0:2048].rearrange("p (k f) -> p k f", k=4), wgl[:, :, :])])
            sg3 = RING[sl][:, 0:2048].rearrange("p (k f) -> p k f", k=4)
            SG = [WORK[:, i * 512:(i + 1) * 512] for i in range(4)]
            for t in group:
                n = t.n
                bs = []
                for oc in range(4):
                    b = bank(("g1", "g3")[oc % 2])
                    bs.append(b)
                    for k in range(4):
                        K.op("pe", lambda: nc.tensor.matmul(PSB[b][:, 0:n], lhsT=sg3[:, k, oc * 128:(oc + 1) * 128], rhs=gyv(k, t.loc, n),
                                                            start=(k == 0), stop=(k == 3)),
                             r=[ringres[sl], res("gy_%d" % t.loc)], w=[bankres[b]], inc=(k == 3))
                for oc in range(4):
                    b = bs[oc]
                    rsg = res("sg_%d" % oc)
                    K.op("act", lambda: nc.scalar.activation(out=SG[oc][:, 0:n], in_=PSB[b][:, 0:n], func=AF.Sigmoid, bias=ptc(l, "b_glu", oc), scale=1.0),
                         r=[bankres[b], rs_pt], w=[rsg])
                for oc in range(4):
                    rsg = res("sg_%d" % oc)
                    K.op("dve", lambda: nc.vector.tensor_tensor(out=gyv(oc, t.loc, n), in0=gyv(oc, t.loc, n), in1=SG[oc][:, 0:n], op=ALU.mult),
                         r=[rsg], w=[res("gy_%d" % t.loc)])
            wos = wout_d[l].rearrange("(k p) f -> p k f", p=128)
            for half in range(2):
                sl = ring_load(lambda slot: [(slot[:, 0:4096].rearrange("p (k f) -> p k f", k=8), wos[:, :, half * 512:(half + 1) * 512])])
                so3 = RING[sl][:, 0:4096].rearrange("p (k f) -> p k f", k=8)
                for t in group:
                    n = t.n
                    for oc in range(4):
                        d = half * 4 + oc
                        b = bank("o")
                        for k in range(KC):
                            rhs = ylv(k, t.loc, n) if k < 4 else gyv(k - 4, t.loc, n)
                            K.op("pe", lambda: nc.tensor.matmul(PSB[b][:, 0:n], lhsT=so3[:, k, oc * 128:(oc + 1) * 128], rhs=rhs,
                                                                start=(k == 0), stop=(k == KC - 1)),
                                 r=[ringres[sl], res("yl_%d" % t.loc), res("gy_%d" % t.loc)], w=[bankres[b]], inc=(k == KC - 1))
                        residual(t, s, d, b)

        ds_o = K.dsem()

        def state_out(l):
            OST = AUX[0:NSEQ, 0:2048]
            rs_ost = res("ost")
            pieces = [(0, 12, oconv_d, None), (12, 4, oh_d, None), (16, 16, osr_d, None), (32, 16, osi_d, None)]
            for (c0, ncnk, dst, _) in pieces:
                for g4 in range(0, ncnk, 4):
                    b = bank("m")
                    for i in range(4):
                        cidx = c0 + g4 + i
                        K.op("pe", lambda: nc.tensor.transpose(out=PSB[b][0:NSEQ, i * 128:(i + 1) * 128], in_=FST[:, cidx * NSEQ:(cidx + 1) * NSEQ], identity=IDF[:]),
                             r=[rs_fst, rs_const], w=[bankres[b]], inc=(i == 3))
                    if c0 == 0:
                        for i in range(4):
                            cidx = g4 + i
                            c, kk = cidx // 3, cidx % 3
                            K.op("dve", lambda: nc.vector.tensor_copy(out=OST[:, kk * 512 + c * 128: kk * 512 + (c + 1) * 128], in_=PSB[b][0:NSEQ, i * 128:(i + 1) * 128]),
                                 r=[bankres[b]], w=[rs_ost, rs_aux[0], rs_aux[1]])
                    else:
                        K.op("dve", lambda: nc.vector.tensor_copy(out=OST[:, g4 * 128:(g4 + 4) * 128], in_=PSB[b][0:NSEQ, 0:512]),
                             r=[bankres[b]], w=[rs_ost, rs_aux[0], rs_aux[1]])
                K.dma("sp", ds_o, [(dst[l], OST[:, 0:ncnk * 128])], r=[rs_ost, rs_aux[0], rs_aux[1]])

        def final_out():
            K.barrier()
            ds_y = [K.dsem() for _ in range(2)]
            XN = [WORK[:, i * 1024:(i + 1) * 1024] for i in range(2)]
            YST = [WORK[:, 2048 + i * 1024: 2048 + (i + 1) * 1024] for i in range(2)]
            XSQfs = [wbf(4352, 8 * 128), wbf(4352 + 512, 8 * 128)]
            RSs = [WORK[:, 5400:5528], WORK[:, 5528:5656]]; SQs = [WORK[:, 5656:5784], WORK[:, 5784:5912]]
            ftiles = []
            for (dst, ntok, colbase) in ((yp_d, TP, 0), (ys_d, NSAMP, TP)):
                for t0 in range(0, ntok, 128):
                    ftiles.append((dst, t0, min(128, ntok - t0), colbase + t0))

            def fres(i):
                pi_ = i % 2
                return (res("f_xsq%d" % pi_), res("f_rs%d" % pi_), res("f_xn%d" % pi_), res("f_yst%d" % pi_))

            def stage_a(i):
                dst, t0, n, col = ftiles[i]
                pi_ = i % 2
                tl = tiles_all[min(col // 512, 4)]
                rxs, rrs, rxn, ryst = fres(i)
                XSQf = XSQfs[pi_]; RS = RSs[pi_]; SQ = SQs[pi_]
                for c in range(KC):
                    K.op("act", lambda: nc.scalar.activation(out=XSQf[:, c * 128: c * 128 + n], in_=xv(c, col, n), func=AF.Square),
                         r=[xres(tl, c)], w=[rxs])
                b = bank("m")
                for c in range(KC):
                    K.op("pe", lambda: nc.tensor.matmul(PSB[b][:, 0:n], lhsT=ONESB[:], rhs=XSQf[:, c * 128: c * 128 + n], start=(c == 0), stop=(c == KC - 1)),
                         r=[rxs, rs_const], w=[bankres[b]], inc=(c == KC - 1))
                K.op("act", lambda: nc.scalar.activation(out=SQ[:, 0:n], in_=PSB[b][:, 0:n], func=AF.Sqrt, bias=EPSC, scale=1.0 / D),
                     r=[bankres[b], rs_const], w=[rrs])

            def stage_b(i):
                dst, t0, n, col = ftiles[i]
                pi_ = i % 2
                tl = tiles_all[min(col // 512, 4)]
                rxs, rrs, rxn, ryst = fres(i)
                RS = RSs[pi_]; SQ = SQs[pi_]
                K.op("dve", lambda: nc.vector.reciprocal(out=RS[:, 0:n], in_=SQ[:, 0:n]), r=[rrs], w=[rrs])
                for c in range(KC):
                    K.op("dve", lambda: nc.vector.scalar_tensor_tensor(out=XN[pi_][:, c * 128: c * 128 + n], in0=xv(c, col, n),
                                                                       scalar=PT[:, PR_FINAL + c: PR_FINAL + c + 1], in1=RS[:, 0:n], op0=ALU.mult, op1=ALU.mult),
                         r=[xres(tl, c), rrs, rs_pt], w=[rxn])
                for half in range(2):
                    b = bank("o")
                    for cc in range(4):
                        c = half * 4 + cc
                        K.op("pe", lambda: nc.tensor.transpose(out=PSB[b][0:n, cc * 128:(cc + 1) * 128], in_=XN[pi_][:, c * 128: c * 128 + n], identity=IDF[:]),
                             r=[rxn, rs_const], w=[bankres[b]], inc=(cc == 3))
                    if half == 0:
                        K.op("dve", lambda: nc.vector.tensor_copy(out=YST[pi_][0:n, 0:512], in_=PSB[b][0:n, 0:512]), r=[bankres[b]], w=[ryst])
                    else:
                        K.op("act", lambda: nc.scalar.activation(out=YST[pi_][0:n, 512:1024], in_=PSB[b][0:n, 0:512], func=AF.Identity), r=[bankres[b]], w=[ryst])
                K.dma("sp", ds_y[pi_], [(dst[t0:t0 + n, :], YST[pi_][0:n, :])], r=[ryst])

            stage_a(0)
            for i in range(len(ftiles)):
                if i + 1 < len(ftiles):
                    stage_a(i + 1)
                stage_b(i)

        def main_prog():
            stage = [0]

            def stop():
                stage[0] += 1
                return STOP_AT is not None and stage[0] > STOP_AT
            if stop():
                return
            ada_enqueue(0, 0, 10)
            ada_flush()
            for l in range(DEPTH):
                if stop():
                    return
                ffn(l, 0, 0, "norm_ffn1", groups[0], True, groups[1])
                ffn(l, 0, 0, "norm_ffn1", groups[1], False, None)
                if stop():
                    return
                mixer_setup(l)
                if stop():
                    return
                for gi, g in enumerate(groups):
                    if gi == 0:
                        ada_enqueue(l, 10, 18)
                    if l + 1 < DEPTH:
                        if gi == 0:
                            ada_enqueue(l + 1, 0, 6)
                        else:
                            ada_enqueue(l + 1, 6, 10)
                    mixer_group(l, gi, g)
                    if stop():
                        return
                ada_flush()
                state_out(l)
                if stop():
                    return
                ffn(l, 1, 2, "norm_ffn2", groups[0], True, groups[1])
                ffn(l, 1, 2, "norm_ffn2", groups[1], False, None)
                if stop():
                    return
            final_out()
        main_prog()
        K.finish()
        rec_out = K.rec
    if want_rec:
        return rec_out
    return nc


_NC_CACHE = {}


def _host_layouts(inp):
    L = DEPTH
    prow = np.zeros((PR_ROWS, 128), np.float32)
    for l in range(L):
        for name, k in PR_NAMES:
            r0 = PR_LAYER * l + PR_OFF[name]
            prow[r0:r0 + k] = np.asarray(inp[name][l], np.float32).reshape(k, 128)
    prow[PR_FINAL:PR_FINAL + 8] = np.asarray(inp["norm_final"], np.float32).reshape(8, 128)
    ld = np.asarray(inp["s5_log_dt"], np.float32)
    dtx = np.repeat(ld.reshape(L, 16, 2).transpose(0, 2, 1), 64, axis=1)
    dtx = np.ascontiguousarray(dtx)
    wg = np.zeros((L, 128, 8, 128), np.float32)
    for gi, nm in enumerate(("w_rg", "w_ig")):
        w = np.asarray(inp[nm], np.float32)
        for c in range(4):
            wg[:, 0:64, gi * 4 + c, 0:64] = w[:, 2 * c]
            wg[:, 64:128, gi * 4 + c, 64:128] = w[:, 2 * c + 1]
    wg = wg.reshape(L, 128, 8 * 128)
    bnat = np.zeros((L, 2, 128, 16, 128), np.float32)
    for comp, nm in enumerate(("s5_b_re", "s5_b_im")):
        Bm = np.asarray(inp[nm], np.float32)
        for o in range(16):
            for gl in range(2):
                g = 2 * o + gl
                cb = (g % 8) * 16
                bnat[:, comp, gl * 64:(gl + 1) * 64, o, cb:cb + 16] = Bm[:, g]
    bnat = bnat.reshape(L, 2, 128, 16 * 128)
    cpad = np.zeros((L, 128, 2, 16, 128), np.float32)
    for comp, nm in enumerate(("s5_c_re", "s5_c_im")):
        Cm = np.asarray(inp[nm], np.float32)
        for o in range(16):
            for gl in range(2):
                g = 2 * o + gl
                cb = (g % 8) * 16
                cpad[:, gl * 64:(gl + 1) * 64, comp, o, cb:cb + 16] = Cm[:, g].transpose(0, 2, 1)
    cpad = cpad.reshape(L, 128, 32 * 128)
    return prow, dtx, wg, bnat, cpad


def kernel(**inp):
    if "nc" not in _NC_CACHE:
        _NC_CACHE["nc"] = build_program()
    nc = _NC_CACHE["nc"]
    f = lambda a: np.ascontiguousarray(np.asarray(a, np.float32))
    prow, dtx, wg, bnat, cpad = _host_layouts(inp)
    shared = {
        "prow": prow, "dtx": dtx, "wgates": wg, "bnat": bnat, "cpad": cpad,
        "w_ada": f(inp["w_ada"]),
        "w1_ffn1": f(inp["w1_ffn1"]), "w3_ffn1": f(inp["w3_ffn1"]), "w2_ffn1": f(inp["w2_ffn1"]),
        "w1_ffn2": f(inp["w1_ffn2"]), "w3_ffn2": f(inp["w3_ffn2"]), "w2_ffn2": f(inp["w2_ffn2"]),
        "w_in": f(inp["w_in"]), "w_glu": f(inp["w_glu"]), "w_out": f(inp["w_out"]),
    }
    xp = f(inp["x_prompt"]); xs = f(inp["x_sample"]); cp = f(inp["c_prompt"]); cs = f(inp["c_sample"])
    stc = f(inp["state_lru_conv"]); sth = f(inp["state_lru_h"]); sr = f(inp["state_s5_re"]); si = f(inp["state_s5_im"])
    in_maps = []
    for i in range(NCORES):
        s0, s1 = NSQ * i, NSQ * (i + 1)
        m = dict(shared)
        m["xp"] = xp[i]
        m["xs"] = np.ascontiguousarray(xs[s0:s1].reshape(NSAMP, D))
        m["c17"] = np.ascontiguousarray(np.concatenate([cp[i:i + 1], cs[s0:s1]], axis=0))
        m["st_conv"] = np.ascontiguousarray(stc[:, s0:s1].reshape(DEPTH, NSQ, 3 * 512))
        m["st_h"] = np.ascontiguousarray(sth[:, s0:s1])
        m["st_sr"] = np.ascontiguousarray(sr[:, s0:s1].reshape(DEPTH, NSQ, 2048))
        m["st_si"] = np.ascontiguousarray(si[:, s0:s1].reshape(DEPTH, NSQ, 2048))
        in_maps.append(m)
    res = run_bass_kernel_spmd(nc, in_maps, core_ids=list(range(NCORES)))
    R = res.results
    B = NCORES
    y_prompt = np.stack([R[i]["y_p"] for i in range(B)], axis=0).astype(np.float32)
    y_sample = np.concatenate([R[i]["y_s"].reshape(NSQ, TS, D) for i in range(B)], axis=0).astype(np.float32)

    def gather(name, tail):
        p = np.stack([R[i][name][:, 0] for i in range(B)], axis=1)
        s = np.concatenate([R[i][name][:, 1:] for i in range(B)], axis=1)
        return (p.reshape((DEPTH, B) + tail).astype(np.float32), s.reshape((DEPTH, NSQ * B) + tail).astype(np.float32))
    p_conv, s_conv = gather("o_conv", (3, 512))
    p_h, s_h = gather("o_h", (512,))
    p_sr, s_sr = gather("o_sr", (32, 64))
    p_si, s_si = gather("o_si", (32, 64))
    return (y_prompt, y_sample, p_conv, p_h, p_sr, p_si, s_conv, s_h, s_sr, s_si)
```

```python
import numpy as np
from contextlib import ExitStack
import concourse.bass as bass
import concourse.mybir as mybir
from concourse.bass_utils import run_bass_kernel_spmd

F32 = mybir.dt.float32
BF16 = mybir.dt.bfloat16
I32 = mybir.dt.int32
AF = mybir.ActivationFunctionType
ALU = mybir.AluOpType

NCORES = 8
D = 1024
KC = 8
DFF = 2816
FC = 22
TP = 2048
NSQ = 16
TS = 4
NSAMP = NSQ * TS
NTOK = TP + NSAMP
NSEQ = 17
DEPTH = 2
NT5 = 128
PI = float(np.pi)
TWO_PI = float(2 * np.pi)
EPS = 1e-6

PR_NAMES = [("norm_ffn1", 8), ("norm_mix", 8), ("norm_ffn2", 8), ("conv_w", 16), ("conv_b", 4),
            ("b_rg", 4), ("b_ig", 4), ("lru_lambda", 4), ("s5_d", 4), ("b_glu", 4),
            ("s5_a_re", 16), ("s5_a_im", 16), ("b_ada", 72)]
PR_OFF = {}
_o = 0
for _n, _k in PR_NAMES:
    PR_OFF[_n] = _o
    _o += _k
PR_LAYER = _o
PR_FINAL = PR_LAYER * DEPTH
PR_ROWS = 384

SAME_ENGINE_SYNC = ('act', 'pool', 'dve')
STOP_AT = None
SETUP_PARTS = 4
import os as _os
DBG_XT = int(_os.environ.get('DBG_XT', '999'))
DBG_XV = _os.environ.get('DBG_XV', '')


class Res:
    __slots__ = ("w", "rs", "name")

    def __init__(self, name=""):
        self.w = None
        self.rs = {}
        self.name = name


class DSem:
    def __init__(self, handle, key):
        self.h = handle
        self.key = key
        self.v = 0


class Sched:
    def __init__(self, nc, es, waited=None):
        self.nc = nc
        self.es = es
        self.waited = waited
        self.rec = {k: set() for k in ("pe", "dve", "act", "pool", "sp")}
        self.last_inc = {k: 0 for k in ("pe", "dve", "act", "pool", "sp")}
        self.eng = dict(pe=nc.tensor, dve=nc.vector, act=nc.scalar, pool=nc.gpsimd, sp=nc.sync)
        self.sem = {k: es.enter_context(nc.semaphore("cs_" + k)) for k in self.eng}
        self.cnt = {k: 0 for k in self.eng}
        self.known = {k: {} for k in self.eng}
        self.ndsem = 0
        self.dsems = []

    def dsem(self):
        h = self.es.enter_context(self.nc.semaphore("ds%d" % self.ndsem))
        d = DSem(h, "d%d" % self.ndsem)
        self.ndsem += 1
        self.dsems.append(d)
        return d

    @staticmethod
    def _flat(xs):
        out = []
        for x in xs:
            if isinstance(x, (list, tuple)):
                out.extend(Sched._flat(x))
            else:
                out.append(x)
        return out

    def barrier(self, engs=("pe", "dve", "act", "sp")):
        for e in engs:
            for f in ("pe", "dve", "act"):
                if self.cnt[f] == 0 or (f == e and e in ("pe", "sp")):
                    continue
                if self.known[e].get(f, 0) < self.cnt[f]:
                    self.eng[e].wait_ge(self.sem[f], self.cnt[f])
                    self.known[e][f] = self.cnt[f]
                    self.rec[f].add(self.cnt[f])

    def _waits(self, e, reads, writes):
        reads = self._flat(reads); writes = self._flat(writes)
        deps = {}
        raw_same = [0]

        def add(tok, raw=False):
            key, h, v = tok
            if key == e:
                if raw and v > raw_same[0]:
                    raw_same[0] = v
                return
            if key not in deps or deps[key][1] < v:
                deps[key] = (h, v)
        for r in reads:
            if r.w is not None:
                add(r.w, True)
        for w in writes:
            if w.w is not None:
                add(w.w, not w.name.startswith("bank"))
            for t in w.rs.values():
                add(t, not w.name.startswith("bank"))
        if raw_same[0] > 0:
            deps[e] = (self.sem[e], raw_same[0])
        for key, (h, v) in deps.items():
            if key == e:
                if e == "pe" or e not in SAME_ENGINE_SYNC:
                    continue
                if v > self.cnt[e]:
                    continue
            if self.known[e].get(key, 0) < v:
                self.eng[e].wait_ge(h, v)
                self.known[e][key] = v
                if key in self.rec:
                    self.rec[key].add(v)

    def _record(self, tok, reads, writes):
        reads = self._flat(reads); writes = self._flat(writes)
        key = tok[0]
        for r in reads:
            if key not in r.rs or r.rs[key][2] < tok[2]:
                r.rs[key] = tok
        for w in writes:
            w.w = tok
            w.rs = {}

    def op(self, e, fn, r=(), w=(), inc=True):
        r = self._flat(r); w = self._flat(w)
        w = w + [x for x in r if x.name.startswith("bank")]
        r = [x for x in r if not x.name.startswith("bank")]
        self._waits(e, r, w)
        inst = fn()
        self.cnt[e] += 1
        k = self.cnt[e]
        if self.waited is None or k in self.waited[e]:
            inst.then_inc(self.sem[e], k - self.last_inc[e])
            self.last_inc[e] = k
        tok = (e, self.sem[e], k)
        self._record(tok, r, w)
        return inst

    def dma(self, q, ds, pairs, r=(), w=()):
        self._waits(q, r, w)
        for (o, i) in pairs:
            self.eng[q].dma_start(out=o, in_=i).then_inc(ds.h, 16)
            ds.v += 16
        tok = (ds.key, ds.h, ds.v)
        self._record(tok, r, w)

    def finish(self):
        for d in self.dsems:
            if d.v > 0 and self.known["sp"].get(d.key, 0) < d.v:
                self.nc.sync.wait_ge(d.h, d.v)
        for e in ("pe", "dve", "act", "pool"):
            if self.cnt[e] > 0:
                self.nc.sync.wait_ge(self.sem[e], self.cnt[e])
                self.rec[e].add(self.cnt[e])


class Tile:
    def __init__(self, col0, n, kind, loc):
        self.col0 = col0
        self.n = n
        self.kind = kind
        self.loc = loc


def build_program(waited=None, want_rec=False):
    if waited is None and not want_rec:
        rec = build_program(None, True)
        return build_program(rec, False)
    nc = bass.Bass("TRN2", target_bir_lowering=False)

    def din(name, shape):
        return nc.dram_tensor(name, list(shape), F32, kind="ExternalInput").ap()

    def dout(name, shape):
        return nc.dram_tensor(name, list(shape), F32, kind="ExternalOutput").ap()

    xp_d = din("xp", [TP, D])
    xs_d = din("xs", [NSAMP, D])
    c17_d = din("c17", [NSEQ, D])
    stc_d = din("st_conv", [DEPTH, NSQ, 3 * 512])
    sth_d = din("st_h", [DEPTH, NSQ, 512])
    str_d = din("st_sr", [DEPTH, NSQ, 2048])
    sti_d = din("st_si", [DEPTH, NSQ, 2048])
    prow_d = din("prow", [PR_ROWS, 128])
    dtx_d = din("dtx", [DEPTH, 128, 16])
    wada_d = din("w_ada", [DEPTH, D, 9 * D])
    w1_d = [din("w1_ffn1", [DEPTH, D, DFF]), din("w1_ffn2", [DEPTH, D, DFF])]
    w3_d = [din("w3_ffn1", [DEPTH, D, DFF]), din("w3_ffn2", [DEPTH, D, DFF])]
    w2_d = [din("w2_ffn1", [DEPTH, DFF, D]), din("w2_ffn2", [DEPTH, DFF, D])]
    win_d = din("w_in", [DEPTH, D, 1536])
    wglu_d = din("w_glu", [DEPTH, 512, 512])
    wout_d = din("w_out", [DEPTH, D, D])
    wg_d = din("wgates", [DEPTH, 128, 8 * 128])
    bnat_d = din("bnat", [DEPTH, 2, 128, 16 * 128])
    cpad_d = din("cpad", [DEPTH, 128, 32 * 128])

    yp_d = dout("y_p", [TP, D])
    ys_d = dout("y_s", [NSAMP, D])
    oconv_d = dout("o_conv", [DEPTH, NSEQ, 3 * 512])
    oh_d = dout("o_h", [DEPTH, NSEQ, 512])
    osr_d = dout("o_sr", [DEPTH, NSEQ, 2048])
    osi_d = dout("o_si", [DEPTH, NSEQ, 2048])

    with ExitStack() as es:
        K = Sched(nc, es, waited)

        def sb(name, shape, dt=F32):
            return es.enter_context(nc.sbuf_tensor(name, list(shape), dt))

        XT = sb("XT", [128, KC * NTOK])
        RING = [sb("RING%d" % i, [128, 4096], BF16) for i in range(3)]
        WORK = sb("WORK", [128, 16320])
        AUX = sb("AUX", [128, 4128])
        WBU = sb("WBU", [128, 32 * 128], BF16)
        WC = sb("WC", [128, 32 * 128], BF16)
        WG = sb("WG", [128, 8 * 128], BF16)
        ABG = sb("ABG", [128, 9 * KC * NSEQ])
        PT = sb("PT", [128, PR_ROWS])
        IDF = sb("IDF", [128, 128])
        IDB = sb("IDB", [128, 128], BF16)
        ONESB = sb("ONESB", [128, 128], BF16)
        SILUC = sb("SILUC", [128, KC * NSEQ], BF16)
        FST = sb("FST", [128, 48 * NSEQ])
        SMALL = sb("SMALL", [128, 1700])
        TAU = sb("TAU", [128, NT5 + 1])
        PSB = [es.enter_context(nc.psum_tensor("PSB%d" % i, [128, 512], F32)) for i in range(8)]

        R = {}

        def res(name):
            if name not in R:
                R[name] = Res(name)
            return R[name]

        bankres = [res("bank%d" % i) for i in range(8)]
        ringres = [res("ring%d" % i) for i in range(3)]
        ringsem = [K.dsem() for _ in range(3)]
        ring_i = [0]
        ada_loaded = []
        role_banks = {"g1": [0, 1], "g3": [2, 3], "o": [4, 5], "m": [6, 7]}
        role_i = {k: 0 for k in role_banks}

        def bank(role):
            b = role_banks[role][role_i[role] % 2]
            role_i[role] += 1
            return b

        def ring_load(pairs_fn, ada=False):
            s = ring_i[0] % 3
            ring_i[0] += 1
            K.dma("pool", ringsem[s], pairs_fn(RING[s]), w=[ringres[s]])
            return s

        sm_off = [0]

        def small(n):
            o = sm_off[0]
            sm_off[0] += n
            assert sm_off[0] <= 1700
            return SMALL[:, o:o + n]

        A_RE = small(16); A_IM = small(16); DTX = small(16); DT = small(16); ARC = small(16)
        RHO = small(16); TH = small(16); TMPA = small(16); TMPB = small(16); TMPC = small(16); TMPD = small(16)
        F_R = small(16); F_I = small(16); ABR = small(16); ABI = small(16)
        SC1 = small(4); SC2 = small(4); SPT = small(4); SCH = small(4); HBG = small(8)
        HC = small(4)
        CR = small(16); CI = small(16)
        SLR = small(16); SLI = small(16)
        H0T = small(64)
        S0R = small(256); S0I = small(256)
        S0TR = small(256); S0TI = small(256)
        EPSC = small(1)
        HIST = small(12); CRPR = small(16); CRPI = small(16); ONEC = small(1); PIC = small(1)
        rs_small = res("small_s5par")

        def xv(c, col0, n):
            return XT[:, c * NTOK + col0: c * NTOK + col0 + n]

        def wbf(off_words, nelem):
            return WORK[:, off_words: off_words + (nelem + 1) // 2].bitcast(BF16)

        PROW = AUX[:, 1024:1024 + 384]
        C17 = AUX[0:NSEQ, 0:1024]
        H_B = wbf(0, 8 * 1088)
        A_B = wbf(4352, 22 * 1088)

        def hv(c, loc, n):
            return H_B[:, c * 1088 + loc: c * 1088 + loc + n]

        def av(f, loc, n):
            return A_B[:, f * 1088 + loc: f * 1088 + loc + n]

        def norm_tmps(tb):
            if tb < 0:
                return (AUX[:, 0:2048].bitcast(BF16), AUX[:, 2048:2560], AUX[:, 2560:3072],
                        [AUX[:, 3072:3584], AUX[:, 3584:4096]],
                        (rs_auxh[0:4], [rs_auxh[4], rs_auxh[5]], [rs_auxh[6], rs_auxh[7]]))
            return (wbf(tb, 8 * 512), WORK[:, tb + 2048: tb + 2048 + 512], WORK[:, tb + 2560: tb + 2560 + 512],
                    [WORK[:, tb + 3072 + i * 512: tb + 3072 + (i + 1) * 512] for i in range(2)],
                    (res("xsq"), res("rstd"), [res("nt0_0"), res("nt0_1")]))
        S32 = [AUX[:, 0:512], AUX[:, 1024:1536]]
        T64 = [AUX[:, 2048:2112], AUX[:, 3072:3136]]

        XBW = 1091
        XB = WORK[:, 4352: 4352 + 4 * XBW]
        GY_B = wbf(8716, 4 * 1088)
        U_B = wbf(10892, 4 * 1088)
        YL_B = wbf(13068, 4 * 1088)
        XBS = WORK[:, 15244: 15244 + 448]
        TAILX = WORK[:, 15244 + 448: 16320]

        def xbv(c, loc, n):
            return XB[:, c * XBW + loc: c * XBW + loc + n]

        def gyv(c, loc, n):
            return GY_B[:, c * 1088 + loc: c * 1088 + loc + n]

        def uv(c, loc, n):
            return U_B[:, c * 1088 + loc: c * 1088 + loc + n]

        def ylv(c, loc, n):
            return YL_B[:, c * 1088 + loc: c * 1088 + loc + n]

        COS = AUX[:, 0:16 * 129]
        SIN = AUX[:, 16 * 129: 32 * 129]

        def cosv(o0, no, t0, nt):
            return COS.rearrange("p (o t) -> p o t", o=16)[:, o0:o0 + no, t0:t0 + nt]

        def sinv(o0, no, t0, nt):
            return SIN.rearrange("p (o t) -> p o t", o=16)[:, o0:o0 + no, t0:t0 + nt]

        rs_auxh = [res("auxh%d" % i) for i in range(9)]
        rs_aux = [[rs_auxh[2 * i], rs_auxh[2 * i + 1]] for i in range(4)] + [[rs_auxh[8]]]
        rs_tab = [res("tables")] + rs_auxh

        def ptc(l, name, c, n=1):
            base = PR_LAYER * l + PR_OFF[name] + c
            return PT[:, base: base + n]

        rs_pt = res("pt")

        def abg(s, which, c, s0, ns):
            base = ((s * 3 + which) * KC + c) * NSEQ
            return ABG[:, base + s0: base + s0 + ns]

        rs_abg = res("abg")
        rs_modt = res("modt")

        tiles_all = [Tile(0, 512, "p", 0), Tile(512, 512, "p", 512),
                     Tile(1024, 512, "p", 0), Tile(1536, 512, "p", 512), Tile(2048, 64, "s", 1024)]
        groups = [tiles_all[0:2], tiles_all[2:5]]

        def xres(t, c):
            return res("x_%d_%d" % (t.col0, c))

        def bc_seq(ap2, n):
            return ap2.unsqueeze(2).to_broadcast([128, NSQ, TS])

        def v3(ap, a, b):
            return ap.rearrange("p (a b) -> p a b", a=a, b=b)

        rs_const = res("const")
        K.op("pool", lambda: nc.gpsimd.memset(IDF[:], 0.0), w=[rs_const])
        K.op("pool", lambda: nc.gpsimd.affine_select(out=IDF[:], in_=IDF[:], compare_op=ALU.not_equal, fill=1.0,
                                                     base=0, pattern=[[-1, 128]], channel_multiplier=1), r=[rs_const], w=[rs_const])
        K.op("pool", lambda: nc.gpsimd.memset(ONESB[:], 1.0), w=[rs_const])
        K.op("pool", lambda: nc.gpsimd.iota(TAU[:], pattern=[[1, NT5 + 1]], base=0, channel_multiplier=0,
                                            allow_small_or_imprecise_dtypes=True), w=[rs_const])
        K.op("dve", lambda: nc.vector.tensor_copy(out=IDB[:], in_=IDF[:]), r=[rs_const], w=[rs_const])
        K.op("dve", lambda: nc.vector.memset(EPSC, EPS), w=[rs_const])
        K.op("dve", lambda: nc.vector.memset(ONEC, 1.0), w=[rs_const])
        K.op("dve", lambda: nc.vector.memset(PIC, PI), w=[rs_const])

        ds_misc = K.dsem()
        rs_prow = res("prow")
        if SETUP_PARTS >= 2: K.dma("sp", ds_misc, [(PROW[:, i * 128:(i + 1) * 128], prow_d[i * 128:(i + 1) * 128, :]) for i in range(3)],
              w=[rs_prow, rs_aux[1]])
        for i in range(3 if SETUP_PARTS >= 2 else 0):
            b = bank("m")
            K.op("pe", lambda: nc.tensor.transpose(out=PSB[b][:, 0:128], in_=PROW[:, i * 128:(i + 1) * 128], identity=IDF[:]),
                 r=[rs_prow, rs_aux[1], rs_const], w=[bankres[b]])
            K.op("dve", lambda: nc.vector.tensor_copy(out=PT[:, i * 128:(i + 1) * 128], in_=PSB[b][:, 0:128]),
                 r=[bankres[b]], w=[rs_pt])

        ds_c = K.dsem()
        rs_c17 = res("c17")
        if SETUP_PARTS >= 3: K.dma("sp", ds_c, [(C17, c17_d[:, :])], w=[rs_c17, rs_aux[0]])
        rs_siluc = res("siluc")
        for c in range(KC if SETUP_PARTS >= 3 else 0):
            b = bank("m")
            K.op("pe", lambda: nc.tensor.transpose(out=PSB[b][:, 0:NSEQ], in_=C17[:, c * 128:(c + 1) * 128],
                                                   identity=IDF[0:NSEQ, 0:NSEQ]),
                 r=[rs_c17, rs_aux[0], rs_const], w=[bankres[b]])
            K.op("act", lambda: nc.scalar.activation(out=SILUC[:, c * NSEQ:(c + 1) * NSEQ], in_=PSB[b][:, 0:NSEQ], func=AF.Silu),
                 r=[bankres[b]], w=[rs_siluc])

        ds_x = [K.dsem() for _ in range(4)]
        n_xt = 0
        for (src, ntok, colbase) in (((xp_d, TP, 0), (xs_d, NSAMP, TP)) if SETUP_PARTS >= 4 else ()):
            for t0 in range(0, ntok, 128):
                if n_xt >= DBG_XT:
                    break
                n = min(128, ntok - t0)
                si = n_xt % 4
                n_xt += 1
                stg = AUX[:, si * 1024:(si + 1) * 1024]
                K.dma("sp", ds_x[si], [(stg[0:n, :], src[t0:t0 + n, :])], w=[rs_aux[si]])
                for half in range(2):
                    b = bank("m")
                    for cc in range(4):
                        c = half * 4 + cc
                        K.op("pe", lambda: nc.tensor.transpose(out=PSB[b][:, cc * 128: cc * 128 + n],
                                                               in_=stg[0:n, c * 128:(c + 1) * 128], identity=IDF[0:n, 0:n]),
                             r=[rs_aux[si], rs_const], w=[bankres[b]], inc=(cc == 3))
                    tl = tiles_all[min((colbase + t0) // 512, 4)]
                    for cc in range(4):
                        c = half * 4 + cc
                        eng = "dve" if (cc % 2 == 0 or DBG_XV == "dve") else "act"
                        if eng == "dve":
                            K.op("dve", lambda: nc.vector.tensor_copy(out=xv(c, colbase + t0, n), in_=PSB[b][:, cc * 128: cc * 128 + n]),
                                 r=[bankres[b]], w=[xres(tl, c)])
                        else:
                            K.op("act", lambda: nc.scalar.activation(out=xv(c, colbase + t0, n), in_=PSB[b][:, cc * 128: cc * 128 + n], func=AF.Identity),
                                 r=[bankres[b]], w=[xres(tl, c)])

        def norm_parts(t, l, s, gain_name, hdst, hres, tb=4352):
            n = t.n
            XSQ, RSTD, SQT, NT0, (rs_xsq, rs_rstd, rs_nt0) = norm_tmps(tb)
            st = {}

            def p1():
                for c in range(KC):
                    K.op("act", lambda: nc.scalar.activation(out=XSQ[:, c * 512: c * 512 + n], in_=xv(c, t.col0, n), func=AF.Square),
                         r=[xres(t, c)], w=[rs_xsq])

            def p2():
                b = bank("m")
                st["b"] = b
                for c in range(KC):
                    K.op("pe", lambda: nc.tensor.matmul(PSB[b][:, 0:n], lhsT=ONESB[:], rhs=XSQ[:, c * 512: c * 512 + n],
                                                        start=(c == 0), stop=(c == KC - 1)),
                         r=[rs_xsq, rs_const], w=[bankres[b]], inc=(c == KC - 1))
                K.op("act", lambda: nc.scalar.activation(out=SQT[:, 0:n], in_=PSB[b][:, 0:n], func=AF.Sqrt,
                                                         bias=EPSC, scale=1.0 / D),
                     r=[bankres[b], rs_const], w=[rs_rstd])

            def p3():
                K.op("dve", lambda: nc.vector.reciprocal(out=RSTD[:, 0:n], in_=SQT[:, 0:n]), r=[rs_rstd], w=[rs_rstd])
                for c in range(KC):
                    tmp = NT0[c % 2]
                    rtmp = rs_nt0[c % 2]
                    K.op("dve", lambda: nc.vector.tensor_tensor(out=tmp[:, 0:n], in0=xv(c, t.col0, n), in1=RSTD[:, 0:n], op=ALU.mult),
                         r=[xres(t, c), rs_rstd], w=[rtmp])
                    if t.kind == "p":
                        K.op("act", lambda: nc.scalar.activation(out=hdst(c), in_=tmp[:, 0:n], func=AF.Identity,
                                                                 scale=abg(s, 0, c, 0, 1), bias=abg(s, 1, c, 0, 1)),
                             r=[rtmp, rs_abg], w=[hres])
                    else:
                        K.op("dve", lambda: nc.vector.tensor_tensor(out=v3(tmp[:, 0:n], NSQ, TS), in0=v3(tmp[:, 0:n], NSQ, TS),
                                                                    in1=bc_seq(abg(s, 0, c, 1, NSQ), TS), op=ALU.mult),
                             r=[rs_abg, rtmp], w=[rtmp])
                        K.op("dve", lambda: nc.vector.tensor_tensor(out=v3(hdst(c), NSQ, TS), in0=v3(tmp[:, 0:n], NSQ, TS),
                                                                    in1=bc_seq(abg(s, 1, c, 1, NSQ), TS), op=ALU.add),
                             r=[rtmp, rs_abg], w=[hres])
            return [p1, p2, p3]

        def norm_mod(t, l, s, gain_name, hdst, hres, tb=4352):
            for pfn in norm_parts(t, l, s, gain_name, hdst, hres, tb):
                pfn()

        def norm_group(tiles, l, s, gain_name, tb):
            pp = [norm_parts(t, l, s, gain_name, lambda c, t=t: hv(c, t.loc, t.n), hres_of(t), tb) for t in tiles]
            nT = len(pp)
            pp[0][0](); pp[0][1]()
            for i in range(1, nT):
                pp[i][0]()
                pp[i - 1][2]()
                pp[i][1]()
            pp[nT - 1][2]()


        def residual(t, s, d, b):
            n = t.n
            if t.kind == "p":
                K.op("dve", lambda: nc.vector.scalar_tensor_tensor(out=xv(d, t.col0, n), in0=PSB[b][:, 0:n], scalar=abg(s, 2, d, 0, 1),
                                                                   in1=xv(d, t.col0, n), op0=ALU.mult, op1=ALU.add),
                     r=[bankres[b], rs_abg], w=[xres(t, d)])
            else:
                tmp = T64[d % 2]
                rtmp = rs_auxh[4 + 2 * (d % 2)]
                K.op("dve", lambda: nc.vector.tensor_tensor(out=v3(tmp[:, 0:n], NSQ, TS), in0=v3(PSB[b][:, 0:n], NSQ, TS),
                                                            in1=bc_seq(abg(s, 2, d, 1, NSQ), TS), op=ALU.mult),
                     r=[bankres[b], rs_abg], w=[rtmp])
                K.op("dve", lambda: nc.vector.tensor_tensor(out=xv(d, t.col0, n), in0=xv(d, t.col0, n), in1=tmp[:, 0:n], op=ALU.add),
                     r=[rtmp], w=[xres(t, d)])

        def hres_of(t):
            return res("h_%d" % t.loc)

        MTS = [small(4 * NSEQ), small(4 * NSEQ)]
        rs_mts = [res("ada_mt0"), res("ada_mt1")]
        ada_q = []
        ada_evac = []
        ada_n = [0]

        def ada_load(l, j):
            wsrc = wada_d[l].rearrange("(k p) f -> p k f", p=128)
            s_ = ring_load(lambda slot: [(slot[:, 0:4096].rearrange("p (k f) -> p k f", k=8), wsrc[:, :, j * 512:(j + 1) * 512])], ada=True)
            ada_loaded.append((l, j, s_))

        def ada_mm(l, j, s_):
            slot3 = RING[s_][:, 0:4096].rearrange("p (k f) -> p k f", k=8)
            b = bank("m")
            mi = ada_n[0] % 2
            ada_n[0] += 1
            for i in range(4):
                for k in range(KC):
                    K.op("pe", lambda: nc.tensor.matmul(PSB[b][:, i * 32: i * 32 + NSEQ], lhsT=slot3[:, k, i * 128:(i + 1) * 128],
                                                        rhs=SILUC[:, k * NSEQ:(k + 1) * NSEQ], start=(k == 0), stop=(k == KC - 1)),
                         r=[ringres[s_], rs_siluc], w=[bankres[b]], inc=(k == KC - 1 and i == 3))
            for i in range(4):
                oc = 4 * j + i
                K.op("act", lambda: nc.scalar.activation(out=MTS[mi][:, i * NSEQ:(i + 1) * NSEQ], in_=PSB[b][:, i * 32: i * 32 + NSEQ],
                                                         func=AF.Identity, bias=ptc(l, "b_ada", oc), scale=1.0),
                     r=[bankres[b], rs_pt], w=[rs_mts[mi]])
            ada_evac.append((l, j, mi))

        def ada_derive(l, j, mi):
            gains = ["norm_ffn1", "norm_mix", "norm_ffn2"]
            for i in range(4):
                oc = 4 * j + i
                m = oc // 8
                c = oc % 8
                sub, which = m // 3, m % 3
                src = MTS[mi][:, i * NSEQ:(i + 1) * NSEQ]
                if which == 0:
                    K.op("dve", lambda: nc.vector.tensor_copy(out=abg(sub, 1, c, 0, NSEQ), in_=src), r=[rs_mts[mi]], w=[rs_abg])
                elif which == 1:
                    K.op("dve", lambda: nc.vector.tensor_scalar(out=abg(sub, 0, c, 0, NSEQ), in0=src, scalar1=1.0,
                                                                scalar2=ptc(l, gains[sub], c), op0=ALU.add, op1=ALU.mult),
                         r=[rs_mts[mi], rs_pt], w=[rs_abg])
                else:
                    K.op("dve", lambda: nc.vector.tensor_scalar(out=abg(sub, 2, c, 0, NSEQ), in0=src, scalar1=(1.0 if sub == 1 else 0.5),
                                                                scalar2=None, op0=ALU.mult),
                         r=[rs_mts[mi]], w=[rs_abg])

        def ada_enqueue(l, j0, j1):
            for j in range(j0, j1):
                ada_q.append((l, j))

        def ada_pump(n=1):
            for _ in range(n):
                if ada_evac:
                    ada_derive(*ada_evac.pop(0))
                if ada_loaded:
                    ada_mm(*ada_loaded.pop(0))
                if ada_q:
                    ada_load(*ada_q.pop(0))

        def ada_flush():
            while ada_q or ada_loaded or ada_evac:
                ada_pump(1)

        def ffn(l, which, s, gain_name, group, do_norm=True, next_group=None):
            w1s = w1_d[which][l].rearrange("(k p) f -> p k f", p=128)
            w3s = w3_d[which][l].rearrange("(k p) f -> p k f", p=128)
            w2s = w2_d[which][l].rearrange("(k p) f -> p k f", p=128)
            if do_norm:
                K.barrier()
                norm_group(group, l, s, gain_name, 4352)
            def load_item(i):
                if i < 11:
                    return ring_load(lambda slot: [
                        (slot[:, 0:2048].rearrange("p (k f) -> p k f", k=8), w1s[:, :, i * 256:(i + 1) * 256]),
                        (slot[:, 2048:4096].rearrange("p (k f) -> p k f", k=8), w3s[:, :, i * 256:(i + 1) * 256])])
                d_ = i - 11
                return ring_load(lambda slot: [(slot[:, 0:2816].rearrange("p (k f) -> p k f", k=FC), w2s[:, :, d_ * 128:(d_ + 1) * 128])])
            slots_ = {0: load_item(0)}
            for j in range(11):
                slots_[j + 1] = load_item(j + 1)
                sl = slots_[j]
                s1 = RING[sl][:, 0:2048].rearrange("p (k f) -> p k f", k=8)
                s3 = RING[sl][:, 2048:4096].rearrange("p (k f) -> p k f", k=8)
                for t in group:
                    n = t.n
                    for f2 in range(2):
                        f = 2 * j + f2
                        b1 = bank("g1"); b3 = bank("g3")
                        for k in range(KC):
                            K.op("pe", lambda: nc.tensor.matmul(PSB[b1][:, 0:n], lhsT=s1[:, k, f2 * 128:(f2 + 1) * 128], rhs=hv(k, t.loc, n),
                                                                start=(k == 0), stop=(k == KC - 1)),
                                 r=[ringres[sl], hres_of(t)], w=[bankres[b1]], inc=(k == KC - 1))
                        for k in range(KC):
                            K.op("pe", lambda: nc.tensor.matmul(PSB[b3][:, 0:n], lhsT=s3[:, k, f2 * 128:(f2 + 1) * 128], rhs=hv(k, t.loc, n),
                                                                start=(k == 0), stop=(k == KC - 1)),
                                 r=[ringres[sl], hres_of(t)], w=[bankres[b3]], inc=(k == KC - 1))
                        si = role_i["g1"] % 2
                        stmp = S32[si]; rst = rs_auxh[2 * si]
                        K.op("act", lambda: nc.scalar.activation(out=stmp[:, 0:n], in_=PSB[b1][:, 0:n], func=AF.Silu),
                             r=[bankres[b1]], w=[rst])
                        K.op("dve", lambda: nc.vector.tensor_tensor(out=av(f, t.loc, n), in0=stmp[:, 0:n], in1=PSB[b3][:, 0:n], op=ALU.mult),
                             r=[rst, bankres[b3]], w=[res("a_%d_%d" % (f, t.loc))])
            for d in range(KC):
                if d + 1 < KC:
                    slots_[11 + d + 1] = load_item(11 + d + 1)
                if next_group is not None:
                    if d == 0:
                        hoist = []
                        for ti_, t in enumerate(next_group):
                            pp = norm_parts(t, l, s, gain_name, lambda c, t=t: hv(c, t.loc, t.n), hres_of(t), tb=-1)
                            for pi2, pfn in enumerate(pp):
                                hoist.append((2 * ti_ + pi2, pfn))
                    for (dd, pfn) in hoist:
                        if dd == d:
                            pfn()
                sl = slots_[11 + d]
                s2 = RING[sl][:, 0:2816].rearrange("p (k f) -> p k f", k=FC)
                for t in group:
                    n = t.n
                    b = bank("o")
                    for f in range(FC):
                        K.op("pe", lambda: nc.tensor.matmul(PSB[b][:, 0:n], lhsT=s2[:, f, :], rhs=av(f, t.loc, n),
                                                            start=(f == 0), stop=(f == FC - 1)),
                             r=[ringres[sl], res("a_%d_%d" % (f, t.loc))], w=[bankres[b]], inc=(f == FC - 1))
                    residual(t, s, d, b)

        def mod_2pi(x, m, kmax, rt):
            k = kmax
            while k >= 1:
                sk = float(k) * TWO_PI
                K.op("dve", lambda: nc.vector.tensor_scalar(out=m, in0=x, scalar1=sk, scalar2=None, op0=ALU.is_ge), r=[rt], w=[rt])
                K.op("dve", lambda: nc.vector.scalar_tensor_tensor(out=x, in0=m, scalar=-sk, in1=x, op0=ALU.mult, op1=ALU.add), r=[rt], w=[rt])
                k //= 2

        def sin_reduced(dst, src, n, tmp1, tmp2i, tmp3, rr, rw, add_half_pi=False):
            rt = res("sinred_tmp")
            K.op("dve", lambda: nc.vector.tensor_scalar(out=tmp1, in0=src, scalar1=(PI / 2 if add_half_pi else 0.0), scalar2=None, op0=ALU.add),
                 r=rr, w=[rt])
            mod_2pi(tmp1, tmp3, 128, rt)
            K.op("act", lambda: nc.scalar.activation(out=dst, in_=tmp1, func=AF.Sin, scale=-1.0, bias=PIC), r=[rt, rs_const], w=rw)


        def cmul(dr, di, ar, ai, br, bi, t1, t2, rr, rw, rtmp):
            K.op("dve", lambda: nc.vector.tensor_tensor(out=t1, in0=ar, in1=br, op=ALU.mult), r=rr, w=[rtmp])
            K.op("dve", lambda: nc.vector.tensor_tensor(out=t2, in0=ai, in1=bi, op=ALU.mult), r=rr, w=[rtmp])
            K.op("dve", lambda: nc.vector.tensor_tensor(out=dr, in0=t1, in1=t2, op=ALU.subtract), r=[rtmp], w=rw)
            K.op("dve", lambda: nc.vector.tensor_tensor(out=t1, in0=ar, in1=bi, op=ALU.mult), r=rr, w=[rtmp])
            K.op("dve", lambda: nc.vector.tensor_tensor(out=t2, in0=ai, in1=br, op=ALU.mult), r=rr, w=[rtmp])
            K.op("dve", lambda: nc.vector.tensor_tensor(out=di, in0=t1, in1=t2, op=ALU.add), r=[rtmp], w=rw)

        rs_wbu = res("wbu"); rs_wc = res("wc"); rs_wg = res("wg")
        ds_w = K.dsem(); ds_w2 = K.dsem(); ds_st = K.dsem(); ds_bn = K.dsem()
        rs_fst = res("fst")
        rs_xbs = res("xbs"); rs_h0t = res("h0t"); rs_s0 = res("s0")
        rs_hc = res("hc"); rs_carry = res("carry")
        rs_work_setup = res("work_setup")

        def mixer_setup(l):
            K.barrier()
            K.dma("pool", ds_w, [(WG[:], wg_d[l])], w=[rs_wg])
            K.dma("pool", ds_w2, [(WC[:], cpad_d[l])], w=[rs_wc])
            K.op("dve", lambda: nc.vector.tensor_copy(out=A_RE, in_=ptc(l, "s5_a_re", 0, 16)), r=[rs_pt], w=[rs_small])
            K.op("dve", lambda: nc.vector.tensor_copy(out=A_IM, in_=ptc(l, "s5_a_im", 0, 16)), r=[rs_pt], w=[rs_small])
            K.dma("sp", ds_misc, [(DTX, dtx_d[l])], w=[rs_small])
            K.op("act", lambda: nc.scalar.activation(out=DT, in_=DTX, func=AF.Exp), r=[rs_small], w=[rs_small])
            K.op("dve", lambda: nc.vector.tensor_scalar(out=ARC, in0=A_RE, scalar1=-1e-4, scalar2=None, op0=ALU.min), r=[rs_small], w=[rs_small])
            K.op("dve", lambda: nc.vector.tensor_tensor(out=TMPA, in0=ARC, in1=DT, op=ALU.mult), r=[rs_small], w=[rs_small])
            K.op("act", lambda: nc.scalar.activation(out=RHO, in_=TMPA, func=AF.Exp), r=[rs_small], w=[rs_small])
            K.op("dve", lambda: nc.vector.tensor_tensor(out=TH, in0=A_IM, in1=DT, op=ALU.mult), r=[rs_small], w=[rs_small])
            K.op("dve", lambda: nc.vector.tensor_scalar(out=TH, in0=TH, scalar1=8.0 * TWO_PI, scalar2=None, op0=ALU.add), r=[rs_small], w=[rs_small])
            mod_2pi(TH, TMPA, 8, rs_small)
            PHI = WORK[:, 0:16 * 129]
            T1 = WORK[:, 2064:2064 + 2064]
            T2 = WORK[:, 4128:4128 + 2064]
            T3 = WORK[:, 6192:6192 + 2064]
            K.op("dve", lambda: nc.vector.tensor_tensor(out=v3(PHI, 16, 129), in0=TH.unsqueeze(2).to_broadcast([128, 16, 129]),
                                                        in1=TAU[:, :].unsqueeze(1).to_broadcast([128, 16, 129]), op=ALU.mult),
                 r=[rs_small, rs_const], w=[rs_work_setup])
            rt_ = res("sinred_tmp")
            K.op("dve", lambda: nc.vector.tensor_copy(out=T1, in_=PHI), r=[rs_work_setup], w=[rt_])
            mod_2pi(T1, T3, 64, rt_)
            K.op("act", lambda: nc.scalar.activation(out=SIN, in_=T1, func=AF.Sin, scale=-1.0, bias=PIC), r=[rt_, rs_const], w=[rs_tab])
            K.op("dve", lambda: nc.vector.tensor_scalar(out=T2, in0=T1, scalar1=PI / 2, scalar2=None, op0=ALU.add), r=[rt_], w=[rt_])
            mod_2pi(T2, T3, 1, rt_)
            K.op("act", lambda: nc.scalar.activation(out=COS, in_=T2, func=AF.Sin, scale=-1.0, bias=PIC), r=[rt_, rs_const], w=[rs_tab])
            c1 = cosv(0, 16, 1, 1).rearrange("p o t -> p (o t)")
            s1 = sinv(0, 16, 1, 1).rearrange("p o t -> p (o t)")
            K.op("dve", lambda: nc.vector.tensor_tensor(out=ABR, in0=RHO, in1=c1, op=ALU.mult), r=[rs_small, rs_tab], w=[rs_small])
            K.op("dve", lambda: nc.vector.tensor_tensor(out=ABI, in0=RHO, in1=s1, op=ALU.mult), r=[rs_small, rs_tab], w=[rs_small])
            K.op("dve", lambda: nc.vector.tensor_tensor(out=TMPA, in0=ARC, in1=ARC, op=ALU.mult), r=[rs_small], w=[rs_small])
            K.op("dve", lambda: nc.vector.tensor_tensor(out=TMPB, in0=A_IM, in1=A_IM, op=ALU.mult), r=[rs_small], w=[rs_small])
            K.op("dve", lambda: nc.vector.tensor_tensor(out=TMPA, in0=TMPA, in1=TMPB, op=ALU.add), r=[rs_small], w=[rs_small])
            K.op("dve", lambda: nc.vector.reciprocal(out=TMPA, in_=TMPA), r=[rs_small], w=[rs_small])
            K.op("dve", lambda: nc.vector.tensor_scalar(out=TMPB, in0=ABR, scalar1=-1.0, scalar2=None, op0=ALU.add), r=[rs_small], w=[rs_small])
            K.op("dve", lambda: nc.vector.tensor_tensor(out=TMPC, in0=TMPB, in1=ARC, op=ALU.mult), r=[rs_small], w=[rs_small])
            K.op("dve", lambda: nc.vector.tensor_tensor(out=TMPD, in0=ABI, in1=A_IM, op=ALU.mult), r=[rs_small], w=[rs_small])
            K.op("dve", lambda: nc.vector.tensor_tensor(out=TMPC, in0=TMPC, in1=TMPD, op=ALU.add), r=[rs_small], w=[rs_small])
            K.op("dve", lambda: nc.vector.tensor_tensor(out=F_R, in0=TMPC, in1=TMPA, op=ALU.mult), r=[rs_small], w=[rs_small])
            K.op("dve", lambda: nc.vector.tensor_tensor(out=TMPC, in0=ABI, in1=ARC, op=ALU.mult), r=[rs_small], w=[rs_small])
            K.op("dve", lambda: nc.vector.tensor_tensor(out=TMPD, in0=TMPB, in1=A_IM, op=ALU.mult), r=[rs_small], w=[rs_small])
            K.op("dve", lambda: nc.vector.tensor_tensor(out=TMPC, in0=TMPC, in1=TMPD, op=ALU.subtract), r=[rs_small], w=[rs_small])
            K.op("dve", lambda: nc.vector.tensor_tensor(out=F_I, in0=TMPC, in1=TMPA, op=ALU.mult), r=[rs_small], w=[rs_small])
            BNR = WORK[:, 8256: 8256 + 2048]
            BNI = WORK[:, 10304: 10304 + 2048]
            U1 = WORK[:, 12352: 12352 + 2048]
            U2 = WORK[:, 14400: 14400 + 1920]
            rs_bn = res("bnat")
            K.dma("sp", ds_bn, [(BNR, bnat_d[l, 0]), (BNI, bnat_d[l, 1])], w=[rs_bn])
            BBR = wbf(0, 2048 * 1)
            BBI = wbf(1024, 2048 * 1)
            rs_bb = res("bb")
            frb = F_R.unsqueeze(2).to_broadcast([128, 16, 128])
            fib = F_I.unsqueeze(2).to_broadcast([128, 16, 128])
            for hh in range(2):
                o0 = hh * 8
                sl_ = slice(o0 * 128, (o0 + 8) * 128)
                fr_h = F_R[:, o0:o0 + 8].unsqueeze(2).to_broadcast([128, 8, 128])
                fi_h = F_I[:, o0:o0 + 8].unsqueeze(2).to_broadcast([128, 8, 128])
                u1 = v3(U1[:, 0:1024], 8, 128); u2 = v3(U2[:, 0:1024], 8, 128)
                bnr = v3(BNR[:, sl_], 8, 128); bni = v3(BNI[:, sl_], 8, 128)
                rtu = res("bb_tmp")
                K.op("dve", lambda: nc.vector.tensor_tensor(out=u1, in0=bnr, in1=fr_h, op=ALU.mult), r=[rs_bn, rs_small, rs_tab], w=[rtu])
                K.op("dve", lambda: nc.vector.tensor_tensor(out=u2, in0=bni, in1=fi_h, op=ALU.mult), r=[rs_bn, rs_small], w=[rtu])
                K.op("dve", lambda: nc.vector.tensor_tensor(out=v3(BBR[:, sl_], 8, 128), in0=u1, in1=u2, op=ALU.subtract), r=[rtu], w=[rs_bb])
                K.op("dve", lambda: nc.vector.tensor_tensor(out=u1, in0=bnr, in1=fi_h, op=ALU.mult), r=[rs_bn, rs_small], w=[rtu])
                K.op("dve", lambda: nc.vector.tensor_tensor(out=u2, in0=bni, in1=fr_h, op=ALU.mult), r=[rs_bn, rs_small], w=[rtu])
                K.op("dve", lambda: nc.vector.tensor_tensor(out=v3(BBI[:, sl_], 8, 128), in0=u1, in1=u2, op=ALU.add), r=[rtu], w=[rs_bb])
            for comp, BB in ((0, BBR), (1, BBI)):
                for o8 in range(2):
                    b = bank("m")
                    pb = PSB[b][:, :].bitcast(BF16)
                    for oi in range(8):
                        o = o8 * 8 + oi
                        K.op("pe", lambda: nc.tensor.transpose(out=pb[:, oi * 128:(oi + 1) * 128], in_=BB[:, o * 128:(o + 1) * 128], identity=IDB[:]),
                             r=[rs_bb, rs_const], w=[bankres[b]], inc=(oi == 7))
                    base = (comp * 16 + o8 * 8) * 128
                    K.op("dve", lambda: nc.vector.tensor_copy(out=WBU[:, base: base + 1024], in_=pb[:, 0:1024]), r=[bankres[b]], w=[rs_wbu])
            K.op("act", lambda: nc.scalar.activation(out=SPT, in_=ptc(l, "lru_lambda", 0, 4), func=AF.Exp, scale=-1.0), r=[rs_pt], w=[rs_small])
            K.op("dve", lambda: nc.vector.tensor_scalar(out=SPT, in0=SPT, scalar1=1.0, scalar2=None, op0=ALU.add), r=[rs_small], w=[rs_small])
            K.op("act", lambda: nc.scalar.activation(out=SPT, in_=SPT, func=AF.Ln), r=[rs_small], w=[rs_small])
            K.op("dve", lambda: nc.vector.tensor_scalar(out=SC1, in0=SPT, scalar1=-8.0, scalar2=None, op0=ALU.mult), r=[rs_small], w=[rs_small])
            K.op("dve", lambda: nc.vector.tensor_scalar(out=SC2, in0=SPT, scalar1=-16.0, scalar2=None, op0=ALU.mult), r=[rs_small], w=[rs_small])
            K.op("dve", lambda: nc.vector.tensor_scalar(out=SCH, in0=SPT, scalar1=-4.0, scalar2=None, op0=ALU.mult), r=[rs_small], w=[rs_small])
            K.op("dve", lambda: nc.vector.tensor_scalar(out=HBG[:, 0:4], in0=ptc(l, "b_rg", 0, 4), scalar1=0.5, scalar2=None, op0=ALU.mult), r=[rs_pt], w=[rs_small])
            K.op("dve", lambda: nc.vector.tensor_scalar(out=HBG[:, 4:8], in0=ptc(l, "b_ig", 0, 4), scalar1=0.5, scalar2=None, op0=ALU.mult), r=[rs_pt], w=[rs_small])
            STG = WORK[0:NSQ, 8256: 8256 + 2048]
            xbs4 = XBS.rearrange("p (c s k) -> p c s k", c=4, s=NSQ, k=7)
            K.dma("sp", ds_st, [(STG[:, 0:1536], stc_d[l])], r=[], w=[rs_bn])
            for k3 in range(3):
                b = bank("m")
                for c in range(4):
                    K.op("pe", lambda: nc.tensor.transpose(out=PSB[b][:, c * 16:(c + 1) * 16], in_=STG[:, k3 * 512 + c * 128: k3 * 512 + (c + 1) * 128],
                                                           identity=IDF[0:NSQ, 0:NSQ]),
                         r=[rs_bn, rs_const], w=[bankres[b]], inc=(c == 3))
                K.op("dve", lambda: nc.vector.tensor_copy(out=xbs4[:, :, :, k3], in_=v3(PSB[b][:, 0:64], 4, 16)), r=[bankres[b]], w=[rs_xbs])
            K.dma("sp", ds_st, [(STG[:, 0:512], sth_d[l])], w=[rs_bn])
            b = bank("m")
            for c in range(4):
                K.op("pe", lambda: nc.tensor.transpose(out=PSB[b][:, c * 16:(c + 1) * 16], in_=STG[:, c * 128:(c + 1) * 128], identity=IDF[0:NSQ, 0:NSQ]),
                     r=[rs_bn, rs_const], w=[bankres[b]], inc=(c == 3))
            K.op("dve", lambda: nc.vector.tensor_copy(out=H0T, in_=PSB[b][:, 0:64]), r=[bankres[b]], w=[rs_h0t])
            for (srcd, dst) in ((str_d, S0R), (sti_d, S0I)):
                K.dma("sp", ds_st, [(STG[:, 0:2048], srcd[l])], w=[rs_bn])
                b = bank("m")
                for o in range(16):
                    K.op("pe", lambda: nc.tensor.transpose(out=PSB[b][:, o * 16:(o + 1) * 16], in_=STG[:, o * 128:(o + 1) * 128], identity=IDF[0:NSQ, 0:NSQ]),
                         r=[rs_bn, rs_const], w=[bankres[b]], inc=(o == 15))
                K.op("dve", lambda: nc.vector.tensor_copy(out=dst, in_=PSB[b][:, 0:256]), r=[bankres[b]], w=[rs_s0])
            c1b = cosv(0, 16, 1, 1).to_broadcast([128, 16, NSQ])
            s1b = sinv(0, 16, 1, 1).to_broadcast([128, 16, NSQ])
            t1 = v3(TAILX[:, 0:256], 16, NSQ); t2 = v3(TAILX[:, 256:512], 16, NSQ)
            cmul(v3(S0TR, 16, NSQ), v3(S0TI, 16, NSQ), v3(S0R, 16, NSQ), v3(S0I, 16, NSQ), c1b, s1b, t1, t2,
                 [rs_s0, rs_tab], [rs_s0], res("tailx"))
            K.op("dve", lambda: nc.vector.memset(HC, 0.0), w=[rs_hc])
            K.op("dve", lambda: nc.vector.memset(CR, 0.0), w=[rs_carry])
            K.op("dve", lambda: nc.vector.memset(CI, 0.0), w=[rs_carry])

        def mixer_group(l, gi, group):
            s = 1
            K.barrier()
            wins = win_d[l].rearrange("(k p) f -> p k f", p=128)
            norm_group(group, l, s, "norm_mix", 8716)
            xbs4 = XBS.rearrange("p (c s k) -> p c s k", c=4, s=NSQ, k=7)
            for part in range(3):
                sl = ring_load(lambda slot: [(slot[:, 0:4096].rearrange("p (k f) -> p k f", k=8), wins[:, :, part * 512:(part + 1) * 512])])
                s3 = RING[sl][:, 0:4096].rearrange("p (k f) -> p k f", k=8)
                for t in group:
                    n = t.n
                    for oc in range(4):
                        role = ("g1", "g3", "o")[(oc + part) % 3]
                        b = bank(role)
                        for k in range(KC):
                            K.op("pe", lambda: nc.tensor.matmul(PSB[b][:, 0:n], lhsT=s3[:, k, oc * 128:(oc + 1) * 128], rhs=hv(k, t.loc, n),
                                                                start=(k == 0), stop=(k == KC - 1)),
                                 r=[ringres[sl], hres_of(t)], w=[bankres[b]], inc=(k == KC - 1))
                        if part == 0:
                            if t.kind == "p":
                                K.op("dve", lambda: nc.vector.tensor_copy(out=xbv(oc, 3 + t.loc, n), in_=PSB[b][:, 0:n]),
                                     r=[bankres[b]], w=[res("xb_%d" % t.loc)])
                            else:
                                K.op("dve", lambda: nc.vector.tensor_copy(out=xbs4[:, oc, :, 3:7], in_=v3(PSB[b][:, 0:n], NSQ, TS)),
                                     r=[bankres[b]], w=[rs_xbs])
                        elif part == 1:
                            K.op("act", lambda: nc.scalar.activation(out=gyv(oc, t.loc, n), in_=PSB[b][:, 0:n], func=AF.Gelu_apprx_tanh),
                                 r=[bankres[b]], w=[res("gy_%d" % t.loc)])
                        else:
                            K.op("act", lambda: nc.scalar.activation(out=uv(oc, t.loc, n), in_=PSB[b][:, 0:n], func=AF.Identity),
                                 r=[bankres[b]], w=[res("u_%d" % t.loc)])
            K.barrier()
            if gi == 0:
                K.op("dve", lambda: nc.vector.memset(XB.rearrange("p (c w) -> p c w", c=4)[:, :, 0:3], 0.0), w=[res("xb_hist")])
            else:
                K.op("dve", lambda: nc.vector.tensor_copy(out=XB.rearrange("p (c w) -> p c w", c=4)[:, :, 0:3], in_=v3(HIST, 4, 3)),
                     r=[res("hist_save")], w=[res("xb_hist")])
            NL = 256
            LSETS = []
            for si_ in range(4):
                base = si_ * 1024
                xcbw = TAILX[:, si_ * 128:(si_ + 1) * 128].bitcast(BF16)
                LSETS.append(([WORK[:, base + i * NL: base + (i + 1) * NL] for i in range(4)], xcbw, res("lru_set%d" % si_)))
            lunits = []
            for t in group:
                subs = [(t.loc + i * NL, NL) for i in range(t.n // NL)] if t.kind == "p" else [(t.loc, t.n)]
                for (sloc, n) in subs:
                    for c in range(4):
                        lunits.append((t, sloc, n, c))
            batches = [lunits[i:i + 2] for i in range(0, len(lunits), 2)]

            def lru_A(k):
                for bi_, (t, sloc, n, c) in enumerate(batches[k]):
                    bufs, xcbf, rlt = LSETS[2 * (k % 2) + bi_]
                    XC, RG, IG, AA = [x[:, 0:n] for x in bufs]
                    xcb = xcbf[:, 0:n]
                    if t.kind == "p":
                        srcs = [xbv(c, sloc + kk, n) for kk in range(4)]
                        rsrc = [res("xb_%d" % t.loc)]
                        if sloc == t.loc:
                            rsrc.append(res("xb_%d" % (t.loc - 512)) if t.loc > 0 else res("xb_hist"))
                        shp = lambda a: a
                    else:
                        srcs = [xbs4[:, c, :, kk:kk + 4] for kk in range(4)]
                        rsrc = [rs_xbs]
                        shp = lambda a: v3(a, NSQ, TS)
                    K.op("dve", lambda: nc.vector.tensor_scalar(out=shp(XC), in0=srcs[3], scalar1=ptc(l, "conv_w", 3 * 4 + c),
                                                                scalar2=ptc(l, "conv_b", c), op0=ALU.mult, op1=ALU.add),
                         r=rsrc + [rs_pt], w=[rlt])
                    for kk in range(3):
                        K.op("dve", lambda: nc.vector.scalar_tensor_tensor(out=shp(XC), in0=srcs[kk], scalar=ptc(l, "conv_w", kk * 4 + c),
                                                                           in1=shp(XC), op0=ALU.mult, op1=ALU.add),
                             r=rsrc + [rs_pt, rlt], w=[rlt])
                    K.op("dve", lambda: nc.vector.tensor_copy(out=xcb, in_=XC), r=[rlt], w=[rlt])
                    b1 = bank("g1"); b3 = bank("g3")
                    K.op("pe", lambda: nc.tensor.matmul(PSB[b1][:, 0:n], lhsT=WG[:, c * 128:(c + 1) * 128], rhs=xcb, start=True, stop=True),
                         r=[rlt, rs_wg], w=[bankres[b1]])
                    K.op("pe", lambda: nc.tensor.matmul(PSB[b3][:, 0:n], lhsT=WG[:, (4 + c) * 128:(5 + c) * 128], rhs=xcb, start=True, stop=True),
                         r=[rlt, rs_wg], w=[bankres[b3]])
                    K.op("act", lambda: nc.scalar.activation(out=RG, in_=PSB[b1][:, 0:n], func=AF.Tanh, bias=HBG[:, c:c + 1], scale=0.5),
                         r=[bankres[b1], rs_small], w=[rlt])
                    K.op("act", lambda: nc.scalar.activation(out=IG, in_=PSB[b3][:, 0:n], func=AF.Tanh, bias=HBG[:, 4 + c:5 + c], scale=0.5),
                         r=[bankres[b3], rs_small], w=[rlt])
                    K.op("act", lambda: nc.scalar.activation(out=AA, in_=RG, func=AF.Exp, scale=SCH[:, c:c + 1], bias=SCH[:, c:c + 1]),
                         r=[rlt, rs_small], w=[rlt])
                    K.op("act", lambda: nc.scalar.activation(out=RG, in_=RG, func=AF.Exp, scale=SC1[:, c:c + 1], bias=SC1[:, c:c + 1]),
                         r=[rlt, rs_small], w=[rlt])

            def lru_B(k):
                for bi_, (t, sloc, n, c) in enumerate(batches[k]):
                    bufs, xcbf, rlt = LSETS[2 * (k % 2) + bi_]
                    MM = bufs[1][:, 0:n]
                    K.op("act", lambda: nc.scalar.activation(out=MM, in_=MM, func=AF.Sqrt, scale=-1.0, bias=ONEC), r=[rlt, rs_const], w=[rlt])

            def lru_C(k):
                for bi_, (t, sloc, n, c) in enumerate(batches[k]):
                    bufs, xcbf, rlt = LSETS[2 * (k % 2) + bi_]
                    XC, MM, IG, AA = [x[:, 0:n] for x in bufs]
                    BBt = IG; HS = MM
                    K.op("dve", lambda: nc.vector.tensor_scalar(out=IG, in0=IG, scalar1=0.5, scalar2=0.5, op0=ALU.mult, op1=ALU.add), r=[rlt], w=[rlt])
                    K.op("dve", lambda: nc.vector.tensor_tensor(out=BBt, in0=MM, in1=IG, op=ALU.mult), r=[rlt], w=[rlt])
                    K.op("dve", lambda: nc.vector.tensor_tensor(out=BBt, in0=BBt, in1=XC, op=ALU.mult), r=[rlt], w=[rlt])
                    if t.kind == "p":
                        K.op("dve", lambda: nc.vector.tensor_tensor_scan(out=HS, data0=AA, data1=BBt, initial=HC[:, c:c + 1],
                                                                         op0=ALU.mult, op1=ALU.add), r=[rlt, rs_hc], w=[rlt])
                        K.op("dve", lambda: nc.vector.tensor_copy(out=HC[:, c:c + 1], in_=HS[:, n - 1:n]), r=[rlt], w=[rs_hc])
                        if t.col0 + (sloc - t.loc) + n == TP:
                            K.op("dve", lambda: nc.vector.tensor_copy(out=FST[:, (12 + c) * NSEQ:(12 + c) * NSEQ + 1], in_=HS[:, n - 1:n]),
                                 r=[rlt], w=[rs_fst])
                            for kk in range(3):
                                K.op("dve", lambda: nc.vector.tensor_copy(out=FST[:, (c * 3 + kk) * NSEQ:(c * 3 + kk) * NSEQ + 1],
                                                                          in_=xbv(c, 3 + sloc + n - 3 + kk, 1)),
                                     r=[res("xb_%d" % t.loc)], w=[rs_fst])
                    else:
                        hs3 = v3(HS, NSQ, TS); aa3 = v3(AA, NSQ, TS); bb3 = v3(BBt, NSQ, TS)
                        for tt in range(TS):
                            prev = H0T[:, c * NSQ:(c + 1) * NSQ] if tt == 0 else hs3[:, :, tt - 1]
                            K.op("dve", lambda: nc.vector.tensor_tensor(out=hs3[:, :, tt], in0=aa3[:, :, tt], in1=prev, op=ALU.mult),
                                 r=[rlt, rs_h0t], w=[rlt])
                            K.op("dve", lambda: nc.vector.tensor_tensor(out=hs3[:, :, tt], in0=hs3[:, :, tt], in1=bb3[:, :, tt], op=ALU.add),
                                 r=[rlt], w=[rlt])
                        K.op("dve", lambda: nc.vector.tensor_copy(out=FST[:, (12 + c) * NSEQ + 1:(12 + c + 1) * NSEQ], in_=hs3[:, :, TS - 1]),
                             r=[rlt], w=[rs_fst])
                        for kk in range(3):
                            K.op("dve", lambda: nc.vector.tensor_copy(out=FST[:, (c * 3 + kk) * NSEQ + 1:(c * 3 + kk + 1) * NSEQ],
                                                                      in_=xbs4[:, c, :, 4 + kk]),
                                 r=[rs_xbs], w=[rs_fst])
                    K.op("dve", lambda: nc.vector.tensor_tensor(out=ylv(c, sloc, n), in0=gyv(c, sloc, n), in1=HS, op=ALU.mult),
                         r=[rlt, res("gy_%d" % t.loc)], w=[res("yl_%d" % t.loc)])

            lru_A(0)
            for k in range(len(batches)):
                lru_B(k)
                if k + 1 < len(batches):
                    lru_A(k + 1)
                lru_C(k)
            if gi == 0:
                xb3 = XB.rearrange("p (c w) -> p c w", c=4)
                K.op("dve", lambda: nc.vector.tensor_copy(out=v3(HIST, 4, 3), in_=xb3[:, :, 1024:1027]),
                     r=[res("xb_512")], w=[res("hist_save")])
            K.barrier()
            Tt = [WORK[:, i * 512:(i + 1) * 512] for i in range(4)]
            BTRf = WORK[:, 2048:2560]; BTIf = WORK[:, 2560:3072]; STRf = WORK[:, 3072:3584]; STIf = WORK[:, 3584:4096]
            PREf = WORK[:, 4096:4224]
            RHOT = WORK[:, 4352:6400]
            SRBs = [wbf(6400, 512), wbf(6656, 512)]
            NSIBs = [wbf(6912, 512), wbf(7168, 512)]
            rT = [res("s5_t%d" % i) for i in range(4)]
            r_btr = res("s5_btr"); r_bti = res("s5_bti"); r_str = res("s5_str"); r_sti = res("s5_sti")
            r_bt = [r_btr, r_bti]; r_st = [r_str, r_sti]; rpre = res("s5_pre"); r_rhot = res("s5_rhot")
            r_slr = res("s5_slr"); r_sli = res("s5_sli")
            r_srb = [res("s5_srb0"), res("s5_srb1")]
            r_crp = res("s5_crp")
            K.op("dve", lambda: nc.vector.tensor_copy(out=v3(RHOT, 16, NT5), in_=RHO.unsqueeze(2).to_broadcast([128, 16, NT5])),
                 r=[rs_small], w=[r_rhot])
            K.op("dve", lambda: nc.vector.memset(v3(RHOT, 16, NT5)[:, :, 0:1], 0.0), w=[r_rhot])
            units = []
            for t in group:
                if t.kind == "p":
                    for i in range(t.n // NT5):
                        for q in range(4):
                            units.append((t, t.loc + i * NT5, q))
            n = NT5
            ubanks = {}

            def emit_bu(ui):
                t, sloc, q = units[ui]
                br_b = bank("g1"); bi_b = bank("g3")
                ubanks[ui] = (br_b, bi_b)
                for oi in range(4):
                    o = 4 * q + oi
                    K.op("pe", lambda: nc.tensor.matmul(PSB[br_b][:, oi * n:(oi + 1) * n], lhsT=WBU[:, o * 128:(o + 1) * 128], rhs=uv(q, sloc, n),
                                                        start=True, stop=True),
                         r=[rs_wbu, res("u_%d" % t.loc)], w=[bankres[br_b]], inc=(oi == 3))
                for oi in range(4):
                    o = 4 * q + oi
                    K.op("pe", lambda: nc.tensor.matmul(PSB[bi_b][:, oi * n:(oi + 1) * n], lhsT=WBU[:, (16 + o) * 128:(17 + o) * 128], rhs=uv(q, sloc, n),
                                                        start=True, stop=True),
                         r=[rs_wbu, res("u_%d" % t.loc)], w=[bankres[bi_b]], inc=(oi == 3))

            pending_post = [None]

            def emit_post():
                if pending_post[0] is None:
                    return
                (t, sloc, q, yb_) = pending_post[0]
                pending_post[0] = None
                K.op("dve", lambda: nc.vector.scalar_tensor_tensor(out=PREf[:, 0:n], in0=uv(q, sloc, n), scalar=ptc(l, "s5_d", q),
                                                                   in1=PSB[yb_][:, 0:n], op0=ALU.mult, op1=ALU.add),
                     r=[bankres[yb_], res("u_%d" % t.loc), rs_pt], w=[rpre])
                K.op("act", lambda: nc.scalar.activation(out=gyv(q, sloc, n), in_=PREf[:, 0:n], func=AF.Gelu_apprx_tanh),
                     r=[rpre], w=[res("gy_%d" % t.loc)])

            if units:
                emit_bu(0)
            for ui, (t, sloc, q) in enumerate(units):
                if ui % 2 == 1:
                    ada_pump(1)
                if ui + 1 < len(units):
                    emit_bu(ui + 1)
                br_b, bi_b = ubanks[ui]
                if q == 0:
                    K.op("dve", lambda: nc.vector.tensor_tensor(out=CRPR, in0=RHO, in1=CR, op=ALU.mult), r=[rs_small, rs_carry], w=[r_crp])
                    K.op("dve", lambda: nc.vector.tensor_tensor(out=CRPI, in0=RHO, in1=CI, op=ALU.mult), r=[rs_small, rs_carry], w=[r_crp])
                cs = cosv(4 * q, 4, 0, n); sn = sinv(4 * q, 4, 0, n)
                sh = lambda a: v3(a, 4, n)
                pbr = sh(PSB[br_b][:, 0:4 * n]); pbi = sh(PSB[bi_b][:, 0:4 * n])
                K.op("dve", lambda: nc.vector.tensor_tensor(out=sh(Tt[0]), in0=pbr, in1=cs, op=ALU.mult), r=[bankres[br_b], rs_tab], w=[rT[0]])
                K.op("dve", lambda: nc.vector.tensor_tensor(out=sh(Tt[1]), in0=pbi, in1=sn, op=ALU.mult), r=[bankres[bi_b], rs_tab], w=[rT[1]])
                K.op("dve", lambda: nc.vector.tensor_tensor(out=sh(Tt[2]), in0=pbi, in1=cs, op=ALU.mult), r=[bankres[bi_b], rs_tab], w=[rT[2]])
                K.op("dve", lambda: nc.vector.tensor_tensor(out=sh(Tt[3]), in0=pbr, in1=sn, op=ALU.mult), r=[bankres[br_b], rs_tab], w=[rT[3]])
                emit_post()
                K.op("dve", lambda: nc.vector.tensor_tensor(out=BTRf, in0=Tt[0], in1=Tt[1], op=ALU.add), r=[rT[0], rT[1]], w=[r_btr])
                K.op("dve", lambda: nc.vector.tensor_tensor(out=BTIf, in0=Tt[2], in1=Tt[3], op=ALU.subtract), r=[rT[2], rT[3]], w=[r_bti])
                K.op("dve", lambda: nc.vector.tensor_tensor(out=sh(BTRf)[:, :, 0], in0=sh(BTRf)[:, :, 0], in1=CRPR[:, 4 * q:4 * q + 4], op=ALU.add),
                     r=[r_crp, r_btr], w=[r_btr])
                K.op("dve", lambda: nc.vector.tensor_tensor(out=sh(BTIf)[:, :, 0], in0=sh(BTIf)[:, :, 0], in1=CRPI[:, 4 * q:4 * q + 4], op=ALU.add),
                     r=[r_crp, r_bti], w=[r_bti])
                rh = RHOT[:, 4 * q * n:(4 * q + 4) * n]
                K.op("dve", lambda: nc.vector.tensor_tensor_scan(out=STRf, data0=rh, data1=BTRf, initial=0.0, op0=ALU.mult, op1=ALU.add),
                     r=[r_btr, r_rhot], w=[r_str])
                K.op("dve", lambda: nc.vector.tensor_tensor_scan(out=STIf, data0=rh, data1=BTIf, initial=0.0, op0=ALU.mult, op1=ALU.add),
                     r=[r_bti, r_rhot], w=[r_sti])
                K.op("dve", lambda: nc.vector.tensor_tensor(out=sh(Tt[0]), in0=sh(STRf), in1=cs, op=ALU.mult), r=[r_str, rs_tab], w=[rT[0]])
                K.op("dve", lambda: nc.vector.tensor_tensor(out=sh(Tt[2]), in0=sh(STRf), in1=sn, op=ALU.mult), r=[r_str, rs_tab], w=[rT[2]])
                K.op("dve", lambda: nc.vector.tensor_tensor(out=sh(Tt[1]), in0=sh(STIf), in1=sn, op=ALU.mult), r=[r_sti, rs_tab], w=[rT[1]])
                K.op("dve", lambda: nc.vector.tensor_tensor(out=sh(Tt[3]), in0=sh(STIf), in1=cs, op=ALU.mult), r=[r_sti, rs_tab], w=[rT[3]])
                sbi = ui % 2
                SRB = SRBs[sbi]; NSIB = NSIBs[sbi]; rsr = r_srb[sbi]
                K.op("dve", lambda: nc.vector.tensor_tensor(out=SRB[:, 0:4 * n], in0=Tt[0], in1=Tt[1], op=ALU.subtract), r=[rT[0], rT[1]], w=[rsr])
                K.op("dve", lambda: nc.vector.scalar_tensor_tensor(out=NSIB[:, 0:4 * n], in0=Tt[2], scalar=-1.0, in1=Tt[3], op0=ALU.mult, op1=ALU.subtract),
                     r=[rT[2], rT[3]], w=[rsr])
                K.op("dve", lambda: nc.vector.tensor_copy(out=SLR[:, 4 * q:4 * q + 4], in_=sh(STRf)[:, :, n - 1]), r=[r_str], w=[r_slr])
                K.op("dve", lambda: nc.vector.tensor_copy(out=SLI[:, 4 * q:4 * q + 4], in_=sh(STIf)[:, :, n - 1]), r=[r_sti], w=[r_sli])
                yb_ = bank("o")
                for oi in range(4):
                    o = 4 * q + oi
                    K.op("pe", lambda: nc.tensor.matmul(PSB[yb_][:, 0:n], lhsT=WC[:, o * 128:(o + 1) * 128], rhs=SRB[:, oi * n:(oi + 1) * n],
                                                        start=(oi == 0), stop=False), r=[rs_wc, rsr], w=[bankres[yb_]], inc=False)
                    K.op("pe", lambda: nc.tensor.matmul(PSB[yb_][:, 0:n], lhsT=WC[:, (16 + o) * 128:(17 + o) * 128], rhs=NSIB[:, oi * n:(oi + 1) * n],
                                                        start=False, stop=(oi == 3)), r=[rs_wc, rsr], w=[bankres[yb_]], inc=(oi == 3))
                pending_post[0] = (t, sloc, q, yb_)
                if q == 3:
                    cN = cosv(0, 16, NT5, 1).rearrange("p o t -> p (o t)"); sN = sinv(0, 16, NT5, 1).rearrange("p o t -> p (o t)")
                    if t.col0 + (sloc - t.loc) + n == TP:
                        cE = cosv(0, 16, NT5 - 1, 1); sE = sinv(0, 16, NT5 - 1, 1)
                        fr = v3(FST[:, 16 * NSEQ:32 * NSEQ], 16, NSEQ)[:, :, 0:1]
                        fi = v3(FST[:, 32 * NSEQ:48 * NSEQ], 16, NSEQ)[:, :, 0:1]
                        cmul(fr, fi, SLR.unsqueeze(2), SLI.unsqueeze(2), cE, sE, TMPC.unsqueeze(2), TMPD.unsqueeze(2),
                             [r_slr, r_sli, rs_tab], [rs_fst], res("s5_cm_tmp"))
                    else:
                        cmul(CR, CI, SLR, SLI, cN, sN, TMPC, TMPD, [r_slr, r_sli, rs_tab], [rs_carry], res("s5_cm_tmp"))
            emit_post()
            ada_flush()
            ST = [BTRf, BTIf, STRf, STIf, Tt[0], Tt[1]]
            rst = res("s5_tmp"); rsr = r_srb[0]
            SRB = SRBs[0]; NSIB = NSIBs[0]
            for t in group:
                if t.kind != "s":
                    continue
                sloc, n = t.loc, t.n
                for q in range(4):
                    br_b = bank("g1"); bi_b = bank("g3")
                    for oi in range(4):
                        o = 4 * q + oi
                        K.op("pe", lambda: nc.tensor.matmul(PSB[br_b][:, oi * n:(oi + 1) * n], lhsT=WBU[:, o * 128:(o + 1) * 128], rhs=uv(q, sloc, n),
                                                            start=True, stop=True),
                             r=[rs_wbu, res("u_%d" % t.loc)], w=[bankres[br_b]], inc=(oi == 3))
                    for oi in range(4):
                        o = 4 * q + oi
                        K.op("pe", lambda: nc.tensor.matmul(PSB[bi_b][:, oi * n:(oi + 1) * n], lhsT=WBU[:, (16 + o) * 128:(17 + o) * 128], rhs=uv(q, sloc, n),
                                                            start=True, stop=True),
                             r=[rs_wbu, res("u_%d" % t.loc)], w=[bankres[bi_b]], inc=(oi == 3))
                    allT = [rT[0], rT[1], rT[2], rT[3], r_bt, r_st]
                    BTR, BTI, STR, STI, T1, T2 = [x[:, 0:4 * n] for x in ST]
                    cs = cosv(4 * q, 4, 0, TS).unsqueeze(2).to_broadcast([128, 4, NSQ, TS])
                    sn = sinv(4 * q, 4, 0, TS).unsqueeze(2).to_broadcast([128, 4, NSQ, TS])
                    sh = lambda a: a.rearrange("p (o s t) -> p o s t", o=4, s=NSQ, t=TS)
                    pbr = sh(PSB[br_b][:, 0:4 * n]); pbi = sh(PSB[bi_b][:, 0:4 * n])
                    K.op("dve", lambda: nc.vector.tensor_tensor(out=sh(T1), in0=pbr, in1=cs, op=ALU.mult), r=[bankres[br_b], rs_tab], w=[rst, allT])
                    K.op("dve", lambda: nc.vector.tensor_tensor(out=sh(T2), in0=pbi, in1=sn, op=ALU.mult), r=[bankres[bi_b], rs_tab], w=[rst])
                    K.op("dve", lambda: nc.vector.tensor_tensor(out=BTR, in0=T1, in1=T2, op=ALU.add), r=[rst], w=[rst])
                    K.op("dve", lambda: nc.vector.tensor_tensor(out=sh(T1), in0=pbi, in1=cs, op=ALU.mult), r=[bankres[bi_b], rs_tab], w=[rst])
                    K.op("dve", lambda: nc.vector.tensor_tensor(out=sh(T2), in0=pbr, in1=sn, op=ALU.mult), r=[bankres[br_b], rs_tab], w=[rst])
                    K.op("dve", lambda: nc.vector.tensor_tensor(out=BTI, in0=T1, in1=T2, op=ALU.subtract), r=[rst], w=[rst])
                    str4 = sh(STR); sti4 = sh(STI); btr4 = sh(BTR); bti4 = sh(BTI)
                    rb = RHO[:, 4 * q:4 * q + 4].unsqueeze(2).to_broadcast([128, 4, NSQ])
                    for tt in range(TS):
                        for (s4, b4, s0) in ((str4, btr4, S0TR), (sti4, bti4, S0TI)):
                            prev = v3(s0, 16, NSQ)[:, 4 * q:4 * q + 4, :] if tt == 0 else s4[:, :, :, tt - 1]
                            K.op("dve", lambda: nc.vector.tensor_tensor(out=s4[:, :, :, tt], in0=prev, in1=rb, op=ALU.mult),
                                 r=[rst, rs_s0, rs_small], w=[rst])
                            K.op("dve", lambda: nc.vector.tensor_tensor(out=s4[:, :, :, tt], in0=s4[:, :, :, tt], in1=b4[:, :, :, tt], op=ALU.add),
                                 r=[rst], w=[rst])
                    c3 = cosv(4 * q, 4, TS - 1, 1).to_broadcast([128, 4, NSQ]); s3_ = sinv(4 * q, 4, TS - 1, 1).to_broadcast([128, 4, NSQ])
                    fr = v3(FST[:, (16 + 4 * q) * NSEQ:(16 + 4 * q + 4) * NSEQ], 4, NSEQ)[:, :, 1:NSEQ]
                    fi = v3(FST[:, (32 + 4 * q) * NSEQ:(32 + 4 * q + 4) * NSEQ], 4, NSEQ)[:, :, 1:NSEQ]
                    tt1 = v3(TAILX[:, 0:64], 4, NSQ); tt2 = v3(TAILX[:, 64:128], 4, NSQ)
                    cmul(fr, fi, str4[:, :, :, TS - 1], sti4[:, :, :, TS - 1], c3, s3_, tt1, tt2, [rst, rs_tab], [rs_fst], res("tailx"))
                    K.op("dve", lambda: nc.vector.tensor_tensor(out=sh(T1), in0=sh(STR), in1=cs, op=ALU.mult), r=[rst, rs_tab], w=[rst])
                    K.op("dve", lambda: nc.vector.tensor_tensor(out=sh(T2), in0=sh(STI), in1=sn, op=ALU.mult), r=[rst, rs_tab], w=[rst])
                    K.op("dve", lambda: nc.vector.tensor_tensor(out=SRB[:, 0:4 * n], in0=T1, in1=T2, op=ALU.subtract), r=[rst], w=[rsr])
                    K.op("dve", lambda: nc.vector.tensor_tensor(out=sh(T1), in0=sh(STR), in1=sn, op=ALU.mult), r=[rst, rs_tab], w=[rst])
                    K.op("dve", lambda: nc.vector.tensor_tensor(out=sh(T2), in0=sh(STI), in1=cs, op=ALU.mult), r=[rst, rs_tab], w=[rst])
                    K.op("dve", lambda: nc.vector.scalar_tensor_tensor(out=NSIB[:, 0:4 * n], in0=T1, scalar=-1.0, in1=T2, op0=ALU.mult, op1=ALU.subtract),
                         r=[rst], w=[rsr])
                    yb_ = bank("o")
                    for oi in range(4):
                        o = 4 * q + oi
                        K.op("pe", lambda: nc.tensor.matmul(PSB[yb_][:, 0:n], lhsT=WC[:, o * 128:(o + 1) * 128], rhs=SRB[:, oi * n:(oi + 1) * n],
                                                            start=(oi == 0), stop=False), r=[rs_wc, rsr], w=[bankres[yb_]], inc=False)
                        K.op("pe", lambda: nc.tensor.matmul(PSB[yb_][:, 0:n], lhsT=WC[:, (16 + o) * 128:(17 + o) * 128], rhs=NSIB[:, oi * n:(oi + 1) * n],
                                                            start=False, stop=(oi == 3)), r=[rs_wc, rsr], w=[bankres[yb_]], inc=(oi == 3))
                    K.op("dve", lambda: nc.vector.scalar_tensor_tensor(out=PREf[:, 0:n], in0=uv(q, sloc, n), scalar=ptc(l, "s5_d", q),
                                                                       in1=PSB[yb_][:, 0:n], op0=ALU.mult, op1=ALU.add),
                         r=[bankres[yb_], res("u_%d" % t.loc), rs_pt], w=[rpre])
                    K.op("act", lambda: nc.scalar.activation(out=gyv(q, sloc, n), in_=PREf[:, 0:n], func=AF.Gelu_apprx_tanh),
                         r=[rpre], w=[res("gy_%d" % t.loc)])
            K.barrier()
            wgl = wglu_d[l].rearrange("(k p) f -> p k f", p=128)
            sl = ring_load(lambda slot: [(slot[:, 0:2048].rearrange("p (k f) -> p k f", k=4), wgl[:, :, :])])
            sg3 = RING[sl][:, 0:2048].rearrange("p (k f) -> p k f", k=4)
            SG = [WORK[:, i * 512:(i + 1) * 512] for i in range(4)]
            for t in group:
                n = t.n
                bs = []
                for oc in range(4):
                    b = bank(("g1", "g3")[oc % 2])
                    bs.append(b)
                    for k in range(4):
                        K.op("pe", lambda: nc.tensor.matmul(PSB[b][:, 0:n], lhsT=sg3[:, k, oc * 128:(oc + 1) * 128], rhs=gyv(k, t.loc, n),
                                                            start=(k == 0), stop=(k == 3)),
                             r=[ringres[sl], res("gy_%d" % t.loc)], w=[bankres[b]], inc=(k == 3))
                for oc in range(4):
                    b = bs[oc]
                    rsg = res("sg_%d" % oc)
                    K.op("act", lambda: nc.scalar.activation(out=SG[oc][:, 0:n], in_=PSB[b][:, 0:n], func=AF.Sigmoid, bias=ptc(l, "b_glu", oc), scale=1.0),
                         r=[bankres[b], rs_pt], w=[rsg])
                for oc in range(4):
                    rsg = res("sg_%d" % oc)
                    K.op("dve", lambda: nc.vector.tensor_tensor(out=gyv(oc, t.loc, n), in0=gyv(oc, t.loc, n), in1=SG[oc][:, 0:n], op=ALU.mult),
                         r=[rsg], w=[res("gy_%d" % t.loc)])
            wos = wout_d[l].rearrange("(k p) f -> p k f", p=128)
            for half in range(2):
                sl = ring_load(lambda slot: [(slot[:, 0:4096].rearrange("p (k f) -> p k f", k=8), wos[:, :, half * 512:(half + 1) * 512])])
                so3 = RING[sl][:, 0:4096].rearrange("p (k f) -> p k f", k=8)
                for t in group:
                    n = t.n
                    for oc in range(4):
                        d = half * 4 + oc
                        b = bank("o")
                        for k in range(KC):
                            rhs = ylv(k, t.loc, n) if k < 4 else gyv(k - 4, t.loc, n)
                            K.op("pe", lambda: nc.tensor.matmul(PSB[b][:, 0:n], lhsT=so3[:, k, oc * 128:(oc + 1) * 128], rhs=rhs,
                                                                start=(k == 0), stop=(k == KC - 1)),
                                 r=[ringres[sl], res("yl_%d" % t.loc), res("gy_%d" % t.loc)], w=[bankres[b]], inc=(k == KC - 1))
                        residual(t, s, d, b)

        ds_o = K.dsem()

        def state_out(l):
            OST = AUX[0:NSEQ, 0:2048]
            rs_ost = res("ost")
            pieces = [(0, 12, oconv_d, None), (12, 4, oh_d, None), (16, 16, osr_d, None), (32, 16, osi_d, None)]
            for (c0, ncnk, dst, _) in pieces:
                for g4 in range(0, ncnk, 4):
                    b = bank("m")
                    for i in range(4):
                        cidx = c0 + g4 + i
                        K.op("pe", lambda: nc.tensor.transpose(out=PSB[b][0:NSEQ, i * 128:(i + 1) * 128], in_=FST[:, cidx * NSEQ:(cidx + 1) * NSEQ], identity=IDF[:]),
                             r=[rs_fst, rs_const], w=[bankres[b]], inc=(i == 3))
                    if c0 == 0:
                        for i in range(4):
                            cidx = g4 + i
                            c, kk = cidx // 3, cidx % 3
                            K.op("dve", lambda: nc.vector.tensor_copy(out=OST[:, kk * 512 + c * 128: kk * 512 + (c + 1) * 128], in_=PSB[b][0:NSEQ, i * 128:(i + 1) * 128]),
                                 r=[bankres[b]], w=[rs_ost, rs_aux[0], rs_aux[1]])
                    else:
                        K.op("dve", lambda: nc.vector.tensor_copy(out=OST[:, g4 * 128:(g4 + 4) * 128], in_=PSB[b][0:NSEQ, 0:512]),
                             r=[bankres[b]], w=[rs_ost, rs_aux[0], rs_aux[1]])
                K.dma("sp", ds_o, [(dst[l], OST[:, 0:ncnk * 128])], r=[rs_ost, rs_aux[0], rs_aux[1]])

        def final_out():
            K.barrier()
            ds_y = [K.dsem() for _ in range(2)]
            XN = [WORK[:, i * 1024:(i + 1) * 1024] for i in range(2)]
            YST = [WORK[:, 2048 + i * 1024: 2048 + (i + 1) * 1024] for i in range(2)]
            XSQfs = [wbf(4352, 8 * 128), wbf(4352 + 512, 8 * 128)]
            RSs = [WORK[:, 5400:5528], WORK[:, 5528:5656]]; SQs = [WORK[:, 5656:5784], WORK[:, 5784:5912]]
            ftiles = []
            for (dst, ntok, colbase) in ((yp_d, TP, 0), (ys_d, NSAMP, TP)):
                for t0 in range(0, ntok, 128):
                    ftiles.append((dst, t0, min(128, ntok - t0), colbase + t0))

            def fres(i):
                pi_ = i % 2
                return (res("f_xsq%d" % pi_), res("f_rs%d" % pi_), res("f_xn%d" % pi_), res("f_yst%d" % pi_))

            def stage_a(i):
                dst, t0, n, col = ftiles[i]
                pi_ = i % 2
                tl = tiles_all[min(col // 512, 4)]
                rxs, rrs, rxn, ryst = fres(i)
                XSQf = XSQfs[pi_]; RS = RSs[pi_]; SQ = SQs[pi_]
                for c in range(KC):
                    K.op("act", lambda: nc.scalar.activation(out=XSQf[:, c * 128: c * 128 + n], in_=xv(c, col, n), func=AF.Square),
                         r=[xres(tl, c)], w=[rxs])
                b = bank("m")
                for c in range(KC):
                    K.op("pe", lambda: nc.tensor.matmul(PSB[b][:, 0:n], lhsT=ONESB[:], rhs=XSQf[:, c * 128: c * 128 + n], start=(c == 0), stop=(c == KC - 1)),
                         r=[rxs, rs_const], w=[bankres[b]], inc=(c == KC - 1))
                K.op("act", lambda: nc.scalar.activation(out=SQ[:, 0:n], in_=PSB[b][:, 0:n], func=AF.Sqrt, bias=EPSC, scale=1.0 / D),
                     r=[bankres[b], rs_const], w=[rrs])

            def stage_b(i):
                dst, t0, n, col = ftiles[i]
                pi_ = i % 2
                tl = tiles_all[min(col // 512, 4)]
                rxs, rrs, rxn, ryst = fres(i)
                RS = RSs[pi_]; SQ = SQs[pi_]
                K.op("dve", lambda: nc.vector.reciprocal(out=RS[:, 0:n], in_=SQ[:, 0:n]), r=[rrs], w=[rrs])
                for c in range(KC):
                    K.op("dve", lambda: nc.vector.scalar_tensor_tensor(out=XN[pi_][:, c * 128: c * 128 + n], in0=xv(c, col, n),
                                                                       scalar=PT[:, PR_FINAL + c: PR_FINAL + c + 1], in1=RS[:, 0:n], op0=ALU.mult, op1=ALU.mult),
                         r=[xres(tl, c), rrs, rs_pt], w=[rxn])
                for half in range(2):
                    b = bank("o")
                    for cc in range(4):
                        c = half * 4 + cc
                        K.op("pe", lambda: nc.tensor.transpose(out=PSB[b][0:n, cc * 128:(cc + 1) * 128], in_=XN[pi_][:, c * 128: c * 128 + n], identity=IDF[:]),
                             r=[rxn, rs_const], w=[bankres[b]], inc=(cc == 3))
                    if half == 0:
                        K.op("dve", lambda: nc.vector.tensor_copy(out=YST[pi_][0:n, 0:512], in_=PSB[b][0:n, 0:512]), r=[bankres[b]], w=[ryst])
                    else:
                        K.op("act", lambda: nc.scalar.activation(out=YST[pi_][0:n, 512:1024], in_=PSB[b][0:n, 0:512], func=AF.Identity), r=[bankres[b]], w=[ryst])
                K.dma("sp", ds_y[pi_], [(dst[t0:t0 + n, :], YST[pi_][0:n, :])], r=[ryst])

            stage_a(0)
            for i in range(len(ftiles)):
                if i + 1 < len(ftiles):
                    stage_a(i + 1)
                stage_b(i)

        def main_prog():
            stage = [0]

            def stop():
                stage[0] += 1
                return STOP_AT is not None and stage[0] > STOP_AT
            if stop():
                return
            ada_enqueue(0, 0, 10)
            ada_flush()
            for l in range(DEPTH):
                if stop():
                    return
                ffn(l, 0, 0, "norm_ffn1", groups[0], True, groups[1])
                ffn(l, 0, 0, "norm_ffn1", groups[1], False, None)
                if stop():
                    return
                mixer_setup(l)
                if stop():
                    return
                for gi, g in enumerate(groups):
                    if gi == 0:
                        ada_enqueue(l, 10, 18)
                    if l + 1 < DEPTH:
                        if gi == 0:
                            ada_enqueue(l + 1, 0, 6)
                        else:
                            ada_enqueue(l + 1, 6, 10)
                    mixer_group(l, gi, g)
                    if stop():
                        return
                ada_flush()
                state_out(l)
                if stop():
                    return
                ffn(l, 1, 2, "norm_ffn2", groups[0], True, groups[1])
                ffn(l, 1, 2, "norm_ffn2", groups[1], False, None)
                if stop():
                    return
            final_out()
        main_prog()
        K.finish()
        rec_out = K.rec
    if want_rec:
        return rec_out
    return nc


_NC_CACHE = {}


def _host_layouts(inp):
    L = DEPTH
    prow = np.zeros((PR_ROWS, 128), np.float32)
    for l in range(L):
        for name, k in PR_NAMES:
            r0 = PR_LAYER * l + PR_OFF[name]
            prow[r0:r0 + k] = np.asarray(inp[name][l], np.float32).reshape(k, 128)
    prow[PR_FINAL:PR_FINAL + 8] = np.asarray(inp["norm_final"], np.float32).reshape(8, 128)
    ld = np.asarray(inp["s5_log_dt"], np.float32)
    dtx = np.repeat(ld.reshape(L, 16, 2).transpose(0, 2, 1), 64, axis=1)
    dtx = np.ascontiguousarray(dtx)
    wg = np.zeros((L, 128, 8, 128), np.float32)
    for gi, nm in enumerate(("w_rg", "w_ig")):
        w = np.asarray(inp[nm], np.float32)
        for c in range(4):
            wg[:, 0:64, gi * 4 + c, 0:64] = w[:, 2 * c]
            wg[:, 64:128, gi * 4 + c, 64:128] = w[:, 2 * c + 1]
    wg = wg.reshape(L, 128, 8 * 128)
    bnat = np.zeros((L, 2, 128, 16, 128), np.float32)
    for comp, nm in enumerate(("s5_b_re", "s5_b_im")):
        Bm = np.asarray(inp[nm], np.float32)
        for o in range(16):
            for gl in range(2):
                g = 2 * o + gl
                cb = (g % 8) * 16
                bnat[:, comp, gl * 64:(gl + 1) * 64, o, cb:cb + 16] = Bm[:, g]
    bnat = bnat.reshape(L, 2, 128, 16 * 128)
    cpad = np.zeros((L, 128, 2, 16, 128), np.float32)
    for comp, nm in enumerate(("s5_c_re", "s5_c_im")):
        Cm = np.asarray(inp[nm], np.float32)
        for o in range(16):
            for gl in range(2):
                g = 2 * o + gl
                cb = (g % 8) * 16
                cpad[:, gl * 64:(gl + 1) * 64, comp, o, cb:cb + 16] = Cm[:, g].transpose(0, 2, 1)
    cpad = cpad.reshape(L, 128, 32 * 128)
    return prow, dtx, wg, bnat, cpad


def kernel(**inp):
    if "nc" not in _NC_CACHE:
        _NC_CACHE["nc"] = build_program()
    nc = _NC_CACHE["nc"]
    f = lambda a: np.ascontiguousarray(np.asarray(a, np.float32))
    prow, dtx, wg, bnat, cpad = _host_layouts(inp)
    shared = {
        "prow": prow, "dtx": dtx, "wgates": wg, "bnat": bnat, "cpad": cpad,
        "w_ada": f(inp["w_ada"]),
        "w1_ffn1": f(inp["w1_ffn1"]), "w3_ffn1": f(inp["w3_ffn1"]), "w2_ffn1": f(inp["w2_ffn1"]),
        "w1_ffn2": f(inp["w1_ffn2"]), "w3_ffn2": f(inp["w3_ffn2"]), "w2_ffn2": f(inp["w2_ffn2"]),
        "w_in": f(inp["w_in"]), "w_glu": f(inp["w_glu"]), "w_out": f(inp["w_out"]),
    }
    xp = f(inp["x_prompt"]); xs = f(inp["x_sample"]); cp = f(inp["c_prompt"]); cs = f(inp["c_sample"])
    stc = f(inp["state_lru_conv"]); sth = f(inp["state_lru_h"]); sr = f(inp["state_s5_re"]); si = f(inp["state_s5_im"])
    in_maps = []
    for i in range(NCORES):
        s0, s1 = NSQ * i, NSQ * (i + 1)
        m = dict(shared)
        m["xp"] = xp[i]
        m["xs"] = np.ascontiguousarray(xs[s0:s1].reshape(NSAMP, D))
        m["c17"] = np.ascontiguousarray(np.concatenate([cp[i:i + 1], cs[s0:s1]], axis=0))
        m["st_conv"] = np.ascontiguousarray(stc[:, s0:s1].reshape(DEPTH, NSQ, 3 * 512))
        m["st_h"] = np.ascontiguousarray(sth[:, s0:s1])
        m["st_sr"] = np.ascontiguousarray(sr[:, s0:s1].reshape(DEPTH, NSQ, 2048))
        m["st_si"] = np.ascontiguousarray(si[:, s0:s1].reshape(DEPTH, NSQ, 2048))
        in_maps.append(m)
    res = run_bass_kernel_spmd(nc, in_maps, core_ids=list(range(NCORES)))
    R = res.results
    B = NCORES
    y_prompt = np.stack([R[i]["y_p"] for i in range(B)], axis=0).astype(np.float32)
    y_sample = np.concatenate([R[i]["y_s"].reshape(NSQ, TS, D) for i in range(B)], axis=0).astype(np.float32)

    def gather(name, tail):
        p = np.stack([R[i][name][:, 0] for i in range(B)], axis=1)
        s = np.concatenate([R[i][name][:, 1:] for i in range(B)], axis=1)
        return (p.reshape((DEPTH, B) + tail).astype(np.float32), s.reshape((DEPTH, NSQ * B) + tail).astype(np.float32))
    p_conv, s_conv = gather("o_conv", (3, 512))
    p_h, s_h = gather("o_h", (512,))
    p_sr, s_sr = gather("o_sr", (32, 64))
    p_si, s_si = gather("o_si", (32, 64))
    return (y_prompt, y_sample, p_conv, p_h, p_sr, p_si, s_conv, s_h, s_sr, s_si)
```

```python
import numpy as np
from contextlib import ExitStack
import concourse.bass as bass
import concourse.mybir as mybir
from concourse.bass_utils import run_bass_kernel_spmd

F32 = mybir.dt.float32
BF16 = mybir.dt.bfloat16
I32 = mybir.dt.int32
AF = mybir.ActivationFunctionType
ALU = mybir.AluOpType

NCORES = 8
D = 1024
KC = 8
DFF = 2816
FC = 22
TP = 2048
NSQ = 16
TS = 4
NSAMP = NSQ * TS
NTOK = TP + NSAMP
NSEQ = 17
DEPTH = 2
NT5 = 128
PI = float(np.pi)
TWO_PI = float(2 * np.pi)
EPS = 1e-6

PR_NAMES = [("norm_ffn1", 8), ("norm_mix", 8), ("norm_ffn2", 8), ("conv_w", 16), ("conv_b", 4),
            ("b_rg", 4), ("b_ig", 4), ("lru_lambda", 4), ("s5_d", 4), ("b_glu", 4),
            ("s5_a_re", 16), ("s5_a_im", 16), ("b_ada", 72)]
PR_OFF = {}
_o = 0
for _n, _k in PR_NAMES:
    PR_OFF[_n] = _o
    _o += _k
PR_LAYER = _o
PR_FINAL = PR_LAYER * DEPTH
PR_ROWS = 384

SAME_ENGINE_SYNC = ('act', 'pool', 'dve')
STOP_AT = None
SETUP_PARTS = 4
import os as _os
DBG_XT = int(_os.environ.get('DBG_XT', '999'))
DBG_XV = _os.environ.get('DBG_XV', '')


class Res:
    __slots__ = ("w", "rs", "name")

    def __init__(self, name=""):
        self.w = None
        self.rs = {}
        self.name = name


class DSem:
    def __init__(self, handle, key):
        self.h = handle
        self.key = key
        self.v = 0


class Sched:
    def __init__(self, nc, es, waited=None):
        self.nc = nc
        self.es = es
        self.waited = waited
        self.rec = {k: set() for k in ("pe", "dve", "act", "pool", "sp")}
        self.last_inc = {k: 0 for k in ("pe", "dve", "act", "pool", "sp")}
        self.eng = dict(pe=nc.tensor, dve=nc.vector, act=nc.scalar, pool=nc.gpsimd, sp=nc.sync)
        self.sem = {k: es.enter_context(nc.semaphore("cs_" + k)) for k in self.eng}
        self.cnt = {k: 0 for k in self.eng}
        self.known = {k: {} for k in self.eng}
        self.ndsem = 0
        self.dsems = []

    def dsem(self):
        h = self.es.enter_context(self.nc.semaphore("ds%d" % self.ndsem))
        d = DSem(h, "d%d" % self.ndsem)
        self.ndsem += 1
        self.dsems.append(d)
        return d

    @staticmethod
    def _flat(xs):
        out = []
        for x in xs:
            if isinstance(x, (list, tuple)):
                out.extend(Sched._flat(x))
            else:
                out.append(x)
        return out

    def barrier(self, engs=("pe", "dve", "act", "sp")):
        for e in engs:
            for f in ("pe", "dve", "act"):
                if self.cnt[f] == 0 or (f == e and e in ("pe", "sp")):
                    continue
                if self.known[e].get(f, 0) < self.cnt[f]:
                    self.eng[e].wait_ge(self.sem[f], self.cnt[f])
                    self.known[e][f] = self.cnt[f]
                    self.rec[f].add(self.cnt[f])

    def _waits(self, e, reads, writes):
        reads = self._flat(reads); writes = self._flat(writes)
        deps = {}
        raw_same = [0]

        def add(tok, raw=False):
            key, h, v = tok
            if key == e:
                if raw and v > raw_same[0]:
                    raw_same[0] = v
                return
            if key not in deps or deps[key][1] < v:
                deps[key] = (h, v)
        for r in reads:
            if r.w is not None:
                add(r.w, True)
        for w in writes:
            if w.w is not None:
                add(w.w, not w.name.startswith("bank"))
            for t in w.rs.values():
                add(t, not w.name.startswith("bank"))
        if raw_same[0] > 0:
            deps[e] = (self.sem[e], raw_same[0])
        for key, (h, v) in deps.items():
            if key == e:
                if e == "pe" or e not in SAME_ENGINE_SYNC:
                    continue
                if v > self.cnt[e]:
                    continue
            if self.known[e].get(key, 0) < v:
                self.eng[e].wait_ge(h, v)
                self.known[e][key] = v
                if key in self.rec:
                    self.rec[key].add(v)

    def _record(self, tok, reads, writes):
        reads = self._flat(reads); writes = self._flat(writes)
        key = tok[0]
        for r in reads:
            if key not in r.rs or r.rs[key][2] < tok[2]:
                r.rs[key] = tok
        for w in writes:
            w.w = tok
            w.rs = {}

    def op(self, e, fn, r=(), w=(), inc=True):
        r = self._flat(r); w = self._flat(w)
        w = w + [x for x in r if x.name.startswith("bank")]
        r = [x for x in r if not x.name.startswith("bank")]
        self._waits(e, r, w)
        inst = fn()
        self.cnt[e] += 1
        k = self.cnt[e]
        if self.waited is None or k in self.waited[e]:
            inst.then_inc(self.sem[e], k - self.last_inc[e])
            self.last_inc[e] = k
        tok = (e, self.sem[e], k)
        self._record(tok, r, w)
        return inst

    def dma(self, q, ds, pairs, r=(), w=()):
        self._waits(q, r, w)
        for (o, i) in pairs:
            self.eng[q].dma_start(out=o, in_=i).then_inc(ds.h, 16)
            ds.v += 16
        tok = (ds.key, ds.h, ds.v)
        self._record(tok, r, w)

    def finish(self):
        for d in self.dsems:
            if d.v > 0 and self.known["sp"].get(d.key, 0) < d.v:
                self.nc.sync.wait_ge(d.h, d.v)
        for e in ("pe", "dve", "act", "pool"):
            if self.cnt[e] > 0:
                self.nc.sync.wait_ge(self.sem[e], self.cnt[e])
                self.rec[e].add(self.cnt[e])


class Tile:
    def __init__(self, col0, n, kind, loc):
        self.col0 = col0
        self.n = n
        self.kind = kind
        self.loc = loc


def build_program(waited=None, want_rec=False):
    if waited is None and not want_rec:
        rec = build_program(None, True)
        return build_program(rec, False)
    nc = bass.Bass("TRN2", target_bir_lowering=False)

    def din(name, shape):
        return nc.dram_tensor(name, list(shape), F32, kind="ExternalInput").ap()

    def dout(name, shape):
        return nc.dram_tensor(name, list(shape), F32, kind="ExternalOutput").ap()

    xp_d = din("xp", [TP, D])
    xs_d = din("xs", [NSAMP, D])
    c17_d = din("c17", [NSEQ, D])
    stc_d = din("st_conv", [DEPTH, NSQ, 3 * 512])
    sth_d = din("st_h", [DEPTH, NSQ, 512])
    str_d = din("st_sr", [DEPTH, NSQ, 2048])
    sti_d = din("st_si", [DEPTH, NSQ, 2048])
    prow_d = din("prow", [PR_ROWS, 128])
    dtx_d = din("dtx", [DEPTH, 128, 16])
    wada_d = din("w_ada", [DEPTH, D, 9 * D])
    w1_d = [din("w1_ffn1", [DEPTH, D, DFF]), din("w1_ffn2", [DEPTH, D, DFF])]
    w3_d = [din("w3_ffn1", [DEPTH, D, DFF]), din("w3_ffn2", [DEPTH, D, DFF])]
    w2_d = [din("w2_ffn1", [DEPTH, DFF, D]), din("w2_ffn2", [DEPTH, DFF, D])]
    win_d = din("w_in", [DEPTH, D, 1536])
    wglu_d = din("w_glu", [DEPTH, 512, 512])
    wout_d = din("w_out", [DEPTH, D, D])
    wg_d = din("wgates", [DEPTH, 128, 8 * 128])
    bnat_d = din("bnat", [DEPTH, 2, 128, 16 * 128])
    cpad_d = din("cpad", [DEPTH, 128, 32 * 128])

    yp_d = dout("y_p", [TP, D])
    ys_d = dout("y_s", [NSAMP, D])
    oconv_d = dout("o_conv", [DEPTH, NSEQ, 3 * 512])
    oh_d = dout("o_h", [DEPTH, NSEQ, 512])
    osr_d = dout("o_sr", [DEPTH, NSEQ, 2048])
    osi_d = dout("o_si", [DEPTH, NSEQ, 2048])

    with ExitStack() as es:
        K = Sched(nc, es, waited)

        def sb(name, shape, dt=F32):
            return es.enter_context(nc.sbuf_tensor(name, list(shape), dt))

        XT = sb("XT", [128, KC * NTOK])
        RING = [sb("RING%d" % i, [128, 4096], BF16) for i in range(3)]
        WORK = sb("WORK", [128, 16320])
        AUX = sb("AUX", [128, 4128])
        WBU = sb("WBU", [128, 32 * 128], BF16)
        WC = sb("WC", [128, 32 * 128], BF16)
        WG = sb("WG", [128, 8 * 128], BF16)
        ABG = sb("ABG", [128, 9 * KC * NSEQ])
        PT = sb("PT", [128, PR_ROWS])
        IDF = sb("IDF", [128, 128])
        IDB = sb("IDB", [128, 128], BF16)
        ONESB = sb("ONESB", [128, 128], BF16)
        SILUC = sb("SILUC", [128, KC * NSEQ], BF16)
        FST = sb("FST", [128, 48 * NSEQ])
        SMALL = sb("SMALL", [128, 1700])
        TAU = sb("TAU", [128, NT5 + 1])
        PSB = [es.enter_context(nc.psum_tensor("PSB%d" % i, [128, 512], F32)) for i in range(8)]

        R = {}

        def res(name):
            if name not in R:
                R[name] = Res(name)
            return R[name]

        bankres = [res("bank%d" % i) for i in range(8)]
        ringres = [res("ring%d" % i) for i in range(3)]
        ringsem = [K.dsem() for _ in range(3)]
        ring_i = [0]
        ada_loaded = []
        role_banks = {"g1": [0, 1], "g3": [2, 3], "o": [4, 5], "m": [6, 7]}
        role_i = {k: 0 for k in role_banks}

        def bank(role):
            b = role_banks[role][role_i[role] % 2]
            role_i[role] += 1
            return b

        def ring_load(pairs_fn, ada=False):
            s = ring_i[0] % 3
            ring_i[0] += 1
            K.dma("pool", ringsem[s], pairs_fn(RING[s]), w=[ringres[s]])
            return s

        sm_off = [0]

        def small(n):
            o = sm_off[0]
            sm_off[0] += n
            assert sm_off[0] <= 1700
            return SMALL[:, o:o + n]

        A_RE = small(16); A_IM = small(16); DTX = small(16); DT = small(16); ARC = small(16)
        RHO = small(16); TH = small(16); TMPA = small(16); TMPB = small(16); TMPC = small(16); TMPD = small(16)
        F_R = small(16); F_I = small(16); ABR = small(16); ABI = small(16)
        SC1 = small(4); SC2 = small(4); SPT = small(4); SCH = small(4); HBG = small(8)
        HC = small(4)
        CR = small(16); CI = small(16)
        SLR = small(16); SLI = small(16)
        H0T = small(64)
        S0R = small(256); S0I = small(256)
        S0TR = small(256); S0TI = small(256)
        EPSC = small(1)
        HIST = small(12); CRPR = small(16); CRPI = small(16); ONEC = small(1); PIC = small(1)
        rs_small = res("small_s5par")

        def xv(c, col0, n):
            return XT[:, c * NTOK + col0: c * NTOK + col0 + n]

        def wbf(off_words, nelem):
            return WORK[:, off_words: off_words + (nelem + 1) // 2].bitcast(BF16)

        PROW = AUX[:, 1024:1024 + 384]
        C17 = AUX[0:NSEQ, 0:1024]
        H_B = wbf(0, 8 * 1088)
        A_B = wbf(4352, 22 * 1088)

        def hv(c, loc, n):
            return H_B[:, c * 1088 + loc: c * 1088 + loc + n]

        def av(f, loc, n):
            return A_B[:, f * 1088 + loc: f * 1088 + loc + n]

        def norm_tmps(tb):
            if tb < 0:
                return (AUX[:, 0:2048].bitcast(BF16), AUX[:, 2048:2560], AUX[:, 2560:3072],
                        [AUX[:, 3072:3584], AUX[:, 3584:4096]],
                        (rs_auxh[0:4], [rs_auxh[4], rs_auxh[5]], [rs_auxh[6], rs_auxh[7]]))
            return (wbf(tb, 8 * 512), WORK[:, tb + 2048: tb + 2048 + 512], WORK[:, tb + 2560: tb + 2560 + 512],
                    [WORK[:, tb + 3072 + i * 512: tb + 3072 + (i + 1) * 512] for i in range(2)],
                    (res("xsq"), res("rstd"), [res("nt0_0"), res("nt0_1")]))
        S32 = [AUX[:, 0:512], AUX[:, 1024:1536]]
        T64 = [AUX[:, 2048:2112], AUX[:, 3072:3136]]

        XBW = 1091
        XB = WORK[:, 4352: 4352 + 4 * XBW]
        GY_B = wbf(8716, 4 * 1088)
        U_B = wbf(10892, 4 * 1088)
        YL_B = wbf(13068, 4 * 1088)
        XBS = WORK[:, 15244: 15244 + 448]
        TAILX = WORK[:, 15244 + 448: 16320]

        def xbv(c, loc, n):
            return XB[:, c * XBW + loc: c * XBW + loc + n]

        def gyv(c, loc, n):
            return GY_B[:, c * 1088 + loc: c * 1088 + loc + n]

        def uv(c, loc, n):
            return U_B[:, c * 1088 + loc: c * 1088 + loc + n]

        def ylv(c, loc, n):
            return YL_B[:, c * 1088 + loc: c * 1088 + loc + n]

        COS = AUX[:, 0:16 * 129]
        SIN = AUX[:, 16 * 129: 32 * 129]

        def cosv(o0, no, t0, nt):
            return COS.rearrange("p (o t) -> p o t", o=16)[:, o0:o0 + no, t0:t0 + nt]

        def sinv(o0, no, t0, nt):
            return SIN.rearrange("p (o t) -> p o t", o=16)[:, o0:o0 + no, t0:t0 + nt]

        rs_auxh = [res("auxh%d" % i) for i in range(9)]
        rs_aux = [[rs_auxh[2 * i], rs_auxh[2 * i + 1]] for i in range(4)] + [[rs_auxh[8]]]
        rs_tab = [res("tables")] + rs_auxh

        def ptc(l, name, c, n=1):
            base = PR_LAYER * l + PR_OFF[name] + c
            return PT[:, base: base + n]

        rs_pt = res("pt")

        def abg(s, which, c, s0, ns):
            base = ((s * 3 + which) * KC + c) * NSEQ
            return ABG[:, base + s0: base + s0 + ns]

        rs_abg = res("abg")
        rs_modt = res("modt")

        tiles_all = [Tile(0, 512, "p", 0), Tile(512, 512, "p", 512),
                     Tile(1024, 512, "p", 0), Tile(1536, 512, "p", 512), Tile(2048, 64, "s", 1024)]
        groups = [tiles_all[0:2], tiles_all[2:5]]

        def xres(t, c):
            return res("x_%d_%d" % (t.col0, c))

        def bc_seq(ap2, n):
            return ap2.unsqueeze(2).to_broadcast([128, NSQ, TS])

        def v3(ap, a, b):
            return ap.rearrange("p (a b) -> p a b", a=a, b=b)

        rs_const = res("const")
        K.op("pool", lambda: nc.gpsimd.memset(IDF[:], 0.0), w=[rs_const])
        K.op("pool", lambda: nc.gpsimd.affine_select(out=IDF[:], in_=IDF[:], compare_op=ALU.not_equal, fill=1.0,
                                                     base=0, pattern=[[-1, 128]], channel_multiplier=1), r=[rs_const], w=[rs_const])
        K.op("pool", lambda: nc.gpsimd.memset(ONESB[:], 1.0), w=[rs_const])
        K.op("pool", lambda: nc.gpsimd.iota(TAU[:], pattern=[[1, NT5 + 1]], base=0, channel_multiplier=0,
                                            allow_small_or_imprecise_dtypes=True), w=[rs_const])
        K.op("dve", lambda: nc.vector.tensor_copy(out=IDB[:], in_=IDF[:]), r=[rs_const], w=[rs_const])
        K.op("dve", lambda: nc.vector.memset(EPSC, EPS), w=[rs_const])
        K.op("dve", lambda: nc.vector.memset(ONEC, 1.0), w=[rs_const])
        K.op("dve", lambda: nc.vector.memset(PIC, PI), w=[rs_const])

        ds_misc = K.dsem()
        rs_prow = res("prow")
        if SETUP_PARTS >= 2: K.dma("sp", ds_misc, [(PROW[:, i * 128:(i + 1) * 128], prow_d[i * 128:(i + 1) * 128, :]) for i in range(3)],
              w=[rs_prow, rs_aux[1]])
        for i in range(3 if SETUP_PARTS >= 2 else 0):
            b = bank("m")
            K.op("pe", lambda: nc.tensor.transpose(out=PSB[b][:, 0:128], in_=PROW[:, i * 128:(i + 1) * 128], identity=IDF[:]),
                 r=[rs_prow, rs_aux[1], rs_const], w=[bankres[b]])
            K.op("dve", lambda: nc.vector.tensor_copy(out=PT[:, i * 128:(i + 1) * 128], in_=PSB[b][:, 0:128]),
                 r=[bankres[b]], w=[rs_pt])

        ds_c = K.dsem()
        rs_c17 = res("c17")
        if SETUP_PARTS >= 3: K.dma("sp", ds_c, [(C17, c17_d[:, :])], w=[rs_c17, rs_aux[0]])
        rs_siluc = res("siluc")
        for c in range(KC if SETUP_PARTS >= 3 else 0):
            b = bank("m")
            K.op("pe", lambda: nc.tensor.transpose(out=PSB[b][:, 0:NSEQ], in_=C17[:, c * 128:(c + 1) * 128],
                                                   identity=IDF[0:NSEQ, 0:NSEQ]),
                 r=[rs_c17, rs_aux[0], rs_const], w=[bankres[b]])
            K.op("act", lambda: nc.scalar.activation(out=SILUC[:, c * NSEQ:(c + 1) * NSEQ], in_=PSB[b][:, 0:NSEQ], func=AF.Silu),
                 r=[bankres[b]], w=[rs_siluc])

        ds_x = [K.dsem() for _ in range(4)]
        n_xt = 0
        for (src, ntok, colbase) in (((xp_d, TP, 0), (xs_d, NSAMP, TP)) if SETUP_PARTS >= 4 else ()):
            for t0 in range(0, ntok, 128):
                if n_xt >= DBG_XT:
                    break
                n = min(128, ntok - t0)
                si = n_xt % 4
                n_xt += 1
                stg = AUX[:, si * 1024:(si + 1) * 1024]
                K.dma("sp", ds_x[si], [(stg[0:n, :], src[t0:t0 + n, :])], w=[rs_aux[si]])
                for half in range(2):
                    b = bank("m")
                    for cc in range(4):
                        c = half * 4 + cc
                        K.op("pe", lambda: nc.tensor.transpose(out=PSB[b][:, cc * 128: cc * 128 + n],
                                                               in_=stg[0:n, c * 128:(c + 1) * 128], identity=IDF[0:n, 0:n]),
                             r=[rs_aux[si], rs_const], w=[bankres[b]], inc=(cc == 3))
                    tl = tiles_all[min((colbase + t0) // 512, 4)]
                    for cc in range(4):
                        c = half * 4 + cc
                        eng = "dve" if (cc % 2 == 0 or DBG_XV == "dve") else "act"
                        if eng == "dve":
                            K.op("dve", lambda: nc.vector.tensor_copy(out=xv(c, colbase + t0, n), in_=PSB[b][:, cc * 128: cc * 128 + n]),
                                 r=[bankres[b]], w=[xres(tl, c)])
                        else:
                            K.op("act", lambda: nc.scalar.activation(out=xv(c, colbase + t0, n), in_=PSB[b][:, cc * 128: cc * 128 + n], func=AF.Identity),
                                 r=[bankres[b]], w=[xres(tl, c)])

        def norm_parts(t, l, s, gain_name, hdst, hres, tb=4352):
            n = t.n
            XSQ, RSTD, SQT, NT0, (rs_xsq, rs_rstd, rs_nt0) = norm_tmps(tb)
            st = {}

            def p1():
                for c in range(KC):
                    K.op("act", lambda: nc.scalar.activation(out=XSQ[:, c * 512: c * 512 + n], in_=xv(c, t.col0, n), func=AF.Square),
                         r=[xres(t, c)], w=[rs_xsq])

            def p2():
                b = bank("m")
                st["b"] = b
                for c in range(KC):
                    K.op("pe", lambda: nc.tensor.matmul(PSB[b][:, 0:n], lhsT=ONESB[:], rhs=XSQ[:, c * 512: c * 512 + n],
                                                        start=(c == 0), stop=(c == KC - 1)),
                         r=[rs_xsq, rs_const], w=[bankres[b]], inc=(c == KC - 1))
                K.op("act", lambda: nc.scalar.activation(out=SQT[:, 0:n], in_=PSB[b][:, 0:n], func=AF.Sqrt,
                                                         bias=EPSC, scale=1.0 / D),
                     r=[bankres[b], rs_const], w=[rs_rstd])

            def p3():
                K.op("dve", lambda: nc.vector.reciprocal(out=RSTD[:, 0:n], in_=SQT[:, 0:n]), r=[rs_rstd], w=[rs_rstd])
                for c in range(KC):
                    tmp = NT0[c % 2]
                    rtmp = rs_nt0[c % 2]
                    K.op("dve", lambda: nc.vector.tensor_tensor(out=tmp[:, 0:n], in0=xv(c, t.col0, n), in1=RSTD[:, 0:n], op=ALU.mult),
                         r=[xres(t, c), rs_rstd], w=[rtmp])
                    if t.kind == "p":
                        K.op("act", lambda: nc.scalar.activation(out=hdst(c), in_=tmp[:, 0:n], func=AF.Identity,
                                                                 scale=abg(s, 0, c, 0, 1), bias=abg(s, 1, c, 0, 1)),
                             r=[rtmp, rs_abg], w=[hres])
                    else:
                        K.op("dve", lambda: nc.vector.tensor_tensor(out=v3(tmp[:, 0:n], NSQ, TS), in0=v3(tmp[:, 0:n], NSQ, TS),
                                                                    in1=bc_seq(abg(s, 0, c, 1, NSQ), TS), op=ALU.mult),
                             r=[rs_abg, rtmp], w=[rtmp])
                        K.op("dve", lambda: nc.vector.tensor_tensor(out=v3(hdst(c), NSQ, TS), in0=v3(tmp[:, 0:n], NSQ, TS),
                                                                    in1=bc_seq(abg(s, 1, c, 1, NSQ), TS), op=ALU.add),
                             r=[rtmp, rs_abg], w=[hres])
            return [p1, p2, p3]

        def norm_mod(t, l, s, gain_name, hdst, hres, tb=4352):
            for pfn in norm_parts(t, l, s, gain_name, hdst, hres, tb):
                pfn()

        def norm_group(tiles, l, s, gain_name, tb):
            pp = [norm_parts(t, l, s, gain_name, lambda c, t=t: hv(c, t.loc, t.n), hres_of(t), tb) for t in tiles]
            nT = len(pp)
            pp[0][0](); pp[0][1]()
            for i in range(1, nT):
                pp[i][0]()
                pp[i - 1][2]()
                pp[i][1]()
            pp[nT - 1][2]()


        def residual(t, s, d, b):
            n = t.n
            if t.kind == "p":
                K.op("dve", lambda: nc.vector.scalar_tensor_tensor(out=xv(d, t.col0, n), in0=PSB[b][:, 0:n], scalar=abg(s, 2, d, 0, 1),
                                                                   in1=xv(d, t.col0, n), op0=ALU.mult, op1=ALU.add),
                     r=[bankres[b], rs_abg], w=[xres(t, d)])
            else:
                tmp = T64[d % 2]
                rtmp = rs_auxh[4 + 2 * (d % 2)]
                K.op("dve", lambda: nc.vector.tensor_tensor(out=v3(tmp[:, 0:n], NSQ, TS), in0=v3(PSB[b][:, 0:n], NSQ, TS),
                                                            in1=bc_seq(abg(s, 2, d, 1, NSQ), TS), op=ALU.mult),
                     r=[bankres[b], rs_abg], w=[rtmp])
                K.op("dve", lambda: nc.vector.tensor_tensor(out=xv(d, t.col0, n), in0=xv(d, t.col0, n), in1=tmp[:, 0:n], op=ALU.add),
                     r=[rtmp], w=[xres(t, d)])

        def hres_of(t):
            return res("h_%d" % t.loc)

        MTS = [small(4 * NSEQ), small(4 * NSEQ)]
        rs_mts = [res("ada_mt0"), res("ada_mt1")]
        ada_q = []
        ada_evac = []
        ada_n = [0]

        def ada_load(l, j):
            wsrc = wada_d[l].rearrange("(k p) f -> p k f", p=128)
            s_ = ring_load(lambda slot: [(slot[:, 0:4096].rearrange("p (k f) -> p k f", k=8), wsrc[:, :, j * 512:(j + 1) * 512])], ada=True)
            ada_loaded.append((l, j, s_))

        def ada_mm(l, j, s_):
            slot3 = RING[s_][:, 0:4096].rearrange("p (k f) -> p k f", k=8)
            b = bank("m")
            mi = ada_n[0] % 2
            ada_n[0] += 1
            for i in range(4):
                for k in range(KC):
                    K.op("pe", lambda: nc.tensor.matmul(PSB[b][:, i * 32: i * 32 + NSEQ], lhsT=slot3[:, k, i * 128:(i + 1) * 128],
                                                        rhs=SILUC[:, k * NSEQ:(k + 1) * NSEQ], start=(k == 0), stop=(k == KC - 1)),
                         r=[ringres[s_], rs_siluc], w=[bankres[b]], inc=(k == KC - 1 and i == 3))
            for i in range(4):
                oc = 4 * j + i
                K.op("act", lambda: nc.scalar.activation(out=MTS[mi][:, i * NSEQ:(i + 1) * NSEQ], in_=PSB[b][:, i * 32: i * 32 + NSEQ],
                                                         func=AF.Identity, bias=ptc(l, "b_ada", oc), scale=1.0),
                     r=[bankres[b], rs_pt], w=[rs_mts[mi]])
            ada_evac.append((l, j, mi))

        def ada_derive(l, j, mi):
            gains = ["norm_ffn1", "norm_mix", "norm_ffn2"]
            for i in range(4):
                oc = 4 * j + i
                m = oc // 8
                c = oc % 8
                sub, which = m // 3, m % 3
                src = MTS[mi][:, i * NSEQ:(i + 1) * NSEQ]
                if which == 0:
                    K.op("dve", lambda: nc.vector.tensor_copy(out=abg(sub, 1, c, 0, NSEQ), in_=src), r=[rs_mts[mi]], w=[rs_abg])
                elif which == 1:
                    K.op("dve", lambda: nc.vector.tensor_scalar(out=abg(sub, 0, c, 0, NSEQ), in0=src, scalar1=1.0,
                                                                scalar2=ptc(l, gains[sub], c), op0=ALU.add, op1=ALU.mult),
                         r=[rs_mts[mi], rs_pt], w=[rs_abg])
                else:
                    K.op("dve", lambda: nc.vector.tensor_scalar(out=abg(sub, 2, c, 0, NSEQ), in0=src, scalar1=(1.0 if sub == 1 else 0.5),
                                                                scalar2=None, op0=ALU.mult),
                         r=[rs_mts[mi]], w=[rs_abg])

        def ada_enqueue(l, j0, j1):
            for j in range(j0, j1):
                ada_q.append((l, j))

        def ada_pump(n=1):
            for _ in range(n):
                if ada_evac:
                    ada_derive(*ada_evac.pop(0))
                if ada_loaded:
                    ada_mm(*ada_loaded.pop(0))
                if ada_q:
                    ada_load(*ada_q.pop(0))

        def ada_flush():
            while ada_q or ada_loaded or ada_evac:
                ada_pump(1)

        def ffn(l, which, s, gain_name, group, do_norm=True, next_group=None):
            w1s = w1_d[which][l].rearrange("(k p) f -> p k f", p=128)
            w3s = w3_d[which][l].rearrange("(k p) f -> p k f", p=128)
            w2s = w2_d[which][l].rearrange("(k p) f -> p k f", p=128)
            if do_norm:
                K.barrier()
                norm_group(group, l, s, gain_name, 4352)
            def load_item(i):
                if i < 11:
                    return ring_load(lambda slot: [
                        (slot[:, 0:2048].rearrange("p (k f) -> p k f", k=8), w1s[:, :, i * 256:(i + 1) * 256]),
                        (slot[:, 2048:4096].rearrange("p (k f) -> p k f", k=8), w3s[:, :, i * 256:(i + 1) * 256])])
                d_ = i - 11
                return ring_load(lambda slot: [(slot[:, 0:2816].rearrange("p (k f) -> p k f", k=FC), w2s[:, :, d_ * 128:(d_ + 1) * 128])])
            slots_ = {0: load_item(0)}
            for j in range(11):
                slots_[j + 1] = load_item(j + 1)
                sl = slots_[j]
                s1 = RING[sl][:, 0:2048].rearrange("p (k f) -> p k f", k=8)
                s3 = RING[sl][:, 2048:4096].rearrange("p (k f) -> p k f", k=8)
                for t in group:
                    n = t.n
                    for f2 in range(2):
                        f = 2 * j + f2
                        b1 = bank("g1"); b3 = bank("g3")
                        for k in range(KC):
                            K.op("pe", lambda: nc.tensor.matmul(PSB[b1][:, 0:n], lhsT=s1[:, k, f2 * 128:(f2 + 1) * 128], rhs=hv(k, t.loc, n),
                                                                start=(k == 0), stop=(k == KC - 1)),
                                 r=[ringres[sl], hres_of(t)], w=[bankres[b1]], inc=(k == KC - 1))
                        for k in range(KC):
                            K.op("pe", lambda: nc.tensor.matmul(PSB[b3][:, 0:n], lhsT=s3[:, k, f2 * 128:(f2 + 1) * 128], rhs=hv(k, t.loc, n),
                                                                start=(k == 0), stop=(k == KC - 1)),
                                 r=[ringres[sl], hres_of(t)], w=[bankres[b3]], inc=(k == KC - 1))
                        si = role_i["g1"] % 2
                        stmp = S32[si]; rst = rs_auxh[2 * si]
                        K.op("act", lambda: nc.scalar.activation(out=stmp[:, 0:n], in_=PSB[b1][:, 0:n], func=AF.Silu),
                             r=[bankres[b1]], w=[rst])
                        K.op("dve", lambda: nc.vector.tensor_tensor(out=av(f, t.loc, n), in0=stmp[:, 0:n], in1=PSB[b3][:, 0:n], op=ALU.mult),
                             r=[rst, bankres[b3]], w=[res("a_%d_%d" % (f, t.loc))])
            for d in range(KC):
                if d + 1 < KC:
                    slots_[11 + d + 1] = load_item(11 + d + 1)
                if next_group is not None:
                    if d == 0:
                        hoist = []
                        for ti_, t in enumerate(next_group):
                            pp = norm_parts(t, l, s, gain_name, lambda c, t=t: hv(c, t.loc, t.n), hres_of(t), tb=-1)
                            for pi2, pfn in enumerate(pp):
                                hoist.append((2 * ti_ + pi2, pfn))
                    for (dd, pfn) in hoist:
                        if dd == d:
                            pfn()
                sl = slots_[11 + d]
                s2 = RING[sl][:, 0:2816].rearrange("p (k f) -> p k f", k=FC)
                for t in group:
                    n = t.n
                    b = bank("o")
                    for f in range(FC):
                        K.op("pe", lambda: nc.tensor.matmul(PSB[b][:, 0:n], lhsT=s2[:, f, :], rhs=av(f, t.loc, n),
                                                            start=(f == 0), stop=(f == FC - 1)),
                             r=[ringres[sl], res("a_%d_%d" % (f, t.loc))], w=[bankres[b]], inc=(f == FC - 1))
                    residual(t, s, d, b)

        def mod_2pi(x, m, kmax, rt):
            k = kmax
            while k >= 1:
                sk = float(k) * TWO_PI
                K.op("dve", lambda: nc.vector.tensor_scalar(out=m, in0=x, scalar1=sk, scalar2=None, op0=ALU.is_ge), r=[rt], w=[rt])
                K.op("dve", lambda: nc.vector.scalar_tensor_tensor(out=x, in0=m, scalar=-sk, in1=x, op0=ALU.mult, op1=ALU.add), r=[rt], w=[rt])
                k //= 2

        def sin_reduced(dst, src, n, tmp1, tmp2i, tmp3, rr, rw, add_half_pi=False):
            rt = res("sinred_tmp")
            K.op("dve", lambda: nc.vector.tensor_scalar(out=tmp1, in0=src, scalar1=(PI / 2 if add_half_pi else 0.0), scalar2=None, op0=ALU.add),
                 r=rr, w=[rt])
            mod_2pi(tmp1, tmp3, 128, rt)
            K.op("act", lambda: nc.scalar.activation(out=dst, in_=tmp1, func=AF.Sin, scale=-1.0, bias=PIC), r=[rt, rs_const], w=rw)


        def cmul(dr, di, ar, ai, br, bi, t1, t2, rr, rw, rtmp):
            K.op("dve", lambda: nc.vector.tensor_tensor(out=t1, in0=ar, in1=br, op=ALU.mult), r=rr, w=[rtmp])
            K.op("dve", lambda: nc.vector.tensor_tensor(out=t2, in0=ai, in1=bi, op=ALU.mult), r=rr, w=[rtmp])
            K.op("dve", lambda: nc.vector.tensor_tensor(out=dr, in0=t1, in1=t2, op=ALU.subtract), r=[rtmp], w=rw)
            K.op("dve", lambda: nc.vector.tensor_tensor(out=t1, in0=ar, in1=bi, op=ALU.mult), r=rr, w=[rtmp])
            K.op("dve", lambda: nc.vector.tensor_tensor(out=t2, in0=ai, in1=br, op=ALU.mult), r=rr, w=[rtmp])
            K.op("dve", lambda: nc.vector.tensor_tensor(out=di, in0=t1, in1=t2, op=ALU.add), r=[rtmp], w=rw)

        rs_wbu = res("wbu"); rs_wc = res("wc"); rs_wg = res("wg")
        ds_w = K.dsem(); ds_w2 = K.dsem(); ds_st = K.dsem(); ds_bn = K.dsem()
        rs_fst = res("fst")
        rs_xbs = res("xbs"); rs_h0t = res("h0t"); rs_s0 = res("s0")
        rs_hc = res("hc"); rs_carry = res("carry")
        rs_work_setup = res("work_setup")

        def mixer_setup(l):
            K.barrier()
            K.dma("pool", ds_w, [(WG[:], wg_d[l])], w=[rs_wg])
            K.dma("pool", ds_w2, [(WC[:], cpad_d[l])], w=[rs_wc])
            K.op("dve", lambda: nc.vector.tensor_copy(out=A_RE, in_=ptc(l, "s5_a_re", 0, 16)), r=[rs_pt], w=[rs_small])
            K.op("dve", lambda: nc.vector.tensor_copy(out=A_IM, in_=ptc(l, "s5_a_im", 0, 16)), r=[rs_pt], w=[rs_small])
            K.dma("sp", ds_misc, [(DTX, dtx_d[l])], w=[rs_small])
            K.op("act", lambda: nc.scalar.activation(out=DT, in_=DTX, func=AF.Exp), r=[rs_small], w=[rs_small])
            K.op("dve", lambda: nc.vector.tensor_scalar(out=ARC, in0=A_RE, scalar1=-1e-4, scalar2=None, op0=ALU.min), r=[rs_small], w=[rs_small])
            K.op("dve", lambda: nc.vector.tensor_tensor(out=TMPA, in0=ARC, in1=DT, op=ALU.mult), r=[rs_small], w=[rs_small])
            K.op("act", lambda: nc.scalar.activation(out=RHO, in_=TMPA, func=AF.Exp), r=[rs_small], w=[rs_small])
            K.op("dve", lambda: nc.vector.tensor_tensor(out=TH, in0=A_IM, in1=DT, op=ALU.mult), r=[rs_small], w=[rs_small])
            K.op("dve", lambda: nc.vector.tensor_scalar(out=TH, in0=TH, scalar1=4.0 * TWO_PI, scalar2=None, op0=ALU.add), r=[rs_small], w=[rs_small])
            mod_2pi(TH, TMPA, 16, rs_small)
            PHI = WORK[:, 0:16 * 129]
            T1 = WORK[:, 2064:2064 + 2064]
            T2 = WORK[:, 4128:4128 + 2064]
            T3 = WORK[:, 6192:6192 + 2064]
            K.op("dve", lambda: nc.vector.tensor_tensor(out=v3(PHI, 16, 129), in0=TH.unsqueeze(2).to_broadcast([128, 16, 129]),
                                                        in1=TAU[:, :].unsqueeze(1).to_broadcast([128, 16, 129]), op=ALU.mult),
                 r=[rs_small, rs_const], w=[rs_work_setup])
            rt_ = res("sinred_tmp")
            K.op("dve", lambda: nc.vector.tensor_copy(out=T1, in_=PHI), r=[rs_work_setup], w=[rt_])
            mod_2pi(T1, T3, 64, rt_)
            K.op("act", lambda: nc.scalar.activation(out=SIN, in_=T1, func=AF.Sin, scale=-1.0, bias=PIC), r=[rt_, rs_const], w=[rs_tab])
            K.op("dve", lambda: nc.vector.tensor_scalar(out=T2, in0=T1, scalar1=PI / 2, scalar2=None, op0=ALU.add), r=[rt_], w=[rt_])
            mod_2pi(T2, T3, 1, rt_)
            K.op("act", lambda: nc.scalar.activation(out=COS, in_=T2, func=AF.Sin, scale=-1.0, bias=PIC), r=[rt_, rs_const], w=[rs_tab])
            c1 = cosv(0, 16, 1, 1).rearrange("p o t -> p (o t)")
            s1 = sinv(0, 16, 1, 1).rearrange("p o t -> p (o t)")
            K.op("dve", lambda: nc.vector.tensor_tensor(out=ABR, in0=RHO, in1=c1, op=ALU.mult), r=[rs_small, rs_tab], w=[rs_small])
            K.op("dve", lambda: nc.vector.tensor_tensor(out=ABI, in0=RHO, in1=s1, op=ALU.mult), r=[rs_small, rs_tab], w=[rs_small])
            K.op("dve", lambda: nc.vector.tensor_tensor(out=TMPA, in0=ARC, in1=ARC, op=ALU.mult), r=[rs_small], w=[rs_small])
            K.op("dve", lambda: nc.vector.tensor_tensor(out=TMPB, in0=A_IM, in1=A_IM, op=ALU.mult), r=[rs_small], w=[rs_small])
            K.op("dve", lambda: nc.vector.tensor_tensor(out=TMPA, in0=TMPA, in1=TMPB, op=ALU.add), r=[rs_small], w=[rs_small])
            K.op("dve", lambda: nc.vector.reciprocal(out=TMPA, in_=TMPA), r=[rs_small], w=[rs_small])
            K.op("dve", lambda: nc.vector.tensor_scalar(out=TMPB, in0=ABR, scalar1=-1.0, scalar2=None, op0=ALU.add), r=[rs_small], w=[rs_small])
            K.op("dve", lambda: nc.vector.tensor_tensor(out=TMPC, in0=TMPB, in1=ARC, op=ALU.mult), r=[rs_small], w=[rs_small])
            K.op("dve", lambda: nc.vector.tensor_tensor(out=TMPD, in0=ABI, in1=A_IM, op=ALU.mult), r=[rs_small], w=[rs_small])
            K.op("dve", lambda: nc.vector.tensor_tensor(out=TMPC, in0=TMPC, in1=TMPD, op=ALU.add), r=[rs_small], w=[rs_small])
            K.op("dve", lambda: nc.vector.tensor_tensor(out=F_R, in0=TMPC, in1=TMPA, op=ALU.mult), r=[rs_small], w=[rs_small])
            K.op("dve", lambda: nc.vector.tensor_tensor(out=TMPC, in0=ABI, in1=ARC, op=ALU.mult), r=[rs_small], w=[rs_small])
            K.op("dve", lambda: nc.vector.tensor_tensor(out=TMPD, in0=TMPB, in1=A_IM, op=ALU.mult), r=[rs_small], w=[rs_small])
            K.op("dve", lambda: nc.vector.tensor_tensor(out=TMPC, in0=TMPC, in1=TMPD, op=ALU.subtract), r=[rs_small], w=[rs_small])
            K.op("dve", lambda: nc.vector.tensor_tensor(out=F_I, in0=TMPC, in1=TMPA, op=ALU.mult), r=[rs_small], w=[rs_small])
            BNR = WORK[:, 8256: 8256 + 2048]
            BNI = WORK[:, 10304: 10304 + 2048]
            U1 = WORK[:, 12352: 12352 + 2048]
            U2 = WORK[:, 14400: 14400 + 1920]
            rs_bn = res("bnat")
            K.dma("sp", ds_bn, [(BNR, bnat_d[l, 0]), (BNI, bnat_d[l, 1])], w=[rs_bn])
            BBR = wbf(0, 2048 * 1)
            BBI = wbf(1024, 2048 * 1)
            rs_bb = res("bb")
            frb = F_R.unsqueeze(2).to_broadcast([128, 16, 128])
            fib = F_I.unsqueeze(2).to_broadcast([128, 16, 128])
            for hh in range(2):
                o0 = hh * 8
                sl_ = slice(o0 * 128, (o0 + 8) * 128)
                fr_h = F_R[:, o0:o0 + 8].unsqueeze(2).to_broadcast([128, 8, 128])
                fi_h = F_I[:, o0:o0 + 8].unsqueeze(2).to_broadcast([128, 8, 128])
                u1 = v3(U1[:, 0:1024], 8, 128); u2 = v3(U2[:, 0:1024], 8, 128)
                bnr = v3(BNR[:, sl_], 8, 128); bni = v3(BNI[:, sl_], 8, 128)
                rtu = res("bb_tmp")
                K.op("dve", lambda: nc.vector.tensor_tensor(out=u1, in0=bnr, in1=fr_h, op=ALU.mult), r=[rs_bn, rs_small, rs_tab], w=[rtu])
                K.op("dve", lambda: nc.vector.tensor_tensor(out=u2, in0=bni, in1=fi_h, op=ALU.mult), r=[rs_bn, rs_small], w=[rtu])
                K.op("dve", lambda: nc.vector.tensor_tensor(out=v3(BBR[:, sl_], 8, 128), in0=u1, in1=u2, op=ALU.subtract), r=[rtu], w=[rs_bb])
                K.op("dve", lambda: nc.vector.tensor_tensor(out=u1, in0=bnr, in1=fi_h, op=ALU.mult), r=[rs_bn, rs_small], w=[rtu])
                K.op("dve", lambda: nc.vector.tensor_tensor(out=u2, in0=bni, in1=fr_h, op=ALU.mult), r=[rs_bn, rs_small], w=[rtu])
                K.op("dve", lambda: nc.vector.tensor_tensor(out=v3(BBI[:, sl_], 8, 128), in0=u1, in1=u2, op=ALU.add), r=[rtu], w=[rs_bb])
            for comp, BB in ((0, BBR), (1, BBI)):
                for o8 in range(2):
                    b = bank("m")
                    pb = PSB[b][:, :].bitcast(BF16)
                    for oi in range(8):
                        o = o8 * 8 + oi
                        K.op("pe", lambda: nc.tensor.transpose(out=pb[:, oi * 128:(oi + 1) * 128], in_=BB[:, o * 128:(o + 1) * 128], identity=IDB[:]),
                             r=[rs_bb, rs_const], w=[bankres[b]], inc=(oi == 7))
                    base = (comp * 16 + o8 * 8) * 128
                    K.op("dve", lambda: nc.vector.tensor_copy(out=WBU[:, base: base + 1024], in_=pb[:, 0:1024]), r=[bankres[b]], w=[rs_wbu])
            K.op("act", lambda: nc.scalar.activation(out=SPT, in_=ptc(l, "lru_lambda", 0, 4), func=AF.Exp, scale=-1.0), r=[rs_pt], w=[rs_small])
            K.op("dve", lambda: nc.vector.tensor_scalar(out=SPT, in0=SPT, scalar1=1.0, scalar2=None, op0=ALU.add), r=[rs_small], w=[rs_small])
            K.op("act", lambda: nc.scalar.activation(out=SPT, in_=SPT, func=AF.Ln), r=[rs_small], w=[rs_small])
            K.op("dve", lambda: nc.vector.tensor_scalar(out=SC1, in0=SPT, scalar1=-8.0, scalar2=None, op0=ALU.mult), r=[rs_small], w=[rs_small])
            K.op("dve", lambda: nc.vector.tensor_scalar(out=SC2, in0=SPT, scalar1=-16.0, scalar2=None, op0=ALU.mult), r=[rs_small], w=[rs_small])
            K.op("dve", lambda: nc.vector.tensor_scalar(out=SCH, in0=SPT, scalar1=-4.0, scalar2=None, op0=ALU.mult), r=[rs_small], w=[rs_small])
            K.op("dve", lambda: nc.vector.tensor_scalar(out=HBG[:, 0:4], in0=ptc(l, "b_rg", 0, 4), scalar1=0.5, scalar2=None, op0=ALU.mult), r=[rs_pt], w=[rs_small])
            K.op("dve", lambda: nc.vector.tensor_scalar(out=HBG[:, 4:8], in0=ptc(l, "b_ig", 0, 4), scalar1=0.5, scalar2=None, op0=ALU.mult), r=[rs_pt], w=[rs_small])
            STG = WORK[0:NSQ, 8256: 8256 + 2048]
            xbs4 = XBS.rearrange("p (c s k) -> p c s k", c=4, s=NSQ, k=7)
            K.dma("sp", ds_st, [(STG[:, 0:1536], stc_d[l])], r=[], w=[rs_bn])
            for k3 in range(3):
                b = bank("m")
                for c in range(4):
                    K.op("pe", lambda: nc.tensor.transpose(out=PSB[b][:, c * 16:(c + 1) * 16], in_=STG[:, k3 * 512 + c * 128: k3 * 512 + (c + 1) * 128],
                                                           identity=IDF[0:NSQ, 0:NSQ]),
                         r=[rs_bn, rs_const], w=[bankres[b]], inc=(c == 3))
                K.op("dve", lambda: nc.vector.tensor_copy(out=xbs4[:, :, :, k3], in_=v3(PSB[b][:, 0:64], 4, 16)), r=[bankres[b]], w=[rs_xbs])
            K.dma("sp", ds_st, [(STG[:, 0:512], sth_d[l])], w=[rs_bn])
            b = bank("m")
            for c in range(4):
                K.op("pe", lambda: nc.tensor.transpose(out=PSB[b][:, c * 16:(c + 1) * 16], in_=STG[:, c * 128:(c + 1) * 128], identity=IDF[0:NSQ, 0:NSQ]),
                     r=[rs_bn, rs_const], w=[bankres[b]], inc=(c == 3))
            K.op("dve", lambda: nc.vector.tensor_copy(out=H0T, in_=PSB[b][:, 0:64]), r=[bankres[b]], w=[rs_h0t])
            for (srcd, dst) in ((str_d, S0R), (sti_d, S0I)):
                K.dma("sp", ds_st, [(STG[:, 0:2048], srcd[l])], w=[rs_bn])
                b = bank("m")
                for o in range(16):
                    K.op("pe", lambda: nc.tensor.transpose(out=PSB[b][:, o * 16:(o + 1) * 16], in_=STG[:, o * 128:(o + 1) * 128], identity=IDF[0:NSQ, 0:NSQ]),
                         r=[rs_bn, rs_const], w=[bankres[b]], inc=(o == 15))
                K.op("dve", lambda: nc.vector.tensor_copy(out=dst, in_=PSB[b][:, 0:256]), r=[bankres[b]], w=[rs_s0])
            c1b = cosv(0, 16, 1, 1).to_broadcast([128, 16, NSQ])
            s1b = sinv(0, 16, 1, 1).to_broadcast([128, 16, NSQ])
            t1 = v3(TAILX[:, 0:256], 16, NSQ); t2 = v3(TAILX[:, 256:512], 16, NSQ)
            cmul(v3(S0TR, 16, NSQ), v3(S0TI, 16, NSQ), v3(S0R, 16, NSQ), v3(S0I, 16, NSQ), c1b, s1b, t1, t2,
                 [rs_s0, rs_tab], [rs_s0], res("tailx"))
            K.op("dve", lambda: nc.vector.memset(HC, 0.0), w=[rs_hc])
            K.op("dve", lambda: nc.vector.memset(CR, 0.0), w=[rs_carry])
            K.op("dve", lambda: nc.vector.memset(CI, 0.0), w=[rs_carry])

        def mixer_group(l, gi, group):
            s = 1
            K.barrier()
            wins = win_d[l].rearrange("(k p) f -> p k f", p=128)
            norm_group(group, l, s, "norm_mix", 8716)
            xbs4 = XBS.rearrange("p (c s k) -> p c s k", c=4, s=NSQ, k=7)
            for part in range(3):
                sl = ring_load(lambda slot: [(slot[:, 0:4096].rearrange("p (k f) -> p k f", k=8), wins[:, :, part * 512:(part + 1) * 512])])
                s3 = RING[sl][:, 0:4096].rearrange("p (k f) -> p k f", k=8)
                for t in group:
                    n = t.n
                    for oc in range(4):
                        role = ("g1", "g3", "o")[(oc + part) % 3]
                        b = bank(role)
                        for k in range(KC):
                            K.op("pe", lambda: nc.tensor.matmul(PSB[b][:, 0:n], lhsT=s3[:, k, oc * 128:(oc + 1) * 128], rhs=hv(k, t.loc, n),
                                                                start=(k == 0), stop=(k == KC - 1)),
                                 r=[ringres[sl], hres_of(t)], w=[bankres[b]], inc=(k == KC - 1))
                        if part == 0:
                            if t.kind == "p":
                                K.op("dve", lambda: nc.vector.tensor_copy(out=xbv(oc, 3 + t.loc, n), in_=PSB[b][:, 0:n]),
                                     r=[bankres[b]], w=[res("xb_%d" % t.loc)])
                            else:
                                K.op("dve", lambda: nc.vector.tensor_copy(out=xbs4[:, oc, :, 3:7], in_=v3(PSB[b][:, 0:n], NSQ, TS)),
                                     r=[bankres[b]], w=[rs_xbs])
                        elif part == 1:
                            K.op("act", lambda: nc.scalar.activation(out=gyv(oc, t.loc, n), in_=PSB[b][:, 0:n], func=AF.Gelu_apprx_tanh),
                                 r=[bankres[b]], w=[res("gy_%d" % t.loc)])
                        else:
                            K.op("act", lambda: nc.scalar.activation(out=uv(oc, t.loc, n), in_=PSB[b][:, 0:n], func=AF.Identity),
                                 r=[bankres[b]], w=[res("u_%d" % t.loc)])
            K.barrier()
            if gi == 0:
                K.op("dve", lambda: nc.vector.memset(XB.rearrange("p (c w) -> p c w", c=4)[:, :, 0:3], 0.0), w=[res("xb_hist")])
            else:
                K.op("dve", lambda: nc.vector.tensor_copy(out=XB.rearrange("p (c w) -> p c w", c=4)[:, :, 0:3], in_=v3(HIST, 4, 3)),
                     r=[res("hist_save")], w=[res("xb_hist")])
            NL = 256
            LSETS = []
            for si_ in range(4):
                base = si_ * 1024
                xcbw = TAILX[:, si_ * 128:(si_ + 1) * 128].bitcast(BF16)
                LSETS.append(([WORK[:, base + i * NL: base + (i + 1) * NL] for i in range(4)], xcbw, res("lru_set%d" % si_)))
            lunits = []
            for t in group:
                subs = [(t.loc + i * NL, NL) for i in range(t.n // NL)] if t.kind == "p" else [(t.loc, t.n)]
                for (sloc, n) in subs:
                    for c in range(4):
                        lunits.append((t, sloc, n, c))
            batches = [lunits[i:i + 2] for i in range(0, len(lunits), 2)]

            def lru_A(k):
                for bi_, (t, sloc, n, c) in enumerate(batches[k]):
                    bufs, xcbf, rlt = LSETS[2 * (k % 2) + bi_]
                    XC, RG, IG, AA = [x[:, 0:n] for x in bufs]
                    xcb = xcbf[:, 0:n]
                    if t.kind == "p":
                        srcs = [xbv(c, sloc + kk, n) for kk in range(4)]
                        rsrc = [res("xb_%d" % t.loc)]
                        if sloc == t.loc:
                            rsrc.append(res("xb_%d" % (t.loc - 512)) if t.loc > 0 else res("xb_hist"))
                        shp = lambda a: a
                    else:
                        srcs = [xbs4[:, c, :, kk:kk + 4] for kk in range(4)]
                        rsrc = [rs_xbs]
                        shp = lambda a: v3(a, NSQ, TS)
                    K.op("dve", lambda: nc.vector.tensor_scalar(out=shp(XC), in0=srcs[3], scalar1=ptc(l, "conv_w", 3 * 4 + c),
                                                                scalar2=ptc(l, "conv_b", c), op0=ALU.mult, op1=ALU.add),
                         r=rsrc + [rs_pt], w=[rlt])
                    for kk in range(3):
                        K.op("dve", lambda: nc.vector.scalar_tensor_tensor(out=shp(XC), in0=srcs[kk], scalar=ptc(l, "conv_w", kk * 4 + c),
                                                                           in1=shp(XC), op0=ALU.mult, op1=ALU.add),
                             r=rsrc + [rs_pt, rlt], w=[rlt])
                    K.op("dve", lambda: nc.vector.tensor_copy(out=xcb, in_=XC), r=[rlt], w=[rlt])
                    b1 = bank("g1"); b3 = bank("g3")
                    K.op("pe", lambda: nc.tensor.matmul(PSB[b1][:, 0:n], lhsT=WG[:, c * 128:(c + 1) * 128], rhs=xcb, start=True, stop=True),
                         r=[rlt, rs_wg], w=[bankres[b1]])
                    K.op("pe", lambda: nc.tensor.matmul(PSB[b3][:, 0:n], lhsT=WG[:, (4 + c) * 128:(5 + c) * 128], rhs=xcb, start=True, stop=True),
                         r=[rlt, rs_wg], w=[bankres[b3]])
                    K.op("act", lambda: nc.scalar.activation(out=RG, in_=PSB[b1][:, 0:n], func=AF.Tanh, bias=HBG[:, c:c + 1], scale=0.5),
                         r=[bankres[b1], rs_small], w=[rlt])
                    K.op("act", lambda: nc.scalar.activation(out=IG, in_=PSB[b3][:, 0:n], func=AF.Tanh, bias=HBG[:, 4 + c:5 + c], scale=0.5),
                         r=[bankres[b3], rs_small], w=[rlt])
                    K.op("act", lambda: nc.scalar.activation(out=AA, in_=RG, func=AF.Exp, scale=SCH[:, c:c + 1], bias=SCH[:, c:c + 1]),
                         r=[rlt, rs_small], w=[rlt])
                    K.op("act", lambda: nc.scalar.activation(out=RG, in_=RG, func=AF.Exp, scale=SC1[:, c:c + 1], bias=SC1[:, c:c + 1]),
                         r=[rlt, rs_small], w=[rlt])

            def lru_B(k):
                for bi_, (t, sloc, n, c) in enumerate(batches[k]):
                    bufs, xcbf, rlt = LSETS[2 * (k % 2) + bi_]
                    MM = bufs[1][:, 0:n]
                    K.op("act", lambda: nc.scalar.activation(out=MM, in_=MM, func=AF.Sqrt, scale=-1.0, bias=ONEC), r=[rlt, rs_const], w=[rlt])

            def lru_C(k):
                for bi_, (t, sloc, n, c) in enumerate(batches[k]):
                    bufs, xcbf, rlt = LSETS[2 * (k % 2) + bi_]
                    XC, MM, IG, AA = [x[:, 0:n] for x in bufs]
                    BBt = IG; HS = MM
                    K.op("dve", lambda: nc.vector.tensor_scalar(out=IG, in0=IG, scalar1=0.5, scalar2=0.5, op0=ALU.mult, op1=ALU.add), r=[rlt], w=[rlt])
                    K.op("dve", lambda: nc.vector.tensor_tensor(out=BBt, in0=MM, in1=IG, op=ALU.mult), r=[rlt], w=[rlt])
                    K.op("dve", lambda: nc.vector.tensor_tensor(out=BBt, in0=BBt, in1=XC, op=ALU.mult), r=[rlt], w=[rlt])
                    if t.kind == "p":
                        K.op("dve", lambda: nc.vector.tensor_tensor_scan(out=HS, data0=AA, data1=BBt, initial=HC[:, c:c + 1],
                                                                         op0=ALU.mult, op1=ALU.add), r=[rlt, rs_hc], w=[rlt])
                        K.op("dve", lambda: nc.vector.tensor_copy(out=HC[:, c:c + 1], in_=HS[:, n - 1:n]), r=[rlt], w=[rs_hc])
                        if t.col0 + (sloc - t.loc) + n == TP:
                            K.op("dve", lambda: nc.vector.tensor_copy(out=FST[:, (12 + c) * NSEQ:(12 + c) * NSEQ + 1], in_=HS[:, n - 1:n]),
                                 r=[rlt], w=[rs_fst])
                            for kk in range(3):
                                K.op("dve", lambda: nc.vector.tensor_copy(out=FST[:, (c * 3 + kk) * NSEQ:(c * 3 + kk) * NSEQ + 1],
                                                                          in_=xbv(c, 3 + sloc + n - 3 + kk, 1)),
                                     r=[res("xb_%d" % t.loc)], w=[rs_fst])
                    else:
                        hs3 = v3(HS, NSQ, TS); aa3 = v3(AA, NSQ, TS); bb3 = v3(BBt, NSQ, TS)
                        for tt in range(TS):
                            prev = H0T[:, c * NSQ:(c + 1) * NSQ] if tt == 0 else hs3[:, :, tt - 1]
                            K.op("dve", lambda: nc.vector.tensor_tensor(out=hs3[:, :, tt], in0=aa3[:, :, tt], in1=prev, op=ALU.mult),
                                 r=[rlt, rs_h0t], w=[rlt])
                            K.op("dve", lambda: nc.vector.tensor_tensor(out=hs3[:, :, tt], in0=hs3[:, :, tt], in1=bb3[:, :, tt], op=ALU.add),
                                 r=[rlt], w=[rlt])
                        K.op("dve", lambda: nc.vector.tensor_copy(out=FST[:, (12 + c) * NSEQ + 1:(12 + c + 1) * NSEQ], in_=hs3[:, :, TS - 1]),
                             r=[rlt], w=[rs_fst])
                        for kk in range(3):
                            K.op("dve", lambda: nc.vector.tensor_copy(out=FST[:, (c * 3 + kk) * NSEQ + 1:(c * 3 + kk + 1) * NSEQ],
                                                                      in_=xbs4[:, c, :, 4 + kk]),
                                 r=[rs_xbs], w=[rs_fst])
                    K.op("dve", lambda: nc.vector.tensor_tensor(out=ylv(c, sloc, n), in0=gyv(c, sloc, n), in1=HS, op=ALU.mult),
                         r=[rlt, res("gy_%d" % t.loc)], w=[res("yl_%d" % t.loc)])

            lru_A(0)
            for k in range(len(batches)):
                lru_B(k)
                if k + 1 < len(batches):
                    lru_A(k + 1)
                lru_C(k)
            if gi == 0:
                xb3 = XB.rearrange("p (c w) -> p c w", c=4)
                K.op("dve", lambda: nc.vector.tensor_copy(out=v3(HIST, 4, 3), in_=xb3[:, :, 1024:1027]),
                     r=[res("xb_512")], w=[res("hist_save")])
            K.barrier()
            Tt = [WORK[:, i * 512:(i + 1) * 512] for i in range(4)]
            BTRf = WORK[:, 2048:2560]; BTIf = WORK[:, 2560:3072]; STRf = WORK[:, 3072:3584]; STIf = WORK[:, 3584:4096]
            PREf = WORK[:, 4096:4224]
            RHOT = WORK[:, 4352:6400]
            SRBs = [wbf(6400, 512), wbf(6656, 512)]
            NSIBs = [wbf(6912, 512), wbf(7168, 512)]
            rT = [res("s5_t%d" % i) for i in range(4)]
            r_btr = res("s5_btr"); r_bti = res("s5_bti"); r_str = res("s5_str"); r_sti = res("s5_sti")
            r_bt = [r_btr, r_bti]; r_st = [r_str, r_sti]; rpre = res("s5_pre"); r_rhot = res("s5_rhot")
            r_slr = res("s5_slr"); r_sli = res("s5_sli")
            r_srb = [res("s5_srb0"), res("s5_srb1")]
            r_crp = res("s5_crp")
            K.op("dve", lambda: nc.vector.tensor_copy(out=v3(RHOT, 16, NT5), in_=RHO.unsqueeze(2).to_broadcast([128, 16, NT5])),
                 r=[rs_small], w=[r_rhot])
            K.op("dve", lambda: nc.vector.memset(v3(RHOT, 16, NT5)[:, :, 0:1], 0.0), w=[r_rhot])
            units = []
            for t in group:
                if t.kind == "p":
                    for i in range(t.n // NT5):
                        for q in range(4):
                            units.append((t, t.loc + i * NT5, q))
            n = NT5
            ubanks = {}

            def emit_bu(ui):
                t, sloc, q = units[ui]
                br_b = bank("g1"); bi_b = bank("g3")
                ubanks[ui] = (br_b, bi_b)
                for oi in range(4):
                    o = 4 * q + oi
                    K.op("pe", lambda: nc.tensor.matmul(PSB[br_b][:, oi * n:(oi + 1) * n], lhsT=WBU[:, o * 128:(o + 1) * 128], rhs=uv(q, sloc, n),
                                                        start=True, stop=True),
                         r=[rs_wbu, res("u_%d" % t.loc)], w=[bankres[br_b]], inc=(oi == 3))
                for oi in range(4):
                    o = 4 * q + oi
                    K.op("pe", lambda: nc.tensor.matmul(PSB[bi_b][:, oi * n:(oi + 1) * n], lhsT=WBU[:, (16 + o) * 128:(17 + o) * 128], rhs=uv(q, sloc, n),
                                                        start=True, stop=True),
                         r=[rs_wbu, res("u_%d" % t.loc)], w=[bankres[bi_b]], inc=(oi == 3))

            pending_post = [None]

            def emit_post():
                if pending_post[0] is None:
                    return
                (t, sloc, q, yb_) = pending_post[0]
                pending_post[0] = None
                K.op("dve", lambda: nc.vector.scalar_tensor_tensor(out=PREf[:, 0:n], in0=uv(q, sloc, n), scalar=ptc(l, "s5_d", q),
                                                                   in1=PSB[yb_][:, 0:n], op0=ALU.mult, op1=ALU.add),
                     r=[bankres[yb_], res("u_%d" % t.loc), rs_pt], w=[rpre])
                K.op("act", lambda: nc.scalar.activation(out=gyv(q, sloc, n), in_=PREf[:, 0:n], func=AF.Gelu_apprx_tanh),
                     r=[rpre], w=[res("gy_%d" % t.loc)])

            if units:
                emit_bu(0)
            for ui, (t, sloc, q) in enumerate(units):
                if ui % 2 == 1:
                    ada_pump(1)
                if ui + 1 < len(units):
                    emit_bu(ui + 1)
                br_b, bi_b = ubanks[ui]
                if q == 0:
                    K.op("dve", lambda: nc.vector.tensor_tensor(out=CRPR, in0=RHO, in1=CR, op=ALU.mult), r=[rs_small, rs_carry], w=[r_crp])
                    K.op("dve", lambda: nc.vector.tensor_tensor(out=CRPI, in0=RHO, in1=CI, op=ALU.mult), r=[rs_small, rs_carry], w=[r_crp])
                cs = cosv(4 * q, 4, 0, n); sn = sinv(4 * q, 4, 0, n)
                sh = lambda a: v3(a, 4, n)
                pbr = sh(PSB[br_b][:, 0:4 * n]); pbi = sh(PSB[bi_b][:, 0:4 * n])
                K.op("dve", lambda: nc.vector.tensor_tensor(out=sh(Tt[0]), in0=pbr, in1=cs, op=ALU.mult), r=[bankres[br_b], rs_tab], w=[rT[0]])
                K.op("dve", lambda: nc.vector.tensor_tensor(out=sh(Tt[1]), in0=pbi, in1=sn, op=ALU.mult), r=[bankres[bi_b], rs_tab], w=[rT[1]])
                K.op("dve", lambda: nc.vector.tensor_tensor(out=sh(Tt[2]), in0=pbi, in1=cs, op=ALU.mult), r=[bankres[bi_b], rs_tab], w=[rT[2]])
                K.op("dve", lambda: nc.vector.tensor_tensor(out=sh(Tt[3]), in0=pbr, in1=sn, op=ALU.mult), r=[bankres[br_b], rs_tab], w=[rT[3]])
                emit_post()
                K.op("dve", lambda: nc.vector.tensor_tensor(out=BTRf, in0=Tt[0], in1=Tt[1], op=ALU.add), r=[rT[0], rT[1]], w=[r_btr])
                K.op("dve", lambda: nc.vector.tensor_tensor(out=BTIf, in0=Tt[2], in1=Tt[3], op=ALU.subtract), r=[rT[2], rT[3]], w=[r_bti])
                K.op("dve", lambda: nc.vector.tensor_tensor(out=sh(BTRf)[:, :, 0], in0=sh(BTRf)[:, :, 0], in1=CRPR[:, 4 * q:4 * q + 4], op=ALU.add),
                     r=[r_crp, r_btr], w=[r_btr])
                K.op("dve", lambda: nc.vector.tensor_tensor(out=sh(BTIf)[:, :, 0], in0=sh(BTIf)[:, :, 0], in1=CRPI[:, 4 * q:4 * q + 4], op=ALU.add),
                     r=[r_crp, r_bti], w=[r_bti])
                rh = RHOT[:, 4 * q * n:(4 * q + 4) * n]
                K.op("dve", lambda: nc.vector.tensor_tensor_scan(out=STRf, data0=rh, data1=BTRf, initial=0.0, op0=ALU.mult, op1=ALU.add),
                     r=[r_btr, r_rhot], w=[r_str])
                K.op("dve", lambda: nc.vector.tensor_tensor_scan(out=STIf, data0=rh, data1=BTIf, initial=0.0, op0=ALU.mult, op1=ALU.add),
                     r=[r_bti, r_rhot], w=[r_sti])
                K.op("dve", lambda: nc.vector.tensor_tensor(out=sh(Tt[0]), in0=sh(STRf), in1=cs, op=ALU.mult), r=[r_str, rs_tab], w=[rT[0]])
                K.op("dve", lambda: nc.vector.tensor_tensor(out=sh(Tt[2]), in0=sh(STRf), in1=sn, op=ALU.mult), r=[r_str, rs_tab], w=[rT[2]])
                K.op("dve", lambda: nc.vector.tensor_tensor(out=sh(Tt[1]), in0=sh(STIf), in1=sn, op=ALU.mult), r=[r_sti, rs_tab], w=[rT[1]])
                K.op("dve", lambda: nc.vector.tensor_tensor(out=sh(Tt[3]), in0=sh(STIf), in1=cs, op=ALU.mult), r=[r_sti, rs_tab], w=[rT[3]])
                sbi = ui % 2
                SRB = SRBs[sbi]; NSIB = NSIBs[sbi]; rsr = r_srb[sbi]
                K.op("dve", lambda: nc.vector.tensor_tensor(out=SRB[:, 0:4 * n], in0=Tt[0], in1=Tt[1], op=ALU.subtract), r=[rT[0], rT[1]], w=[rsr])
                K.op("dve", lambda: nc.vector.scalar_tensor_tensor(out=NSIB[:, 0:4 * n], in0=Tt[2], scalar=-1.0, in1=Tt[3], op0=ALU.mult, op1=ALU.subtract),
                     r=[rT[2], rT[3]], w=[rsr])
                K.op("dve", lambda: nc.vector.tensor_copy(out=SLR[:, 4 * q:4 * q + 4], in_=sh(STRf)[:, :, n - 1]), r=[r_str], w=[r_slr])
                K.op("dve", lambda: nc.vector.tensor_copy(out=SLI[:, 4 * q:4 * q + 4], in_=sh(STIf)[:, :, n - 1]), r=[r_sti], w=[r_sli])
                yb_ = bank("o")
                for oi in range(4):
                    o = 4 * q + oi
                    K.op("pe", lambda: nc.tensor.matmul(PSB[yb_][:, 0:n], lhsT=WC[:, o * 128:(o + 1) * 128], rhs=SRB[:, oi * n:(oi + 1) * n],
                                                        start=(oi == 0), stop=False), r=[rs_wc, rsr], w=[bankres[yb_]], inc=False)
                    K.op("pe", lambda: nc.tensor.matmul(PSB[yb_][:, 0:n], lhsT=WC[:, (16 + o) * 128:(17 + o) * 128], rhs=NSIB[:, oi * n:(oi + 1) * n],
                                                        start=False, stop=(oi == 3)), r=[rs_wc, rsr], w=[bankres[yb_]], inc=(oi == 3))
                pending_post[0] = (t, sloc, q, yb_)
                if q == 3:
                    cN = cosv(0, 16, NT5, 1).rearrange("p o t -> p (o t)"); sN = sinv(0, 16, NT5, 1).rearrange("p o t -> p (o t)")
                    if t.col0 + (sloc - t.loc) + n == TP:
                        cE = cosv(0, 16, NT5 - 1, 1); sE = sinv(0, 16, NT5 - 1, 1)
                        fr = v3(FST[:, 16 * NSEQ:32 * NSEQ], 16, NSEQ)[:, :, 0:1]
                        fi = v3(FST[:, 32 * NSEQ:48 * NSEQ], 16, NSEQ)[:, :, 0:1]
                        cmul(fr, fi, SLR.unsqueeze(2), SLI.unsqueeze(2), cE, sE, TMPC.unsqueeze(2), TMPD.unsqueeze(2),
                             [r_slr, r_sli, rs_tab], [rs_fst], res("s5_cm_tmp"))
                    else:
                        cmul(CR, CI, SLR, SLI, cN, sN, TMPC, TMPD, [r_slr, r_sli, rs_tab], [rs_carry], res("s5_cm_tmp"))
            emit_post()
            ada_flush()
            ST = [BTRf, BTIf, STRf, STIf, Tt[0], Tt[1]]
            rst = res("s5_tmp"); rsr = r_srb[0]
            SRB = SRBs[0]; NSIB = NSIBs[0]
            for t in group:
                if t.kind != "s":
                    continue
                sloc, n = t.loc, t.n
                for q in range(4):
                    br_b = bank("g1"); bi_b = bank("g3")
                    for oi in range(4):
                        o = 4 * q + oi
                        K.op("pe", lambda: nc.tensor.matmul(PSB[br_b][:, oi * n:(oi + 1) * n], lhsT=WBU[:, o * 128:(o + 1) * 128], rhs=uv(q, sloc, n),
                                                            start=True, stop=True),
                             r=[rs_wbu, res("u_%d" % t.loc)], w=[bankres[br_b]], inc=(oi == 3))
                    for oi in range(4):
                        o = 4 * q + oi
                        K.op("pe", lambda: nc.tensor.matmul(PSB[bi_b][:, oi * n:(oi + 1) * n], lhsT=WBU[:, (16 + o) * 128:(17 + o) * 128], rhs=uv(q, sloc, n),
                                                            start=True, stop=True),
                             r=[rs_wbu, res("u_%d" % t.loc)], w=[bankres[bi_b]], inc=(oi == 3))
                    allT = [rT[0], rT[1], rT[2], rT[3], r_bt, r_st]
                    BTR, BTI, STR, STI, T1, T2 = [x[:, 0:4 * n] for x in ST]
                    cs = cosv(4 * q, 4, 0, TS).unsqueeze(2).to_broadcast([128, 4, NSQ, TS])
                    sn = sinv(4 * q, 4, 0, TS).unsqueeze(2).to_broadcast([128, 4, NSQ, TS])
                    sh = lambda a: a.rearrange("p (o s t) -> p o s t", o=4, s=NSQ, t=TS)
                    pbr = sh(PSB[br_b][:, 0:4 * n]); pbi = sh(PSB[bi_b][:, 0:4 * n])
                    K.op("dve", lambda: nc.vector.tensor_tensor(out=sh(T1), in0=pbr, in1=cs, op=ALU.mult), r=[bankres[br_b], rs_tab], w=[rst, allT])
                    K.op("dve", lambda: nc.vector.tensor_tensor(out=sh(T2), in0=pbi, in1=sn, op=ALU.mult), r=[bankres[bi_b], rs_tab], w=[rst])
                    K.op("dve", lambda: nc.vector.tensor_tensor(out=BTR, in0=T1, in1=T2, op=ALU.add), r=[rst], w=[rst])
                    K.op("dve", lambda: nc.vector.tensor_tensor(out=sh(T1), in0=pbi, in1=cs, op=ALU.mult), r=[bankres[bi_b], rs_tab], w=[rst])
                    K.op("dve", lambda: nc.vector.tensor_tensor(out=sh(T2), in0=pbr, in1=sn, op=ALU.mult), r=[bankres[br_b], rs_tab], w=[rst])
                    K.op("dve", lambda: nc.vector.tensor_tensor(out=BTI, in0=T1, in1=T2, op=ALU.subtract), r=[rst], w=[rst])
                    str4 = sh(STR); sti4 = sh(STI); btr4 = sh(BTR); bti4 = sh(BTI)
                    rb = RHO[:, 4 * q:4 * q + 4].unsqueeze(2).to_broadcast([128, 4, NSQ])
                    for tt in range(TS):
                        for (s4, b4, s0) in ((str4, btr4, S0TR), (sti4, bti4, S0TI)):
                            prev = v3(s0, 16, NSQ)[:, 4 * q:4 * q + 4, :] if tt == 0 else s4[:, :, :, tt - 1]
                            K.op("dve", lambda: nc.vector.tensor_tensor(out=s4[:, :, :, tt], in0=prev, in1=rb, op=ALU.mult),
                                 r=[rst, rs_s0, rs_small], w=[rst])
                            K.op("dve", lambda: nc.vector.tensor_tensor(out=s4[:, :, :, tt], in0=s4[:, :, :, tt], in1=b4[:, :, :, tt], op=ALU.add),
                                 r=[rst], w=[rst])
                    c3 = cosv(4 * q, 4, TS - 1, 1).to_broadcast([128, 4, NSQ]); s3_ = sinv(4 * q, 4, TS - 1, 1).to_broadcast([128, 4, NSQ])
                    fr = v3(FST[:, (16 + 4 * q) * NSEQ:(16 + 4 * q + 4) * NSEQ], 4, NSEQ)[:, :, 1:NSEQ]
                    fi = v3(FST[:, (32 + 4 * q) * NSEQ:(32 + 4 * q + 4) * NSEQ], 4, NSEQ)[:, :, 1:NSEQ]
                    tt1 = v3(TAILX[:, 0:64], 4, NSQ); tt2 = v3(TAILX[:, 64:128], 4, NSQ)
                    cmul(fr, fi, str4[:, :, :, TS - 1], sti4[:, :, :, TS - 1], c3, s3_, tt1, tt2, [rst, rs_tab], [rs_fst], res("tailx"))
                    K.op("dve", lambda: nc.vector.tensor_tensor(out=sh(T1), in0=sh(STR), in1=cs, op=ALU.mult), r=[rst, rs_tab], w=[rst])
                    K.op("dve", lambda: nc.vector.tensor_tensor(out=sh(T2), in0=sh(STI), in1=sn, op=ALU.mult), r=[rst, rs_tab], w=[rst])
                    K.op("dve", lambda: nc.vector.tensor_tensor(out=SRB[:, 0:4 * n], in0=T1, in1=T2, op=ALU.subtract), r=[rst], w=[rsr])
                    K.op("dve", lambda: nc.vector.tensor_tensor(out=sh(T1), in0=sh(STR), in1=sn, op=ALU.mult), r=[rst, rs_tab], w=[rst])
                    K.op("dve", lambda: nc.vector.tensor_tensor(out=sh(T2), in0=sh(STI), in1=cs, op=ALU.mult), r=[rst, rs_tab], w=[rst])
                    K.op("dve", lambda: nc.vector.scalar_tensor_tensor(out=NSIB[:, 0:4 * n], in0=T1, scalar=-1.0, in1=T2, op0=ALU.mult, op1=ALU.subtract),
                         r=[rst], w=[rsr])
                    yb_ = bank("o")
                    for oi in range(4):
                        o = 4 * q + oi
                        K.op("pe", lambda: nc.tensor.matmul(PSB[yb_][:, 0:n], lhsT=WC[:, o * 128:(o + 1) * 128], rhs=SRB[:, oi * n:(oi + 1) * n],
                                                            start=(oi == 0), stop=False), r=[rs_wc, rsr], w=[bankres[yb_]], inc=False)
                        K.op("pe", lambda: nc.tensor.matmul(PSB[yb_][:, 0:n], lhsT=WC[:, (16 + o) * 128:(17 + o) * 128], rhs=NSIB[:, oi * n:(oi + 1) * n],
                                                            start=False, stop=(oi == 3)), r=[rs_wc, rsr], w=[bankres[yb_]], inc=(oi == 3))
                    K.op("dve", lambda: nc.vector.scalar_tensor_tensor(out=PREf[:, 0:n], in0=uv(q, sloc, n), scalar=ptc(l, "s5_d", q),
                                                                       in1=PSB[yb_][:, 0:n], op0=ALU.mult, op1=ALU.add),
                         r=[bankres[yb_], res("u_%d" % t.loc), rs_pt], w=[rpre])
                    K.op("act", lambda: nc.scalar.activation(out=gyv(q, sloc, n), in_=PREf[:, 0:n], func=AF.Gelu_apprx_tanh),
                         r=[rpre], w=[res("gy_%d" % t.loc)])
            K.barrier()
            wgl = wglu_d[l].rearrange("(k p) f -> p k f", p=128)
            sl = ring_load(lambda slot: [(slot[:, 0:2048].rearrange("p (k f) -> p k f", k=4), wgl[:, :, :])])
            sg3 = RING[sl][:, 0:2048].rearrange("p (k f) -> p k f", k=4)
            SG = [WORK[:, i * 512:(i + 1) * 512] for i in range(4)]
            for t in group:
                n = t.n
                bs = []
                for oc in range(4):
                    b = bank(("g1", "g3")[oc % 2])
                    bs.append(b)
                    for k in range(4):
                        K.op("pe", lambda: nc.tensor.matmul(PSB[b][:, 0:n], lhsT=sg3[:, k, oc * 128:(oc + 1) * 128], rhs=gyv(k, t.loc, n),
                                                            start=(k == 0), stop=(k == 3)),
                             r=[ringres[sl], res("gy_%d" % t.loc)], w=[bankres[b]], inc=(k == 3))
                for oc in range(4):
                    b = bs[oc]
                    rsg = res("sg_%d" % oc)
                    K.op("act", lambda: nc.scalar.activation(out=SG[oc][:, 0:n], in_=PSB[b][:, 0:n], func=AF.Sigmoid, bias=ptc(l, "b_glu", oc), scale=1.0),
                         r=[bankres[b], rs_pt], w=[rsg])
                for oc in range(4):
                    rsg = res("sg_%d" % oc)
                    K.op("dve", lambda: nc.vector.tensor_tensor(out=gyv(oc, t.loc, n), in0=gyv(oc, t.loc, n), in1=SG[oc][:, 0:n], op=ALU.mult),
                         r=[rsg], w=[res("gy_%d" % t.loc)])
            wos = wout_d[l].rearrange("(k p) f -> p k f", p=128)
            for half in range(2):
                sl = ring_load(lambda slot: [(slot[:, 0:4096].rearrange("p (k f) -> p k f", k=8), wos[:, :, half * 512:(half + 1) * 512])])
                so3 = RING[sl][:, 0:4096].rearrange("p (k f) -> p k f", k=8)
                for t in group:
                    n = t.n
                    for oc in range(4):
                        d = half * 4 + oc
                        b = bank("o")
                        for k in range(KC):
                            rhs = ylv(k, t.loc, n) if k < 4 else gyv(k - 4, t.loc, n)
                            K.op("pe", lambda: nc.tensor.matmul(PSB[b][:, 0:n], lhsT=so3[:, k, oc * 128:(oc + 1) * 128], rhs=rhs,
                                                                start=(k == 0), stop=(k == KC - 1)),
                                 r=[ringres[sl], res("yl_%d" % t.loc), res("gy_%d" % t.loc)], w=[bankres[b]], inc=(k == KC - 1))
                        residual(t, s, d, b)

        ds_o = K.dsem()

        def state_out(l):
            OST = AUX[0:NSEQ, 0:2048]
            rs_ost = res("ost")
            pieces = [(0, 12, oconv_d, None), (12, 4, oh_d, None), (16, 16, osr_d, None), (32, 16, osi_d, None)]
            for (c0, ncnk, dst, _) in pieces:
                for g4 in range(0, ncnk, 4):
                    b = bank("m")
                    for i in range(4):
                        cidx = c0 + g4 + i
                        K.op("pe", lambda: nc.tensor.transpose(out=PSB[b][0:NSEQ, i * 128:(i + 1) * 128], in_=FST[:, cidx * NSEQ:(cidx + 1) * NSEQ], identity=IDF[:]),
                             r=[rs_fst, rs_const], w=[bankres[b]], inc=(i == 3))
                    if c0 == 0:
                        for i in range(4):
                            cidx = g4 + i
                            c, kk = cidx // 3, cidx % 3
                            K.op("dve", lambda: nc.vector.tensor_copy(out=OST[:, kk * 512 + c * 128: kk * 512 + (c + 1) * 128], in_=PSB[b][0:NSEQ, i * 128:(i + 1) * 128]),
                                 r=[bankres[b]], w=[rs_ost, rs_aux[0], rs_aux[1]])
                    else:
                        K.op("dve", lambda: nc.vector.tensor_copy(out=OST[:, g4 * 128:(g4 + 4) * 128], in_=PSB[b][0:NSEQ, 0:512]),
                             r=[bankres[b]], w=[rs_ost, rs_aux[0], rs_aux[1]])
                K.dma("sp", ds_o, [(dst[l], OST[:, 0:ncnk * 128])], r=[rs_ost, rs_aux[0], rs_aux[1]])

        def final_out():
            K.barrier()
            ds_y = [K.dsem() for _ in range(2)]
            XN = [WORK[:, i * 1024:(i + 1) * 1024] for i in range(2)]
            YST = [WORK[:, 2048 + i * 1024: 2048 + (i + 1) * 1024] for i in range(2)]
            XSQfs = [wbf(4352, 8 * 128), wbf(4352 + 512, 8 * 128)]
            RSs = [WORK[:, 5400:5528], WORK[:, 5528:5656]]; SQs = [WORK[:, 5656:5784], WORK[:, 5784:5912]]
            ftiles = []
            for (dst, ntok, colbase) in ((yp_d, TP, 0), (ys_d, NSAMP, TP)):
                for t0 in range(0, ntok, 128):
                    ftiles.append((dst, t0, min(128, ntok - t0), colbase + t0))

            def fres(i):
                pi_ = i % 2
                return (res("f_xsq%d" % pi_), res("f_rs%d" % pi_), res("f_xn%d" % pi_), res("f_yst%d" % pi_))

            def stage_a(i):
                dst, t0, n, col = ftiles[i]
                pi_ = i % 2
                tl = tiles_all[min(col // 512, 4)]
                rxs, rrs, rxn, ryst = fres(i)
                XSQf = XSQfs[pi_]; RS = RSs[pi_]; SQ = SQs[pi_]
                for c in range(KC):
                    K.op("act", lambda: nc.scalar.activation(out=XSQf[:, c * 128: c * 128 + n], in_=xv(c, col, n), func=AF.Square),
                         r=[xres(tl, c)], w=[rxs])
                b = bank("m")
                for c in range(KC):
                    K.op("pe", lambda: nc.tensor.matmul(PSB[b][:, 0:n], lhsT=ONESB[:], rhs=XSQf[:, c * 128: c * 128 + n], start=(c == 0), stop=(c == KC - 1)),
                         r=[rxs, rs_const], w=[bankres[b]], inc=(c == KC - 1))
                K.op("act", lambda: nc.scalar.activation(out=SQ[:, 0:n], in_=PSB[b][:, 0:n], func=AF.Sqrt, bias=EPSC, scale=1.0 / D),
                     r=[bankres[b], rs_const], w=[rrs])

            def stage_b(i):
                dst, t0, n, col = ftiles[i]
                pi_ = i % 2
                tl = tiles_all[min(col // 512, 4)]
                rxs, rrs, rxn, ryst = fres(i)
                RS = RSs[pi_]; SQ = SQs[pi_]
                K.op("dve", lambda: nc.vector.reciprocal(out=RS[:, 0:n], in_=SQ[:, 0:n]), r=[rrs], w=[rrs])
                for c in range(KC):
                    K.op("dve", lambda: nc.vector.scalar_tensor_tensor(out=XN[pi_][:, c * 128: c * 128 + n], in0=xv(c, col, n),
                                                                       scalar=PT[:, PR_FINAL + c: PR_FINAL + c + 1], in1=RS[:, 0:n], op0=ALU.mult, op1=ALU.mult),
                         r=[xres(tl, c), rrs, rs_pt], w=[rxn])
                for half in range(2):
                    b = bank("o")
                    for cc in range(4):
                        c = half * 4 + cc
                        K.op("pe", lambda: nc.tensor.transpose(out=PSB[b][0:n, cc * 128:(cc + 1) * 128], in_=XN[pi_][:, c * 128: c * 128 + n], identity=IDF[:]),
                             r=[rxn, rs_const], w=[bankres[b]], inc=(cc == 3))
                    if half == 0:
                        K.op("dve", lambda: nc.vector.tensor_copy(out=YST[pi_][0:n, 0:512], in_=PSB[b][0:n, 0:512]), r=[bankres[b]], w=[ryst])
                    else:
                        K.op("act", lambda: nc.scalar.activation(out=YST[pi_][0:n, 512:1024], in_=PSB[b][0:n, 0:512], func=AF.Identity), r=[bankres[b]], w=[ryst])
                K.dma("sp", ds_y[pi_], [(dst[t0:t0 + n, :], YST[pi_][0:n, :])], r=[ryst])

            stage_a(0)
            for i in range(len(ftiles)):
                if i + 1 < len(ftiles):
                    stage_a(i + 1)
                stage_b(i)

        def main_prog():
            stage = [0]

            def stop():
                stage[0] += 1
                return STOP_AT is not None and stage[0] > STOP_AT
            if stop():
                return
            ada_enqueue(0, 0, 10)
            ada_flush()
            for l in range(DEPTH):
                if stop():
                    return
                ffn(l, 0, 0, "norm_ffn1", groups[0], True, groups[1])
                ffn(l, 0, 0, "norm_ffn1", groups[1], False, None)
                if stop():
                    return
                mixer_setup(l)
                if stop():
                    return
                for gi, g in enumerate(groups):
                    if gi == 0:
                        ada_enqueue(l, 10, 18)
                    if l + 1 < DEPTH:
                        if gi == 0:
                            ada_enqueue(l + 1, 0, 6)
                        else:
                            ada_enqueue(l + 1, 6, 10)
                    mixer_group(l, gi, g)
                    if stop():
                        return
                ada_flush()
                state_out(l)
                if stop():
                    return
                ffn(l, 1, 2, "norm_ffn2", groups[0], True, groups[1])
                ffn(l, 1, 2, "norm_ffn2", groups[1], False, None)
                if stop():
                    return
            final_out()
        main_prog()
        K.finish()
        rec_out = K.rec
    if want_rec:
        return rec_out
    return nc


_NC_CACHE = {}


def _host_layouts(inp):
    L = DEPTH
    prow = np.zeros((PR_ROWS, 128), np.float32)
    for l in range(L):
        for name, k in PR_NAMES:
            r0 = PR_LAYER * l + PR_OFF[name]
            prow[r0:r0 + k] = np.asarray(inp[name][l], np.float32).reshape(k, 128)
    prow[PR_FINAL:PR_FINAL + 8] = np.asarray(inp["norm_final"], np.float32).reshape(8, 128)
    ld = np.asarray(inp["s5_log_dt"], np.float32)
    dtx = np.repeat(ld.reshape(L, 16, 2).transpose(0, 2, 1), 64, axis=1)
    dtx = np.ascontiguousarray(dtx)
    wg = np.zeros((L, 128, 8, 128), np.float32)
    for gi, nm in enumerate(("w_rg", "w_ig")):
        w = np.asarray(inp[nm], np.float32)
        for c in range(4):
            wg[:, 0:64, gi * 4 + c, 0:64] = w[:, 2 * c]
            wg[:, 64:128, gi * 4 + c, 64:128] = w[:, 2 * c + 1]
    wg = wg.reshape(L, 128, 8 * 128)
    bnat = np.zeros((L, 2, 128, 16, 128), np.float32)
    for comp, nm in enumerate(("s5_b_re", "s5_b_im")):
        Bm = np.asarray(inp[nm], np.float32)
        for o in range(16):
            for gl in range(2):
                g = 2 * o + gl
                cb = (g % 8) * 16
                bnat[:, comp, gl * 64:(gl + 1) * 64, o, cb:cb + 16] = Bm[:, g]
    bnat = bnat.reshape(L, 2, 128, 16 * 128)
    cpad = np.zeros((L, 128, 2, 16, 128), np.float32)
    for comp, nm in enumerate(("s5_c_re", "s5_c_im")):
        Cm = np.asarray(inp[nm], np.float32)
        for o in range(16):
            for gl in range(2):
                g = 2 * o + gl
                cb = (g % 8) * 16
                cpad[:, gl * 64:(gl + 1) * 64, comp, o, cb:cb + 16] = Cm[:, g].transpose(0, 2, 1)
    cpad = cpad.reshape(L, 128, 32 * 128)
    return prow, dtx, wg, bnat, cpad


def kernel(**inp):
    if "nc" not in _NC_CACHE:
        _NC_CACHE["nc"] = build_program()
    nc = _NC_CACHE["nc"]
    f = lambda a: np.ascontiguousarray(np.asarray(a, np.float32))
    prow, dtx, wg, bnat, cpad = _host_layouts(inp)
    shared = {
        "prow": prow, "dtx": dtx, "wgates": wg, "bnat": bnat, "cpad": cpad,
        "w_ada": f(inp["w_ada"]),
        "w1_ffn1": f(inp["w1_ffn1"]), "w3_ffn1": f(inp["w3_ffn1"]), "w2_ffn1": f(inp["w2_ffn1"]),
        "w1_ffn2": f(inp["w1_ffn2"]), "w3_ffn2": f(inp["w3_ffn2"]), "w2_ffn2": f(inp["w2_ffn2"]),
        "w_in": f(inp["w_in"]), "w_glu": f(inp["w_glu"]), "w_out": f(inp["w_out"]),
    }
    xp = f(inp["x_prompt"]); xs = f(inp["x_sample"]); cp = f(inp["c_prompt"]); cs = f(inp["c_sample"])
    stc = f(inp["state_lru_conv"]); sth = f(inp["state_lru_h"]); sr = f(inp["state_s5_re"]); si = f(inp["state_s5_im"])
    in_maps = []
    for i in range(NCORES):
        s0, s1 = NSQ * i, NSQ * (i + 1)
        m = dict(shared)
        m["xp"] = xp[i]
        m["xs"] = np.ascontiguousarray(xs[s0:s1].reshape(NSAMP, D))
        m["c17"] = np.ascontiguousarray(np.concatenate([cp[i:i + 1], cs[s0:s1]], axis=0))
        m["st_conv"] = np.ascontiguousarray(stc[:, s0:s1].reshape(DEPTH, NSQ, 3 * 512))
        m["st_h"] = np.ascontiguousarray(sth[:, s0:s1])
        m["st_sr"] = np.ascontiguousarray(sr[:, s0:s1].reshape(DEPTH, NSQ, 2048))
        m["st_si"] = np.ascontiguousarray(si[:, s0:s1].reshape(DEPTH, NSQ, 2048))
        in_maps.append(m)
    res = run_bass_kernel_spmd(nc, in_maps, core_ids=list(range(NCORES)))
    R = res.results
    B = NCORES
    y_prompt = np.stack([R[i]["y_p"] for i in range(B)], axis=0).astype(np.float32)
    y_sample = np.concatenate([R[i]["y_s"].reshape(NSQ, TS, D) for i in range(B)], axis=0).astype(np.float32)

    def gather(name, tail):
        p = np.stack([R[i][name][:, 0] for i in range(B)], axis=1)
        s = np.concatenate([R[i][name][:, 1:] for i in range(B)], axis=1)
        return (p.reshape((DEPTH, B) + tail).astype(np.float32), s.reshape((DEPTH, NSQ * B) + tail).astype(np.float32))
    p_conv, s_conv = gather("o_conv", (3, 512))
    p_h, s_h = gather("o_h", (512,))
    p_sr, s_sr = gather("o_sr", (32, 64))
    p_si, s_si = gather("o_si", (32, 64))
    return (y_prompt, y_sample, p_conv, p_h, p_sr, p_si, s_conv, s_h, s_sr, s_si)
```
